# Optimizing a Trainium2 kernel written in Bass

```python
import jax, jax.numpy as jnp
from jax import lax
import numpy as np

D_MODEL = 1024
BATCH = 2
SEQ = 8192
DEPTH = 2

GRID_W = 64
CTX_LEN = 256

RET_W = D_MODEL // 2
RET_HEADS = 4
RET_HEAD_DIM = RET_W // RET_HEADS
RET_CHUNK = 128
ROPE_BASE = 10000.0
ROPE_FREQS = RET_HEAD_DIM // 4
FNET_W = D_MODEL // 4
FNET_GROUPS = 4
FNET_GROUP_DIM = FNET_W // FNET_GROUPS
GMLP_W = D_MODEL // 4
GMLP_GROUPS = 4
GMLP_GROUP_DIM = GMLP_W // GMLP_GROUPS
GMLP_CHUNK = 128
N_BRANCH = 3
D_FF = 4 * D_MODEL
EPS = 1e-6
IN_W = 4 * RET_W + FNET_W + 2 * GMLP_W + N_BRANCH * D_MODEL
IN_SPLITS = (RET_W, 2 * RET_W, 3 * RET_W, 4 * RET_W, 4 * RET_W + FNET_W,
             4 * RET_W + FNET_W + GMLP_W, 4 * RET_W + FNET_W + 2 * GMLP_W)

kernel_name = "hybrid_retention_fourier_sgu_prefix_dit"


def rms_norm(x, gain=None):
    xf = x.astype(jnp.float32)
    y = xf * lax.rsqrt(jnp.mean(xf * xf, axis=-1, keepdims=True) + EPS)
    if gain is not None:
        y = y * gain.astype(jnp.float32)
    return y.astype(x.dtype)


def modulate(h, shift, scale):
    return h * (1.0 + scale) + shift


def grid_rope_angles(n_tokens):
    rows = n_tokens // GRID_W
    row = jnp.repeat(jnp.arange(rows, dtype=jnp.float32), GRID_W)
    col = jnp.tile(jnp.arange(GRID_W, dtype=jnp.float32), rows)
    freqs = ROPE_BASE ** (-jnp.arange(ROPE_FREQS, dtype=jnp.float32) / ROPE_FREQS)
    return jnp.concatenate([row[:, None] * freqs, col[:, None] * freqs], axis=-1)


def apply_grid_rope(a, angles):
    cos = jnp.cos(angles)[None, :, None, :]
    sin = jnp.sin(angles)[None, :, None, :]
    a1, a2 = jnp.split(a, 2, axis=-1)
    return jnp.concatenate([a1 * cos - a2 * sin, a1 * sin + a2 * cos], axis=-1)


def ret_scan_states(k, v, log_g, s0):
    B, L, H, dk = k.shape
    n = L // RET_CHUNK
    kc = k.reshape(B, n, RET_CHUNK, H, dk)
    vc = v.reshape(B, n, RET_CHUNK, H, -1)
    pos = jnp.arange(RET_CHUNK, dtype=jnp.float32)
    w_state = jnp.exp(log_g[:, None] * (RET_CHUNK - 1.0 - pos)[None, :])
    u = jnp.einsum('bnshd,hs,bnshe->nbhde', kc, w_state, vc)
    decay = jnp.exp(log_g * RET_CHUNK)[None, :, None, None]

    def step(s, u_i):
        return decay * s + u_i, s

    s_final, s_starts = lax.scan(step, s0, u)
    return s_starts, s_final


def ret_chunk_out(q, k, v, log_g, s_starts, inclusive):
    B, L, H, dk = q.shape
    n = L // RET_CHUNK
    qc = q.reshape(B, n, RET_CHUNK, H, dk)
    kc = k.reshape(B, n, RET_CHUNK, H, dk)
    vc = v.reshape(B, n, RET_CHUNK, H, -1)
    pos = jnp.arange(RET_CHUNK, dtype=jnp.float32)
    diff = pos[:, None] - pos[None, :]
    keep = (diff >= 0) if inclusive else (diff > 0)
    dmask = jnp.where(keep[None], jnp.exp(log_g[:, None, None] * jnp.maximum(diff, 0.0)[None]), 0.0)
    scores = jnp.einsum('bnchd,bnshd->bnhcs', qc, kc) * dmask[None, None]
    inner = jnp.einsum('bnhcs,bnshe->bnche', scores, vc)
    q_decay = jnp.exp(log_g[:, None] * (pos + 1.0)[None, :])
    cross = jnp.einsum('bnchd,hc,nbhde->bnche', qc, q_decay, s_starts)
    return (inner + cross).reshape(B, L, H, -1)


def retention_bidir(q, k, v, log_g_f, log_g_b, s0_f, s0_b):
    st_f, sf = ret_scan_states(k, v, log_g_f, s0_f)
    o_f = ret_chunk_out(q, k, v, log_g_f, st_f, True)
    qr, kr, vr = jnp.flip(q, axis=1), jnp.flip(k, axis=1), jnp.flip(v, axis=1)
    st_b, sb = ret_scan_states(kr, vr, log_g_b, s0_b)
    o_b = jnp.flip(ret_chunk_out(qr, kr, vr, log_g_b, st_b, False), axis=1)
    return o_f + o_b, sf, sb


def fourier_mix(f):
    B, L, _ = f.shape
    fg = f.reshape(B, L, FNET_GROUPS, FNET_GROUP_DIM).astype(jnp.float32)
    y = jnp.fft.fft2(fg, axes=(1, 3), norm='ortho').real
    return y.reshape(B, L, FNET_W).astype(f.dtype)


def sgu_mix(u, v, w_s, b_s, g_norm):
    B, L, _ = v.shape
    u = jax.nn.gelu(u)
    v = jax.nn.gelu(v).reshape(B, L, GMLP_GROUPS, GMLP_GROUP_DIM)
    v = rms_norm(v, g_norm.reshape(GMLP_GROUPS, GMLP_GROUP_DIM))
    v = v.reshape(B, L // GMLP_CHUNK, GMLP_CHUNK, GMLP_GROUPS, GMLP_GROUP_DIM)
    s = jnp.einsum('gts,bnsgc->bntgc', w_s, v) + jnp.transpose(b_s)[None, None, :, :, None]
    return u * s.reshape(B, L, GMLP_W)


def token_mixer(h, w_in, log_g, w_s, b_s, g_norm, w_a, w_b, w_c, w_o, s0_f, s0_b, angles):
    B, L, _ = h.shape
    z = h @ w_in
    q, k, v, g, f, u, vs, gates = jnp.split(z, IN_SPLITS, axis=-1)
    q = q.reshape(B, L, RET_HEADS, RET_HEAD_DIM).astype(jnp.float32)
    k = k.reshape(B, L, RET_HEADS, RET_HEAD_DIM).astype(jnp.float32) * (RET_HEAD_DIM ** -0.5)
    v = v.reshape(B, L, RET_HEADS, RET_HEAD_DIM).astype(jnp.float32)
    if angles is not None:
        q = apply_grid_rope(q, angles)
        k = apply_grid_rope(k, angles)
    o, sf, sb = retention_bidir(q, k, v, log_g[0], log_g[1], s0_f, s0_b)
    ret = rms_norm(o).reshape(B, L, RET_W).astype(h.dtype) * jax.nn.silu(g)
    four = fourier_mix(f)
    sgu = sgu_mix(u, vs, w_s, b_s, g_norm)
    ga, gb, gc = jnp.split(jax.nn.sigmoid(gates), N_BRANCH, axis=-1)
    merged = ga * (ret @ w_a) + gb * (four @ w_b) + gc * (sgu @ w_c)
    return merged @ w_o, sf, sb


def sq_relu_mlp(h, w_up, w_down):
    a = jax.nn.relu(h @ w_up)
    return (a * a) @ w_down


def setup_inputs(seed: int = 0) -> dict:
    key = jax.random.key(seed)
    ks = jax.random.split(key, 24)
    f32 = jnp.float32
    nrm = lambda k, shape, s: jax.random.normal(k, shape, f32) * s
    base_logit = jnp.log(2.0 ** (5.0 + jnp.arange(RET_HEADS, dtype=f32)) - 1.0)
    return {
        'x': nrm(ks[0], (BATCH, SEQ, D_MODEL), 1.0),
        'c': nrm(ks[1], (BATCH, D_MODEL), 1.0),
        'ctx': nrm(ks[2], (BATCH, CTX_LEN, D_MODEL), 1.0),
        'c_ctx': nrm(ks[3], (D_MODEL,), 1.0),
        'w_mod': nrm(ks[4], (DEPTH, D_MODEL, 6 * D_MODEL), 0.5 * D_MODEL ** -0.5),
        'b_mod': nrm(ks[5], (DEPTH, 6 * D_MODEL), 0.02),
        'g_pre_mix': 1.0 + nrm(ks[6], (DEPTH, D_MODEL), 0.02),
        'g_post_mix': 1.0 + nrm(ks[7], (DEPTH, D_MODEL), 0.02),
        'g_pre_mlp': 1.0 + nrm(ks[8], (DEPTH, D_MODEL), 0.02),
        'g_post_mlp': 1.0 + nrm(ks[9], (DEPTH, D_MODEL), 0.02),
        'w_in': nrm(ks[10], (DEPTH, D_MODEL, IN_W), D_MODEL ** -0.5),
        'ret_decay_logit': base_logit[None, None, :] + nrm(ks[11], (DEPTH, 2, RET_HEADS), 0.1),
        'sgu_w_s': nrm(ks[12], (DEPTH, GMLP_GROUPS, GMLP_CHUNK, GMLP_CHUNK), GMLP_CHUNK ** -0.5),
        'sgu_b_s': 1.0 + nrm(ks[13], (DEPTH, GMLP_GROUPS, GMLP_CHUNK), 0.02),
        'sgu_norm': 1.0 + nrm(ks[14], (DEPTH, GMLP_W), 0.02),
        'w_branch_a': nrm(ks[15], (DEPTH, RET_W, D_MODEL), RET_W ** -0.5),
        'w_branch_b': nrm(ks[16], (DEPTH, FNET_W, D_MODEL), FNET_W ** -0.5),
        'w_branch_c': nrm(ks[17], (DEPTH, GMLP_W, D_MODEL), GMLP_W ** -0.5),
        'w_out': nrm(ks[18], (DEPTH, D_MODEL, D_MODEL), D_MODEL ** -0.5),
        'w_up': nrm(ks[19], (DEPTH, D_MODEL, D_FF), D_MODEL ** -0.5),
        'w_down': nrm(ks[20], (DEPTH, D_FF, D_MODEL), D_FF ** -0.5),
    }


def reference(x, c, ctx, c_ctx, w_mod, b_mod, g_pre_mix, g_post_mix, g_pre_mlp, g_post_mlp,
              w_in, ret_decay_logit, sgu_w_s, sgu_b_s, sgu_norm, w_branch_a, w_branch_b,
              w_branch_c, w_out, w_up, w_down):
    B, L, _ = x.shape
    angles = grid_rope_angles(L)
    silu_c = jax.nn.silu(c)
    silu_cc = jax.nn.silu(c_ctx)
    s0 = jnp.zeros((B, RET_HEADS, RET_HEAD_DIM, RET_HEAD_DIM), jnp.float32)
    for l in range(DEPTH):
        last = l == DEPTH - 1
        mod = (silu_c @ w_mod[l] + b_mod[l])[:, None, :]
        sh1, sc1, gt1, sh2, sc2, gt2 = jnp.split(mod, 6, axis=-1)
        mod_c = silu_cc @ w_mod[l] + b_mod[l]
        csh1, csc1, cgt1, csh2, csc2, cgt2 = jnp.split(mod_c, 6, axis=-1)
        log_g = jax.nn.log_sigmoid(ret_decay_logit[l].astype(jnp.float32))

        hc = modulate(rms_norm(ctx, g_pre_mix[l]), csh1, csc1)
        if last:
            kc_, vc_ = jnp.split(hc @ w_in[l][:, RET_W:3 * RET_W], 2, axis=-1)
            kc_ = kc_.reshape(B, CTX_LEN, RET_HEADS, RET_HEAD_DIM).astype(jnp.float32) * (RET_HEAD_DIM ** -0.5)
            vc_ = vc_.reshape(B, CTX_LEN, RET_HEADS, RET_HEAD_DIM).astype(jnp.float32)
            _, sf = ret_scan_states(kc_, vc_, log_g[0], s0)
            _, sb = ret_scan_states(jnp.flip(kc_, axis=1), jnp.flip(vc_, axis=1), log_g[1], s0)
        else:
            yc, sf, sb = token_mixer(hc, w_in[l], log_g, sgu_w_s[l], sgu_b_s[l], sgu_norm[l],
                                     w_branch_a[l], w_branch_b[l], w_branch_c[l], w_out[l],
                                     s0, s0, None)
            ctx_mid = ctx + cgt1 * rms_norm(yc, g_post_mix[l])
            hc2 = modulate(rms_norm(ctx_mid, g_pre_mlp[l]), csh2, csc2)
            ctx_next = ctx_mid + cgt2 * rms_norm(sq_relu_mlp(hc2, w_up[l], w_down[l]), g_post_mlp[l])

        h = modulate(rms_norm(x, g_pre_mix[l]), sh1, sc1)
        y, _, _ = token_mixer(h, w_in[l], log_g, sgu_w_s[l], sgu_b_s[l], sgu_norm[l],
                              w_branch_a[l], w_branch_b[l], w_branch_c[l], w_out[l],
                              sf, sb, angles)
        x = x + gt1 * rms_norm(y, g_post_mix[l])
        h2 = modulate(rms_norm(x, g_pre_mlp[l]), sh2, sc2)
        x = x + gt2 * rms_norm(sq_relu_mlp(h2, w_up[l], w_down[l]), g_post_mlp[l])

        if not last:
            ctx = ctx_next
    return x
```

```python
from contextlib import ExitStack
import math
import numpy as np
import ml_dtypes
import concourse.bass as bass
import concourse.mybir as mybir
from concourse.bass_utils import run_bass_kernel_spmd

F32 = mybir.dt.float32
BF16 = mybir.dt.bfloat16
AF = mybir.ActivationFunctionType
ALU = mybir.AluOpType

D = 1024
DEPTH = 2
NCH = 16
G = 4
CTXCH = 2
IN_W = 5888
EPS = 1e-6
SK = 128 ** -0.5
DEBUG = {}


class Tile:
    __slots__ = ("name", "h", "w", "r")

    def __init__(self, name, h):
        self.name = name
        self.h = h
        self.w = {}
        self.r = {}

    def __getitem__(self, k):
        return self.h[k]


class Eng:
    def __init__(self, name, sem):
        self.name = name
        self.sem = sem
        self.count = 0
        self.ops = []
        self.waited = {}


class Stream:
    K = 8

    def __init__(self, name, sems):
        self.name = name
        self.sems = sems
        self.n = 0
        self.sem = sems[0]
        self.count = 0


class Ctx:
    ENG_NAMES = ("pe", "act", "dve", "pool", "sp")

    def __init__(self, nc):
        self.nc = nc
        self.stack = ExitStack()
        self.engs = {}
        for n in self.ENG_NAMES:
            sem = self.stack.enter_context(nc.semaphore("sem_" + n))
            self.engs[n] = Eng(n, sem)
        self.streams = {}
        self.ntiles = 0

    def stream(self, name, k=None):
        if name not in self.streams:
            k = k or Stream.K
            sems = [self.stack.enter_context(self.nc.semaphore("dq_%s%d" % (name, i))) for i in range(k)]
            self.streams[name] = Stream(name, sems)
        return self.streams[name]

    def sbuf(self, name, shape, dtype, stack=None):
        self.ntiles += 1
        h = (stack or self.stack).enter_context(self.nc.sbuf_tensor(f"{name}_{self.ntiles}", list(shape), dtype))
        self.min_free = min(getattr(self, "min_free", 1 << 30), self.nc.sbuf_bytes_remaining)
        return Tile(name, h)

    def psum(self, name, shape, dtype=F32):
        self.ntiles += 1
        h = self.stack.enter_context(self.nc.psum_tensor(f"{name}_{self.ntiles}", list(shape), dtype))
        return Tile(name, h)

    def dram(self, name, shape, dtype, kind="Internal"):
        t = self.nc.dram_tensor(name, list(shape), dtype, kind=kind)
        return Tile(name, t.ap())

    def _collect(self, eng, reads, writes):
        need = {}

        def add(d):
            for s, v in d.items():
                if need.get(s, 0) < v:
                    need[s] = v
        for t in reads:
            add(t.w)
        for t in writes:
            add(t.w)
            add(t.r)
        waits = []
        for s, v in need.items():
            if s is eng.sem and eng.name == "pe":
                continue
            if eng.waited.get(s, 0) >= v:
                continue
            eng.waited[s] = v
            waits.append((s, v))
        return waits

    def _mark(self, ev, reads, writes):
        for t in reads:
            if t.r.get(ev[0], 0) < ev[1]:
                t.r[ev[0]] = ev[1]
        for t in writes:
            t.w = {ev[0]: ev[1]}
            t.r = {}

    def op(self, engname, fn, reads=(), writes=()):
        eng = self.engs[engname]
        waits = self._collect(eng, reads, writes)
        eng.count += 1
        ev = (eng.sem, eng.count)
        eng.ops.append((waits, fn, (eng.sem, 1)))
        self._mark(ev, reads, writes)
        return ev

    def dma(self, engname, streamname, out_t, out_ap, in_t, in_ap, **kw):
        eng = self.engs[engname]
        st = self.stream(streamname)
        waits = self._collect(eng, [in_t], [out_t])
        k = len(st.sems)
        idx = st.n
        st.n += 1
        sem = st.sems[idx % k]
        if idx >= k:
            pv = 16 * (idx // k)
            if eng.waited.get(sem, 0) < pv:
                eng.waited[sem] = pv
                waits.append((sem, pv))
        ev = (sem, 16 * (idx // k + 1))

        def fn(e, out_ap=out_ap, in_ap=in_ap, kw=kw):
            return e.dma_start(out=out_ap, in_=in_ap, **kw)
        eng.ops.append((waits, fn, (sem, 16)))
        self._mark(ev, [in_t], [out_t])
        return ev

    def custom(self, engname, fn, reads, writes, st, inc):
        eng = self.engs[engname]
        waits = self._collect(eng, reads, writes)
        if st.count and eng.waited.get(st.sem, 0) < st.count:
            eng.waited[st.sem] = st.count
            waits.append((st.sem, st.count))
        st.count += inc
        ev = (st.sem, st.count)
        eng.ops.append((waits, fn, (st.sem, inc)))
        self._mark(ev, reads, writes)
        return ev

    def barrier(self, full=False, pool=False):
        evs = {}
        for e in self.engs.values():
            if e.count:
                evs[e.sem] = e.count
        for s in self.streams.values():
            if not full and s.name in ("w", "cc"):
                continue
            if s.count:
                evs[s.sem] = s.count
            k = len(s.sems)
            for i in range(min(k, s.n)):
                evs[s.sems[i]] = 16 * ((s.n - 1 - i) // k + 1)
        for e in self.engs.values():
            if e.name == "pool" and not (full or pool):
                continue
            waits = []
            for s, v in evs.items():
                if s is e.sem:
                    continue
                if e.waited.get(s, 0) >= v:
                    continue
                e.waited[s] = v
                waits.append((s, v))
            if waits:
                e.ops.append((waits, None, None))

    def emit(self):
        engs = self.engs

        def replay(handle, ops):
            for waits, fn, inc in ops:
                for s, v in waits:
                    handle.wait_ge(s, v)
                if fn is None:
                    continue
                fn(handle).then_inc(inc[0], inc[1])

        with self.nc.allow_non_contiguous_dma(reason="strided layout DMAs (small)"), self.nc.Block() as block:
            @block.tensor
            def _(e):
                replay(e, engs["pe"].ops)

            @block.scalar
            def _(e):
                replay(e, engs["act"].ops)

            @block.vector
            def _(e):
                replay(e, engs["dve"].ops)

            @block.gpsimd
            def _(e):
                replay(e, engs["pool"].ops)

            @block.sync
            def _(e):
                replay(e, engs["sp"].ops)

    def close(self):
        self.stack.close()


class Pool:
    def __init__(self, tiles):
        self.tiles = tiles
        self.i = 0

    def get(self):
        t = self.tiles[self.i % len(self.tiles)]
        self.i += 1
        return t


def host_consts(core):
    j = core % 4
    bf = ml_dtypes.bfloat16
    c = {}
    t = np.arange(2048, dtype=np.float64) + 2048 * j
    row = np.floor(t / 64.0)
    col = t - 64.0 * row
    freqs = (10000.0 ** (-np.arange(32, dtype=np.float32) / np.float32(32))).astype(np.float64)
    ang = np.concatenate([row[:, None] * freqs, col[:, None] * freqs], -1)
    ang32 = np.concatenate([(row.astype(np.float32)[:, None] * freqs.astype(np.float32)),
                            (col.astype(np.float32)[:, None] * freqs.astype(np.float32))], -1).astype(np.float64)
    c["rope_cos"] = np.cos(ang32).astype(np.float32)
    c["rope_sin"] = np.sin(ang32).astype(np.float32)
    a = np.arange(64)[:, None]
    kl = np.arange(64)[None, :]
    th = 2 * np.pi * a * kl / 64.0
    c["L1"] = np.concatenate([np.cos(th), -np.sin(th)], 1).astype(bf)
    s = 1.0 / math.sqrt(8192 * 64)
    b = np.arange(128)[:, None, None]
    klo = np.arange(64)[None, :, None]
    kh = (32 * j + np.arange(32))[None, None, :]
    k = 64 * kh + klo
    th3 = 2 * np.pi * ((b * k) % 8192) / 8192.0
    Ere = s * np.cos(th3)
    Eim = -s * np.sin(th3)
    c["E3"] = np.concatenate([-Eim, Ere, Eim], 2).astype(bf)
    s2 = 1.0 / math.sqrt(256 * 64)
    n = np.arange(256)[:, None]
    kk = np.arange(256)[None, :]
    th2 = 2 * np.pi * ((n * kk) % 256) / 256.0
    d256 = np.concatenate([s2 * np.cos(th2), -s2 * np.sin(th2)], 1)
    c["D256"] = d256.reshape(2, 128, 512).transpose(1, 0, 2).astype(bf).copy()
    m = np.arange(64)[:, None]
    jj = np.arange(64)[None, :]
    thc = 2 * np.pi * ((m * jj) % 64) / 64.0
    c["CS64"] = np.concatenate([np.cos(thc), np.sin(thc)], 1).astype(bf)
    sidx = np.arange(128)[:, None].astype(np.float32)
    cidx = np.arange(128)[None, :].astype(np.float32)
    rc = np.zeros((128, 6, 128), np.float32)
    rc[:, 0, :] = np.maximum(cidx - sidx, 0)
    rc[:, 1, :] = np.maximum(sidx - cidx, 0)
    rc[:, 2, :] = (cidx >= sidx) * SK
    rc[:, 3, :] = (sidx > cidx) * SK
    rc[:, 4, :] = cidx + 1.0
    rc[:, 5, :] = 128.0 - cidx
    c["RC"] = rc
    wc = np.zeros((128, 2), np.float32)
    wc[:, 0] = 127.0 - np.arange(128)
    wc[:, 1] = np.arange(128)
    c["WC"] = wc
    mexp = np.zeros((2, 5), np.float32)
    mask = np.zeros((2, 5), np.float32)
    for jp in range(4):
        if jp < j:
            mexp[0, jp] = j - 1 - jp
            mask[0, jp] = 1
        if jp > j:
            mexp[1, jp] = jp - j - 1
            mask[1, jp] = 1
    mexp[0, 4] = j
    mask[0, 4] = 1
    mexp[1, 4] = 3 - j
    mask[1, 4] = 1
    cm = np.zeros((128, 2, 2, 5), np.float32)
    cm[:, 0] = mexp[None] * 2048.0
    cm[:, 1] = mask[None]
    c["CMX"] = cm.reshape(128, 20)
    c["IDENT"] = np.eye(128, dtype=np.float32).astype(bf)
    return c


CONST_SPECS = {
    "rope_cos": ([2048, 64], F32), "rope_sin": ([2048, 64], F32),
    "L1": ([64, 128], BF16), "E3": ([128, 64, 96], BF16), "D256": ([128, 2, 512], BF16),
    "CS64": ([64, 128], BF16), "RC": ([128, 6, 128], F32), "WC": ([128, 2], F32),
    "CMX": ([128, 20], F32), "IDENT": ([128, 128], BF16),
}

WEIGHT_SPECS = {
    "w_mod": [DEPTH, D, 6 * D], "b_mod": [DEPTH, 6 * D], "g_pre_mix": [DEPTH, D], "g_post_mix": [DEPTH, D],
    "g_pre_mlp": [DEPTH, D], "g_post_mlp": [DEPTH, D], "w_in": [DEPTH, D, IN_W],
    "ret_decay_logit": [DEPTH, 8], "sgu_w_s": [DEPTH, 4, 128, 128], "sgu_b_s": [DEPTH, 4, 128],
    "sgu_norm": [DEPTH, 256], "w_branch_a": [DEPTH, 512, D], "w_branch_b": [DEPTH, 256, D],
    "w_branch_c": [DEPTH, 256, D], "w_out": [DEPTH, D, D], "w_up": [DEPTH, D, 4 * D], "w_down": [DEPTH, 4 * D, D],
}


def build(debug=(), nlayers=DEPTH, stop_after=None):
    nc = bass.Bass("TRN2", target_bir_lowering=False)
    cx = Ctx(nc)
    dbg_out = {}

    x_in = cx.dram("x", [2048, D], F32, kind="ExternalInput")
    ctx_in = cx.dram("ctx", [256, D], F32, kind="ExternalInput")
    c2_in = cx.dram("c2", [2, D], F32, kind="ExternalInput")
    W = {k: cx.dram(k, shp, F32, kind="ExternalInput") for k, shp in WEIGHT_SPECS.items()}
    K = {k: cx.dram(k, shp, dt, kind="ExternalInput") for k, (shp, dt) in CONST_SPECS.items()}
    out_d = cx.dram("out", [2048, D], F32, kind="ExternalOutput")
    xs_d = cx.dram("xs_d", [2048, D], F32)
    cs_d = cx.dram("cs_d", [256, D], F32)
    modraw_d = cx.dram("modraw_d", [DEPTH, 2, 6 * D], F32)
    modv_d = cx.dram("modv_d", [DEPTH, 2, 2, 2, D], F32)
    hT_d = cx.dram("hT_d", [NCH // G, 128, 8, G * 128], BF16)
    hTc_d = cx.dram("hTc_d", [128, 8, G * 128], BF16)
    kv_d = cx.dram("kv_d", [NCH // G, 2, 128, G, 512], BF16)
    kvc_d = cx.dram("kvc_d", [2, 128, G, 512], BF16)
    U_d = cx.dram("U_d", [2, NCH, 128, 512], F32)
    Uc_d = cx.dram("Uc_d", [2, CTXCH, 128, 512], F32)
    S_d = cx.dram("S_d", [2, NCH, 128, 512], BF16)
    Sc_d = cx.dram("Sc_d", [2, CTXCH, 128, 512], BF16)
    st_loc = cx.dram("st_loc", [2 * 128, 512], F32)
    st_all = cx.dram("st_all", [4 * 2 * 128, 512], F32)
    f_loc = cx.dram("f_loc", [4 * 2048, 64], BF16)
    f_all = cx.dram("f_all", [4 * 4 * 2048, 64], BF16)
    zt_d = cx.dram("zt_d", [64, 8, 2048], BF16)
    wbp_d = cx.dram("wbp_d", [64, 8, D], BF16)
    cc = cx.stream("cc", k=1)

    def dbg(name, t, ap, shape, dtype=F32):
        if name not in debug:
            return
        o = cx.dram("dbg_" + name, list(shape), dtype, kind="ExternalOutput")
        cx.dma("sp", "dbg", o, o.h, t, ap)
        dbg_out[name] = (shape, dtype)

    WP = Pool([cx.sbuf(f"wp{i}", [128, 4096], BF16) for i in range(6)])
    XIN = Pool([cx.sbuf(f"xin{i}", [128, D], F32) for i in range(3)])
    JUNK = Pool([cx.sbuf(f"junk{i}", [128, D], BF16) for i in range(1)])
    TMPF = Pool([cx.sbuf(f"tmpf{i}", [128, D], F32) for i in range(2)])
    TMPH = Pool([cx.sbuf(f"tmph{i}", [128, 512], F32) for i in range(6)])
    HB = Pool([cx.sbuf(f"hb{i}", [128, D], BF16) for i in range(2)])
    TB = Pool([cx.sbuf(f"tb{i}", [128, 512], BF16) for i in range(16)])
    SFB = Pool([cx.sbuf(f"sfb{i}", [128, 2, 512], BF16) for i in range(4)])
    STAT = Pool([cx.sbuf(f"stat{i}", [128, 8], F32) for i in range(6)])
    MODA = cx.sbuf("modA", [128, D], F32)
    MODB = cx.sbuf("modB", [128, D], F32)
    MODG = cx.sbuf("modG", [128, D], F32)
    MODG2 = cx.sbuf("modG2", [128, D], F32)
    ROPE = cx.sbuf("rope", [128, 2, G, 64], F32)
    ROPE1 = cx.sbuf("rope1", [128, 2, CTXCH, 64], F32)
    ident = cx.sbuf("ident", [128, 128], BF16)
    RC = cx.sbuf("RC", [128, 6, 128], F32)
    WC = cx.sbuf("WC", [128, 2], F32)
    CMX = cx.sbuf("CMX", [128, 20], F32)
    LG = cx.sbuf("LG", [128, 8], F32)
    DT = cx.sbuf("DT", [128, 4, 128], F32)
    QF = cx.sbuf("QF", [128, 4, 128], F32)
    QB = cx.sbuf("QB", [128, 4, 128], F32)
    WFB = cx.sbuf("WFB", [128, 2, 4], F32)
    G128 = cx.sbuf("G128", [128, 8], F32)
    COEF = cx.sbuf("COEF", [128, 2, 5, 4], F32)
    epsb = cx.sbuf("epsb", [128, 2], F32)
    SCUR = [cx.sbuf(f"scur{i}", [128, 512], F32) for i in range(2)]
    SCTX = [cx.sbuf(f"sctx{i}", [128, 512], F32) for i in range(2)]
    SSTART = [cx.sbuf(f"sstart{i}", [128, 512], F32) for i in range(2)]
    c2T = cx.sbuf("c2T", [128, 8, 2], F32)
    c2Tb = cx.sbuf("c2Tb", [128, 8, 2], BF16)
    SGW = cx.sbuf("SGW", [128, 4, 128], BF16)
    SGWt = cx.sbuf("SGWt", [128, 4, 128], BF16)
    SGB = cx.sbuf("SGB", [128, 4], F32)
    SGN = cx.sbuf("SGN", [128, 256], F32)
    L1 = cx.sbuf("L1", [64, 128], BF16)
    CS64 = cx.sbuf("CS64", [64, 128], BF16)
    ZTC = cx.sbuf("ZTC", [64, 8, 256], BF16)

    PF = Pool([cx.psum(f"pf{i}", [128, 512], F32) for i in range(6)])
    PBT = Pool([cx.psum(f"pb{i}", [128, 1024], BF16) for i in range(2)])

    def O(eng, meth, writes, reads, *a, **k):
        return cx.op(eng, lambda e: getattr(e, meth)(*a, **k), reads=reads, writes=writes)

    def load(t, ap, src, sap, eng="sp", stream="ld"):
        cx.dma(eng, stream, t, ap, src, sap)

    def wload(t, ap, src, sap):
        cx.dma("pool", "w", t, ap, src, sap)

    cpy_i = [0]

    def evac(out_t, out_ap, in_t, in_ap, eng=None):
        cpy_i[0] += 1
        use_act = (cpy_i[0] % 3 != 0) if eng is None else (eng == "act")
        if use_act:
            O("act", "activation", [out_t], [in_t], out=out_ap, in_=in_ap, func=AF.Copy)
        else:
            O("dve", "tensor_copy", [out_t], [in_t], out=out_ap, in_=in_ap)

    def transpose_blocks(src_t, src_aps, dst_t, dst_ap, eng=None):
        n = len(src_aps)
        pb = PBT.get()
        for i, sap in enumerate(src_aps):
            O("pe", "transpose", [pb], [src_t, ident], out=pb[:, i * 128:(i + 1) * 128], in_=sap, identity=ident[:])
        evac(dst_t, dst_ap, pb, pb[:, 0:n * 128].rearrange("p (n c) -> p n c", n=n), eng=eng)

    for name, t in (("IDENT", ident), ("RC", RC), ("WC", WC), ("CMX", CMX), ("L1", L1), ("CS64", CS64)):
        load(t, t[:], K[name], K[name].h)
    O("dve", "memset", [epsb], [], epsb[:, 0:1], EPS)
    O("dve", "memset", [epsb], [], epsb[:, 1:2], 1.0)
    O("dve", "memset", [ROPE1], [], ROPE1[:, 0], 1.0)
    O("dve", "memset", [ROPE1], [], ROPE1[:, 1], 0.0)
    for r in range(2):
        load(c2T, c2T[:, :, r], c2_in, c2_in.h[r].rearrange("(kc p) -> p kc", p=128))
    O("act", "activation", [c2Tb], [c2T], out=c2Tb[:], in_=c2T[:], func=AF.Silu)

    def layer_setup(l):
        load(LG, LG[:], W["ret_decay_logit"], W["ret_decay_logit"].h[l:l + 1, :].broadcast_to([128, 8]))
        O("act", "activation", [LG], [LG], out=LG[:], in_=LG[:], func=AF.Exp, scale=-1.0)
        O("act", "activation", [LG], [LG, epsb], out=LG[:], in_=LG[:], func=AF.Ln, bias=epsb[:, 1:2])
        O("dve", "tensor_scalar", [LG], [LG], out=LG[:], in0=LG[:], scalar1=-1.0, scalar2=None, op0=ALU.mult)
        tA = TMPH.get()
        tB = TMPH.get()
        for h in range(4):
            O("act", "activation", [tA], [RC, LG], out=tA[:, 0:128], in_=RC[:, 0, :], func=AF.Exp, scale=LG[:, h:h + 1])
            O("act", "activation", [tB], [RC, LG], out=tB[:, 0:128], in_=RC[:, 1, :], func=AF.Exp, scale=LG[:, 4 + h:5 + h])
            O("dve", "tensor_tensor", [tA], [tA, RC], out=tA[:, 0:128], in0=tA[:, 0:128], in1=RC[:, 2, :], op=ALU.mult)
            O("dve", "tensor_tensor", [tB], [tB, RC], out=tB[:, 0:128], in0=tB[:, 0:128], in1=RC[:, 3, :], op=ALU.mult)
            O("dve", "tensor_tensor", [DT], [tA, tB], out=DT[:, h, :], in0=tA[:, 0:128], in1=tB[:, 0:128], op=ALU.add)
            O("act", "activation", [QF], [RC, LG], out=QF[:, h, :], in_=RC[:, 4, :], func=AF.Exp, scale=LG[:, h:h + 1])
            O("act", "activation", [QB], [RC, LG], out=QB[:, h, :], in_=RC[:, 5, :], func=AF.Exp, scale=LG[:, 4 + h:5 + h])
        for d in range(2):
            O("act", "activation", [WFB], [LG, WC], out=WFB[:, d, :], in_=LG[:, 4 * d:4 * d + 4], func=AF.Exp,
              scale=WC[:, d:d + 1])
        O("dve", "tensor_scalar", [WFB], [WFB], out=WFB[:], in0=WFB[:], scalar1=SK, scalar2=None, op0=ALU.mult)
        O("act", "activation", [G128], [LG], out=G128[:], in_=LG[:], func=AF.Exp, scale=128.0)
        cmv = CMX[:].rearrange("p (a d s) -> p a d s", a=2, d=2)
        for d in range(2):
            for s in range(5):
                O("act", "activation", [COEF], [LG, CMX], out=COEF[:, d, s, :], in_=LG[:, 4 * d:4 * d + 4], func=AF.Exp,
                  scale=cmv[:, 0, d, s:s + 1])
                O("dve", "tensor_scalar", [COEF], [COEF, CMX], out=COEF[:, d, s, :], in0=COEF[:, d, s, :],
                  scalar1=cmv[:, 1, d, s:s + 1], scalar2=None, op0=ALU.mult)

        wt = WP.get()
        wf = wt[:, 0:512].rearrange("p (g s) -> p g s", g=4)
        wload(wt, wf, W["sgu_w_s"], W["sgu_w_s"].h[l].rearrange("g t s -> t g s"))
        O("dve", "tensor_copy", [SGWt], [wt], out=SGWt[:], in_=wf)
        transpose_blocks(SGWt, [SGWt[:, g, :] for g in range(4)], SGW, SGW[:])
        with nc.allow_non_contiguous_dma(reason="tiny transposed bias load"):
            load(SGB, SGB[:], W["sgu_b_s"], W["sgu_b_s"].h[l].rearrange("g t -> t g"))
        load(SGN, SGN[:], W["sgu_norm"], W["sgu_norm"].h[l:l + 1, :].broadcast_to([128, 256]))

        WBR = WP.get()
        WBRv = WBR[0:64, :].rearrange("p (g d) -> p g d", g=4)
        wload(WBR, WBRv, W["w_branch_b"], W["w_branch_b"].h[l].rearrange("(g j) d -> j g d", j=64))
        for g in range(4):
            for c in range(2):
                tb = TB.get()
                tb2 = TB.get()
                for half, tt in enumerate((tb, tb2)):
                    ps = PF.get()
                    O("pe", "matmul", [ps], [CS64, WBR], ps[0:64, :], lhsT=CS64[:, c * 64:(c + 1) * 64],
                      rhs=WBRv[:, g, half * 512:(half + 1) * 512], start=True, stop=True)
                    evac(tt, tt[0:64, :], ps, ps[0:64, :])
                    cx.dma("sp", "st", wbp_d, wbp_d.h[:, g * 2 + c, half * 512:(half + 1) * 512], tt, tt[0:64, :])


    def setup_mod(l):
        for cb in range(12):
            wt = WP.get()
            wv = wt[:].rearrange("p (kc n) -> p kc n", kc=8)
            wload(wt, wv, W["w_mod"], W["w_mod"].h[l, :, cb * 512:(cb + 1) * 512].rearrange("(kc p) n -> p kc n", p=128))
            bch = TMPH.get()
            load(bch, bch[0:2, :], W["b_mod"], W["b_mod"].h[l:l + 1, cb * 512:(cb + 1) * 512].broadcast_to([2, 512]))
            ps = PF.get()
            for kc in range(8):
                O("pe", "matmul", [ps], [c2Tb, wt], ps[0:2, :], lhsT=c2Tb[:, kc, :], rhs=wv[:, kc, :],
                  start=(kc == 0), stop=(kc == 7))
            O("dve", "tensor_tensor", [bch], [ps, bch], out=bch[0:2, :], in0=ps[0:2, :], in1=bch[0:2, :], op=ALU.add)
            cx.dma("sp", "st", modraw_d, modraw_d.h[l][:, cb * 512:(cb + 1) * 512], bch, bch[0:2, :])

        setup_modv(l)

    def setup_modv(l):
        def bc(t, ap, src, sap):
            load(t, ap, src, sap.broadcast_to([128, D]))
        for row in range(2):
            for which in range(2):
                o = 3 * which
                gpre = W["g_pre_mix"] if which == 0 else W["g_pre_mlp"]
                gpost = W["g_post_mix"] if which == 0 else W["g_post_mlp"]
                ta = TMPF.get()
                t1 = TMPF.get()
                bc(ta, ta[:], modraw_d, modraw_d.h[l][row:row + 1, (o + 1) * D:(o + 2) * D])
                bc(t1, t1[:], gpre, gpre.h[l:l + 1, :])
                O("dve", "scalar_tensor_tensor", [ta], [ta, t1], out=ta[:], in0=ta[:], scalar=1.0, in1=t1[:], op0=ALU.add, op1=ALU.mult)
                cx.dma("sp", "st", modv_d, modv_d.h[l][row, which, 0:1, :], ta, ta[0:1, :])
                tg = TMPF.get()
                t2 = TMPF.get()
                bc(tg, tg[:], modraw_d, modraw_d.h[l][row:row + 1, (o + 2) * D:(o + 3) * D])
                bc(t2, t2[:], gpost, gpost.h[l:l + 1, :])
                O("dve", "tensor_tensor", [tg], [tg, t2], out=tg[:], in0=tg[:], in1=t2[:], op=ALU.mult)
                cx.dma("sp", "st", modv_d, modv_d.h[l][row, which, 1:2, :], tg, tg[0:1, :])

    def load_vec(t, row, which, kind, l):
        if kind == "B":
            src, sap = modraw_d, modraw_d.h[l][row:row + 1, (3 * which) * D:(3 * which + 1) * D]
        else:
            i = 0 if kind == "A" else 1
            src, sap = modv_d, modv_d.h[l][row, which, i:i + 1, :]
        load(t, t[:], src, sap.broadcast_to([128, D]))

    def g_rms_rstd(src_t, src_aps, n, st):
        srcs = src_t if isinstance(src_t, (list, tuple)) else [src_t] * len(src_aps)
        O("dve", "memset", [st], [], st[:], 0.0)
        yield
        single = len(src_aps) == 1
        for i, sap in enumerate(src_aps):
            junk = JUNK.get()
            jv = junk[:, 0:sap.shape[-1]]
            col = 0 if single else 4 + i
            O("act", "activation", [junk, st], [srcs[i], st], out=jv, in_=sap, func=AF.Square, accum_out=st[:, col:col + 1])
            yield
        if not single:
            O("dve", "tensor_tensor", [st], [st], out=st[:, 0:1], in0=st[:, 4:5], in1=st[:, 5:6], op=ALU.add)
            yield
        O("act", "activation", [st], [st, epsb], out=st[:, 1:2], in_=st[:, 0:1], func=AF.Sqrt, scale=1.0 / n, bias=epsb[:, 0:1])
        yield
        O("dve", "reciprocal", [st], [st], out=st[:, 2:3], in_=st[:, 1:2])
        yield

    def run(gen):
        for _ in gen:
            pass

    def interleave(gens):
        gens = list(gens)
        while gens:
            for g in list(gens):
                try:
                    next(g)
                except StopIteration:
                    gens.remove(g)

    def rms_rstd(src_t, src_aps, n, st):
        run(g_rms_rstd(src_t, src_aps, n, st))

    def g_norm_mod_T(xt, dstT, ci, out=None):
        st = STAT.get()
        yield from g_rms_rstd(xt, [xt[:]], D, st)
        tmp = TMPF.get()
        O("dve", "scalar_tensor_tensor", [tmp], [xt, st, MODA], out=tmp[:], in0=xt[:], scalar=st[:, 2:3], in1=MODA[:],
          op0=ALU.mult, op1=ALU.mult)
        yield
        hb = HB.get()
        O("dve", "tensor_tensor", [hb], [tmp, MODB], out=hb[:], in0=tmp[:], in1=MODB[:], op=ALU.add)
        if out is not None:
            out.append(hb)
        yield
        transpose_blocks(hb, [hb[:, kc * 128:(kc + 1) * 128] for kc in range(8)], dstT, dstT[:, :, ci * 128:(ci + 1) * 128])
        yield

    def norm_mod_T(xt, dstT, ci, add_eng="dve"):
        out = []
        run(g_norm_mod_T(xt, dstT, ci, out))
        return out[0]

    def zblock(hT, ci, wv, ncols, col0=0):
        ps = PF.get()
        for kc in range(8):
            O("pe", "matmul", [ps], [hT, wv.tile], ps[:, 0:ncols], lhsT=hT[:, kc, ci * 128:(ci + 1) * 128],
              rhs=wv.ap[:, kc, col0:col0 + ncols], start=(kc == 0), stop=(kc == 7))
        return ps

    class WV:
        def __init__(self, tile, ap):
            self.tile = tile
            self.ap = ap

    def load_win(l, c0, ncols):
        wt = WP.get()
        wv = wt[:, 0:8 * ncols].rearrange("p (kc n) -> p kc n", kc=8)
        wload(wt, wv, W["w_in"], W["w_in"].h[l, :, c0:c0 + ncols].rearrange("(kc p) n -> p kc n", p=128))
        return WV(wt, wv)

    def rope(ps, ropet, ci, dst_t, dst_ap, comb_eng="dve"):
        pv = ps[:].rearrange("p (h t i) -> p h t i", h=4, t=2)
        cosb = ropet[:, 0, ci, :].unsqueeze(1).broadcast_to([128, 4, 64])
        sinb = ropet[:, 1, ci, :].unsqueeze(1).broadcast_to([128, 4, 64])
        t1 = TMPH.get()
        t2 = TMPH.get()
        t1v = t1[:].rearrange("p (h t i) -> p h t i", h=4, t=2)
        t2v = t2[:].rearrange("p (h t i) -> p h t i", h=4, t=2)
        dv = dst_ap.rearrange("p (h t i) -> p h t i", h=4, t=2)
        O("dve", "tensor_tensor", [t1], [ps, ropet], out=t1v[:, :, 0, :], in0=pv[:, :, 0, :], in1=cosb, op=ALU.mult)
        O("dve", "tensor_tensor", [t1], [ps, ropet], out=t1v[:, :, 1, :], in0=pv[:, :, 1, :], in1=cosb, op=ALU.mult)
        O("dve", "tensor_tensor", [t2], [ps, ropet], out=t2v[:, :, 0, :], in0=pv[:, :, 1, :], in1=sinb, op=ALU.mult)
        O("dve", "tensor_tensor", [t2], [ps, ropet], out=t2v[:, :, 1, :], in0=pv[:, :, 0, :], in1=sinb, op=ALU.mult)
        O(comb_eng, "tensor_tensor", [dst_t], [t1, t2], out=dv[:, :, 0, :], in0=t1v[:, :, 0, :], in1=t2v[:, :, 0, :], op=ALU.subtract)
        O(comb_eng, "tensor_tensor", [dst_t], [t1, t2], out=dv[:, :, 1, :], in0=t1v[:, :, 1, :], in1=t2v[:, :, 1, :], op=ALU.add)

    def alloc_passA(es):
        return dict(hT=cx.sbuf("hT", [128, 8, G * 128], BF16, stack=es), k_tok=cx.sbuf("k_tok", [128, G, 512], BF16, stack=es),
                    vf=cx.sbuf("vf", [128, G, 512], BF16, stack=es), vb=cx.sbuf("vb", [128, G, 512], BF16, stack=es),
                    f_tok=cx.sbuf("f_tok", [128, G, 256], BF16, stack=es), v_pl=cx.sbuf("v_pl", [128, G, 512], BF16, stack=es))

    def pass_A(l, grp, bufs=None):
        nch, xsrc, x_ap, is_ctx = grp["nch"], grp["xsrc"], grp["x_ap"], grp["is_ctx"]
        ropet = ROPE1 if is_ctx else ROPE
        es = None
        if bufs is None:
            es = ExitStack()
            bufs = alloc_passA(es)
        hT, k_tok, vf, vb, f_tok, v_pl = bufs["hT"], bufs["k_tok"], bufs["vf"], bufs["vb"], bufs["f_tok"], bufs["v_pl"]
        kvd_ap = kvc_d.h if is_ctx else kv_d.h[grp["c0"] // G]
        kvd = kvc_d if is_ctx else kv_d
        k3, vf3, vb3, f3 = k_tok[:], vf[:], vb[:], f_tok[:]
        row = 1 if is_ctx else 0
        load_vec(MODA, row, 0, "A", l)
        load_vec(MODB, row, 0, "B", l)
        if not is_ctx:
            c0 = grp["c0"]
            for i, nm in enumerate(("rope_cos", "rope_sin")):
                load(ROPE, ROPE[:, i], K[nm], K[nm].h[c0 * 128:(c0 + nch) * 128, :].rearrange("(c p) i -> p c i", p=128))
        for c2 in range(0, nch, 2):
            gens = []
            for ci in range(c2, min(c2 + 2, nch)):
                xt = XIN.get()
                load(xt, xt[:], xsrc, x_ap(ci))
                gens.append(g_norm_mod_T(xt, hT, ci))
            interleave(gens)
        hTd = hTc_d if is_ctx else hT_d
        hTd_ap = hTc_d.h if is_ctx else hT_d.h[grp["c0"] // G]
        cx.dma("sp", "st", hTd, hTd_ap[:, :, 0:nch * 128], hT, hT[:, :, 0:nch * 128])
        wv = load_win(l, 512, 512)
        for ci in range(nch):
            ps = zblock(hT, ci, wv, 512)
            rope(ps, ropet, ci, k_tok, k3[:, ci, :])
        cx.dma("sp", "st", kvd, kvd_ap[0, :, 0:nch, :], k_tok, k3[:, 0:nch, :])
        wv = load_win(l, 1024, 512)
        for ci in range(nch):
            ps = zblock(hT, ci, wv, 512)
            pv = ps[:].rearrange("p (h e) -> p h e", h=4)
            O("dve", "tensor_tensor", [vf], [ps, WFB], out=vf3[:, ci, :].rearrange("p (h e) -> p h e", h=4), in0=pv,
              in1=WFB[:, 0, :].unsqueeze(2).broadcast_to([128, 4, 128]), op=ALU.mult)
            O("dve", "tensor_tensor", [vb], [ps, WFB], out=vb3[:, ci, :].rearrange("p (h e) -> p h e", h=4), in0=pv,
              in1=WFB[:, 1, :].unsqueeze(2).broadcast_to([128, 4, 128]), op=ALU.mult)
            O("dve", "tensor_copy", [v_pl], [ps], out=v_pl[:, ci, :], in_=ps[:])
        cx.dma("sp", "st", kvd, kvd_ap[1, :, 0:nch, :], v_pl, v_pl[:, 0:nch, :])
        wv = load_win(l, 2048, 256)
        for ci in range(nch):
            ps = zblock(hT, ci, wv, 256)
            evac(f_tok, f3[:, ci, :], ps, ps[:, 0:256])
            if not is_ctx:
                tok0 = (grp["c0"] + ci) * 128
                with nc.allow_non_contiguous_dma(reason="quarter-major f layout for the Fourier exchange"):
                    cx.dma("sp", "st", f_loc,
                           f_loc.h.rearrange("(q t) c -> t q c", q=4)[tok0:tok0 + 128, :, :], f_tok,
                           f3[:, ci, :].rearrange("p (q c) -> p q c", q=4))
        Ud = Uc_d if is_ctx else U_d
        for ci in range(nch):
            gci = ci if is_ctx else grp["c0"] + ci
            for d, vw in enumerate((vf3, vb3)):
                ps = PF.get()
                for h in range(4):
                    O("pe", "matmul", [ps], [k_tok, vf if d == 0 else vb], ps[:, h * 128:(h + 1) * 128],
                      lhsT=k3[:, ci, h * 128:(h + 1) * 128], rhs=vw[:, ci, h * 128:(h + 1) * 128], start=True, stop=True)
                tmp = TMPH.get()
                evac(tmp, tmp[:], ps, ps[:])
                cx.dma("sp", "st", Ud, Ud.h[d, gci], tmp, tmp[:])
                if not is_ctx:
                    pw = STAT.get()
                    O("act", "activation", [pw], [LG], out=pw[:, 0:4], in_=LG[:, 4 * d:4 * d + 4], func=AF.Exp,
                      scale=float(128 * ((NCH - 1 - gci) if d == 0 else gci)))
                    t2 = TMPH.get()
                    O("dve", "tensor_tensor", [t2], [tmp, pw], out=t2[:].rearrange("p (h e) -> p h e", h=4),
                      in0=tmp[:].rearrange("p (h e) -> p h e", h=4), in1=pw[:, 0:4].unsqueeze(2).broadcast_to([128, 4, 128]), op=ALU.mult)
                    if gci == 0:
                        O("dve", "tensor_copy", [SCUR[d]], [t2], out=SCUR[d][:], in_=t2[:])
                    else:
                        O("dve", "tensor_tensor", [SCUR[d]], [SCUR[d], t2], out=SCUR[d][:], in0=SCUR[d][:], in1=t2[:], op=ALU.add)
        if is_ctx:
            dbg("kctx_l%d" % l, k_tok, k3[:, 0, :], [128, 512], BF16)
            if not grp["last"]:
                fourier_ctx(f_tok)
        else:
            if grp["c0"] == 0:
                dbg("k0_l%d" % l, k_tok, k3[:, 0, :], [128, 512], BF16)
        if es is not None:
            cx.barrier()
            es.close()

    def decay_mul(S, d):
        sv = S[:].rearrange("p (h e) -> p h e", h=4)
        O("dve", "tensor_tensor", [S], [S, G128], out=sv, in0=sv,
          in1=G128[:, 4 * d:4 * d + 4].unsqueeze(2).broadcast_to([128, 4, 128]), op=ALU.mult)

    def recur_steps(Ud, Sd, nch, start, store):
        steps = []

        def init(d):
            S = SCUR[d]
            if start is None:
                O("dve", "memset", [S], [], S[:], 0.0)
            else:
                O("dve", "tensor_copy", [S], [start[d]], out=S[:], in_=start[d][:])

        def step(d, ci):
            S = SCUR[d]
            if store:
                sb = TB.get()
                O("dve", "tensor_copy", [sb], [S], out=sb[:], in_=S[:])
                cx.dma("sp", "st", Sd, Sd.h[d, ci], sb, sb[:])
            u = TMPH.get()
            load(u, u[:], Ud, Ud.h[d, ci])
            decay_mul(S, d)
            O("dve", "tensor_tensor", [S], [S, u], out=S[:], in0=S[:], in1=u[:], op=ALU.add)

        steps.append(lambda: (init(0), init(1)))
        for k in range(nch):
            steps.append(lambda k=k: step(0, k))
            steps.append(lambda k=k: step(1, nch - 1 - k))
        return steps

    def recur(Ud, Sd, nch, start, store):
        for f in recur_steps(Ud, Sd, nch, start, store):
            f()

    def exchange_issue():
        for d in range(2):
            cx.dma("sp", "st", st_loc, st_loc.h[d * 128:(d + 1) * 128, :], SCUR[d], SCUR[d][:])
        cx.custom("pool", lambda e: e.collective_compute("AllGather", ALU.bypass, replica_groups=[[0, 1, 2, 3], [4, 5, 6, 7]],
                                                         ins=[st_loc.h.opt()], outs=[st_all.h.opt()]),
                  reads=[st_loc], writes=[st_all], st=cc, inc=1)

    def exchange_combine():
        for d in range(2):
            S = SSTART[d]
            sv = S[:].rearrange("p (h e) -> p h e", h=4)
            O("dve", "tensor_tensor", [S], [SCTX[d], COEF], out=sv, in0=SCTX[d][:].rearrange("p (h e) -> p h e", h=4),
              in1=COEF[:, d, 4, :].unsqueeze(2).broadcast_to([128, 4, 128]), op=ALU.mult)
            for r in range(4):
                u = TMPH.get()
                load(u, u[:], st_all, st_all.h[(r * 2 + d) * 128:(r * 2 + d + 1) * 128, :])
                O("dve", "tensor_tensor", [u], [u, COEF], out=u[:].rearrange("p (h e) -> p h e", h=4),
                  in0=u[:].rearrange("p (h e) -> p h e", h=4),
                  in1=COEF[:, d, r, :].unsqueeze(2).broadcast_to([128, 4, 128]), op=ALU.mult)
                O("dve", "tensor_tensor", [S], [S, u], out=S[:], in0=S[:], in1=u[:], op=ALU.add)

    def fourier_ctx(f_tok):
        f3 = f_tok[:]
        dt_ = WP.get()
        D256 = dt_[:, 0:1024].rearrange("p (n k) -> p n k", n=2)
        load(dt_, D256, K["D256"], K["D256"].h)
        for q in range(4):
            ps = PF.get()
            for nchk in range(2):
                O("pe", "matmul", [ps], [f_tok, dt_], ps[0:64, :], lhsT=f3[:, nchk, q * 64:(q + 1) * 64],
                  rhs=D256[:, nchk, :], start=(nchk == 0), stop=(nchk == 1))
            evac(ZTC, ZTC[:, 2 * q:2 * q + 2, :], ps, ps[0:64, :].rearrange("p (c k) -> p c k", c=2))

    def fourier_gather():
        cx.custom("pool", lambda e: e.collective_compute("AllGather", ALU.bypass, replica_groups=[[0, 1, 2, 3], [4, 5, 6, 7]],
                                                         ins=[f_loc.h.opt()], outs=[f_all.h.opt()]),
                  reads=[f_loc], writes=[f_all], st=cc, inc=1)

    def fourier_latent(side_steps=None):
        fav = f_all.h.rearrange("(r q a b) c -> r q a (b c)", r=4, q=4, a=16)
        cx.barrier(pool=True)
        es = ExitStack()
        X_t = cx.sbuf("fX", [64, 8192], BF16, stack=es)
        TT_t = cx.sbuf("fTT", [128, 8192], BF16, stack=es)
        E3 = cx.sbuf("E3", [128, 64, 96], BF16, stack=es)
        load(E3, E3[:], K["E3"], K["E3"].h, eng="pool", stream="fld")
        X, TT = X_t[:], TT_t[:]
        for q in range(4):
            for r in range(4):
                load(X_t, X[r * 16:(r + 1) * 16, 0:128 * 64], f_all, fav[r, q], eng="pool", stream="fld")
            Xv = X[0:64, :].rearrange("p (b c) -> p c b", c=64)
            TTv = TT[:, 0:64 * 128].rearrange("p (c k) -> p c k", c=64)
            for cg in range(16):
                if side_steps and cg % 2 == 0:
                    side_steps.pop(0)()
                ps = PF.get()
                for i in range(4):
                    O("pe", "matmul", [ps], [L1, X_t], ps[:, i * 128:(i + 1) * 128], lhsT=Xv[:, cg * 4 + i, :], rhs=L1[:], start=True, stop=True)
                evac(TT_t, TTv[:, cg * 4:(cg + 1) * 4, :], ps, ps[:].rearrange("p (c k) -> p c k", c=4), eng="act")
            zq = WP.get()
            zqv = zq[0:64, :].rearrange("p (c kh kl) -> p c kl kh", c=2, kl=64)
            for kg in range(8):
                ps = PF.get()
                for i in range(8):
                    klo = kg * 8 + i
                    O("pe", "matmul", [ps], [TT_t, E3], ps[0:64, i * 64:(i + 1) * 64], lhsT=TTv[:, :, klo], rhs=E3[:, klo, 32:96],
                      start=True, stop=False)
                    O("pe", "matmul", [ps], [TT_t, E3], ps[0:64, i * 64:(i + 1) * 64], lhsT=TTv[:, :, 64 + klo], rhs=E3[:, klo, 0:64],
                      start=False, stop=True)
                psv = ps[0:64, :].rearrange("p (kl c kh) -> p kl c kh", kl=8, c=2)
                for c in range(2):
                    evac(zq, zqv[:, c, kg * 8:(kg + 1) * 8, :], ps, psv[:, :, c, :], eng="act")
            cx.dma("pool", "fst", zt_d, zt_d.h[:, 2 * q:2 * q + 2, :], zq, zq[0:64, :].rearrange("p (c t) -> p c t", c=2))
        return es

    def g_post_norm_residual(xt, ysrc_t, y_aps, st, gt=None):
        srcs = ysrc_t if isinstance(ysrc_t, (list, tuple)) else [ysrc_t] * len(y_aps)
        yield from g_rms_rstd(srcs, y_aps, D, st)
        gtt = gt or MODG
        for i, yap in enumerate(y_aps):
            tmp = TMPH.get()
            O("dve", "scalar_tensor_tensor", [tmp], [srcs[i], st, gtt], out=tmp[:], in0=yap, scalar=st[:, 2:3],
              in1=gtt[:, i * 512:(i + 1) * 512], op0=ALU.mult, op1=ALU.mult)
            yield
            O("dve", "tensor_tensor", [xt], [tmp, xt], out=xt[:, i * 512:(i + 1) * 512], in0=tmp[:],
              in1=xt[:, i * 512:(i + 1) * 512], op=ALU.add)
            yield

    def post_norm_residual(xt, ysrc_t, y_aps, st, gt=None):
        run(g_post_norm_residual(xt, ysrc_t, y_aps, st, gt))

    def alloc_passB(es):
        return dict(mT=cx.sbuf("mT", [128, 8, G * 128], BF16, stack=es), hT=cx.sbuf("hT", [128, 8, G * 128], BF16, stack=es),
                    big16=cx.sbuf("big16", [128, G * D], F32, stack=es), r4=cx.sbuf("r4", [128, 4, G * 128], BF16, stack=es),
                    sguT=cx.sbuf("sguT", [128, 2, G * 128], BF16, stack=es), ZTL=cx.sbuf("ZTL", [64, 8, G * 128], BF16, stack=es))

    def pass_B(l, grp, last, bufs, prev_fin=None):
        nch, xsrc, x_ap, is_ctx = grp["nch"], grp["xsrc"], grp["x_ap"], grp["is_ctx"]
        xdst, xd_ap = grp["xdst"], grp["xd_ap"]
        xmid, xm_ap = grp["xmid"], grp["xm_ap"]
        T = nch * 128
        ropet = ROPE1 if is_ctx else ROPE
        Sd = Sc_d if is_ctx else S_d
        mT, hT, big16, r4, sguT, ZTL = bufs["mT"], bufs["hT"], bufs["big16"], bufs["r4"], bufs["sguT"], bufs["ZTL"]
        qT = kT = v_tok = gs = big16
        retT = aT = r4
        B16 = big16[:].bitcast(BF16)
        qT3 = B16[:, 0:2048].rearrange("p (h t) -> p h t", h=4)
        kT3 = B16[:, 2048:4096].rearrange("p (h t) -> p h t", h=4)
        v3 = B16[:, 4096:6144].rearrange("p (c n) -> p c n", c=G)
        gs3 = B16[:, 6144:8192].rearrange("p (c n) -> p c n", c=G)
        y2v = big16[:].rearrange("p (c d) -> p c d", c=G)
        retT3, sguT3 = r4[:], sguT[:]
        tag = "l%d_%s" % (l, "c" if is_ctx else "x%d" % grp["c0"])
        row = 1 if is_ctx else 0

        def prefetch_inputs(g):
            gctx = g["is_ctx"]
            Tg = g["nch"] * 128
            hTd = hTc_d if gctx else hT_d
            hTd_ap = hTc_d.h if gctx else hT_d.h[g["c0"] // G]
            load(hT, hT[:, :, 0:Tg], hTd, hTd_ap[:, :, 0:Tg])
            if not gctx:
                c0g = g["c0"]
                for i, nm in enumerate(("rope_cos", "rope_sin")):
                    load(ROPE, ROPE[:, i], K[nm], K[nm].h[c0g * 128:(c0g + g["nch"]) * 128, :].rearrange("(c p) i -> p c i", p=128))

        if not grp.get("prefetched"):
            prefetch_inputs(grp)
        if grp.get("load_mod", True):
            load_vec(MODG, row, 0, "G", l)
            load_vec(MODA, row, 1, "A", l)
            load_vec(MODB, row, 1, "B", l)
            load_vec(MODG2, row, 1, "G", l)
        wv = load_win(l, 2304, 512)

        def g_uvs_b(ci, ps):
            gu = TMPH.get()
            tt = TMPH.get()
            O("act", "activation", [gu], [ps], out=gu[:], in_=ps[:], func=AF.Gelu_apprx_tanh)
            yield
            st = STAT.get()
            O("act", "activation", [tt], [gu], out=tt[:, 0:256], in_=gu[:, 256:512], func=AF.Square)
            yield
            O("dve", "tensor_reduce", [st], [tt], out=st[:, 0:4], in_=tt[:, 0:256].rearrange("p (g c) -> p g c", g=4),
              axis=mybir.AxisListType.X, op=ALU.add)
            yield
            O("act", "activation", [st], [st, epsb], out=st[:, 0:4], in_=st[:, 0:4], func=AF.Sqrt, scale=1.0 / 64, bias=epsb[:, 0:1])
            yield
            O("dve", "reciprocal", [st], [st], out=st[:, 4:8], in_=st[:, 0:4])
            yield
            O("dve", "tensor_tensor", [tt], [gu, st], out=tt[:, 0:256].rearrange("p (g c) -> p g c", g=4),
              in0=gu[:, 256:512].rearrange("p (g c) -> p g c", g=4), in1=st[:, 4:8].unsqueeze(2).broadcast_to([128, 4, 64]), op=ALU.mult)
            yield
            vnb = TB.get()
            O("dve", "tensor_tensor", [vnb], [tt, SGN], out=vnb[:, 0:256], in0=tt[:, 0:256], in1=SGN[:], op=ALU.mult)
            yield
            ps2 = PF.get()
            for g in range(4):
                O("pe", "matmul", [ps2], [SGW, vnb], ps2[:, g * 64:(g + 1) * 64], lhsT=SGW[:, g, :], rhs=vnb[:, g * 64:(g + 1) * 64],
                  start=True, stop=True)
            O("dve", "tensor_tensor", [tt], [ps2, SGB], out=tt[:, 256:512].rearrange("p (g c) -> p g c", g=4),
              in0=ps2[:, 0:256].rearrange("p (g c) -> p g c", g=4), in1=SGB[:].unsqueeze(2).broadcast_to([128, 4, 64]), op=ALU.add)
            yield
            sgb = TB.get()
            O("dve", "tensor_tensor", [sgb], [tt, gu], out=sgb[:, 0:256], in0=tt[:, 256:512], in1=gu[:, 0:256], op=ALU.mult)
            if ci == 0:
                dbg("sgu_" + tag, sgb, sgb[:, 0:256], [128, 256], BF16)
            yield
            transpose_blocks(sgb, [sgb[:, h * 128:(h + 1) * 128] for h in range(2)], sguT, sguT3[:, :, ci * 128:(ci + 1) * 128])
            yield

        prev_fin = list(prev_fin or [])
        for c2 in range(0, nch, 2):
            cis = list(range(c2, min(c2 + 2, nch)))
            pss = [zblock(hT, ci, wv, 512) for ci in cis]
            interleave([g_uvs_b(ci, ps) for ci, ps in zip(cis, pss)])
            if prev_fin:
                prev_fin.pop(0)()
        while prev_fin:
            prev_fin.pop(0)()
        wv = load_win(l, 0, 512)

        def q_b(ci, ps):
            tb = TB.get()
            rope(ps, ropet, ci, tb, tb[:])
            transpose_blocks(tb, [tb[:, h * 128:(h + 1) * 128] for h in range(4)], qT, qT3[:, :, ci * 128:(ci + 1) * 128])
        pend = zblock(hT, 0, wv, 512)
        for ci in range(nch):
            nxt = zblock(hT, ci + 1, wv, 512) if ci + 1 < nch else None
            q_b(ci, pend)
            pend = nxt
        kvd_ap = kvc_d.h if is_ctx else kv_d.h[grp["c0"] // G]
        kvd = kvc_d if is_ctx else kv_d
        load(r4, r4[:].rearrange("p h t -> p (h t)").rearrange("p (c n) -> p c n", c=G)[:, 0:nch, :], kvd, kvd_ap[0, :, 0:nch, :])
        load(big16, v3[:, 0:nch, :], kvd, kvd_ap[1, :, 0:nch, :])
        kst = r4[:].rearrange("p h t -> p (h t)").rearrange("p (c n) -> p c n", c=G)
        for ci in range(nch):
            transpose_blocks(r4, [kst[:, ci, h * 128:(h + 1) * 128] for h in range(4)], kT, kT3[:, :, ci * 128:(ci + 1) * 128])
        wv = load_win(l, 1536, 512)
        for ci in range(nch):
            ps = zblock(hT, ci, wv, 512)
            O("act", "activation", [gs], [ps], out=gs3[:, ci, :], in_=ps[:], func=AF.Silu)
        def b3_s1(ci):
            gci = ci if is_ctx else grp["c0"] + ci
            sl = slice(ci * 128, (ci + 1) * 128)
            sfb = SFB.get()
            load(sfb, sfb[:], Sd, Sd.h[:, gci].rearrange("d p n -> p d n"))
            Sf = Sb = sfb
            ps = PF.get()
            for h in range(4):
                O("pe", "matmul", [ps], [kT, qT], ps[:, h * 128:(h + 1) * 128], lhsT=kT3[:, h, sl], rhs=qT3[:, h, sl], start=True, stop=True)
            scm = TB.get()
            O("dve", "tensor_tensor", [scm], [ps, DT], out=scm[:], in0=ps[:], in1=DT[:].rearrange("p h c -> p (h c)"), op=ALU.mult)
            qf = TB.get()
            qb = TB.get()
            O("dve", "tensor_tensor", [qf], [qT, QF], out=qf[:].rearrange("p (h c) -> p h c", h=4), in0=qT3[:, :, sl], in1=QF[:], op=ALU.mult)
            O("dve", "tensor_tensor", [qb], [qT, QB], out=qb[:].rearrange("p (h c) -> p h c", h=4), in0=qT3[:, :, sl], in1=QB[:], op=ALU.mult)
            return Sf, Sb, scm, qf, qb

        def g_b3_s2(ci, Sf, Sb, scm, qf, qb):
            sl = slice(ci * 128, (ci + 1) * 128)
            po = PF.get()
            for h in range(4):
                hs = slice(h * 128, (h + 1) * 128)
                O("pe", "matmul", [po], [scm, v_tok], po[:, hs], lhsT=scm[:, hs], rhs=v3[:, ci, hs], start=True, stop=False)
                O("pe", "matmul", [po], [qf, Sf], po[:, hs], lhsT=qf[:, hs], rhs=Sf[:, 0, hs], start=False, stop=False)
                O("pe", "matmul", [po], [qb, Sb], po[:, hs], lhsT=qb[:, hs], rhs=Sb[:, 1, hs], start=False, stop=True)
            st = STAT.get()
            O("dve", "memset", [st], [], st[:], 0.0)
            yield
            for h in range(4):
                junk = JUNK.get()
                O("act", "activation", [junk, st], [po, st], out=junk[:, 0:128], in_=po[:, h * 128:(h + 1) * 128], func=AF.Square,
                  accum_out=st[:, h:h + 1])
            yield
            O("act", "activation", [st], [st, epsb], out=st[:, 0:4], in_=st[:, 0:4], func=AF.Sqrt, scale=1.0 / 128, bias=epsb[:, 0:1])
            yield
            O("dve", "reciprocal", [st], [st], out=st[:, 4:8], in_=st[:, 0:4])
            yield
            rt = TB.get()
            for h in range(4):
                hs = slice(h * 128, (h + 1) * 128)
                O("dve", "scalar_tensor_tensor", [rt], [po, st, gs], out=rt[:, hs], in0=po[:, hs], scalar=st[:, 4 + h:5 + h],
                  in1=gs3[:, ci, hs], op0=ALU.mult, op1=ALU.mult)
            if ci == 0:
                dbg("ret_" + tag, rt, rt[:], [128, 512], BF16)
            yield
            transpose_blocks(rt, [rt[:, h * 128:(h + 1) * 128] for h in range(4)], retT, retT3[:, :, sl])
            yield

        s1 = [b3_s1(ci) for ci in range(nch)]
        for c2 in range(0, nch, 2):
            interleave([g_b3_s2(ci, *s1[ci]) for ci in range(c2, min(c2 + 2, nch))])
        if is_ctx:
            ZT_t, ZT3 = ZTC, ZTC[:]
        else:
            ZT_t = ZTL
            ZT3 = ZTL[:]
            load(ZTL, ZT3, zt_d, zt_d.h[:, :, grp["c0"] * 128:grp["c0"] * 128 + T])
        TBW = min(T, 512)
        ntb = T // TBW
        for db in range(8):
            j = db % 2
            if j == 0:
                dbp = db // 2
                wgA = WP.get()
                wgAv = wgA[:].rearrange("p (b kc n) -> p b kc n", b=2, kc=8)
                for br in range(2):
                    c0w = 2816 + br * 1024 + dbp * 256
                    wload(wgA, wgAv[:, br], W["w_in"], W["w_in"].h[l, :, c0w:c0w + 256].rearrange("(kc p) n -> p kc n", p=128))
                wgB = WP.get()
                wg2v = wgB[:, 0:2048].rearrange("p (kc n) -> p kc n", kc=8)
                wa2v = wgB[:, 2048:3072].rearrange("p (kc n) -> p kc n", kc=4)
                wc2v = wgB[:, 3072:3584].rearrange("p (kc n) -> p kc n", kc=2)
                c0w = 2816 + 2 * 1024 + dbp * 256
                wload(wgB, wg2v, W["w_in"], W["w_in"].h[l, :, c0w:c0w + 256].rearrange("(kc p) n -> p kc n", p=128))
                wload(wgB, wa2v, W["w_branch_a"], W["w_branch_a"].h[l, :, dbp * 256:(dbp + 1) * 256].rearrange("(kc p) n -> p kc n", p=128))
                wload(wgB, wc2v, W["w_branch_c"], W["w_branch_c"].h[l, :, dbp * 256:(dbp + 1) * 256].rearrange("(kc p) n -> p kc n", p=128))
                wgC = WP.get()
                wbp2v = wgC[0:64, 0:2048].rearrange("p (kc n) -> p kc n", kc=8)
                load(wgC, wbp2v, wbp_d, wbp_d.h[:, :, dbp * 256:(dbp + 1) * 256], eng="pool", stream="w")
            js = slice(j * 128, (j + 1) * 128)
            gate_t = [wgA, wgA, wgB]
            gate_v = [wgAv[:, 0, :, js], wgAv[:, 1, :, js], wg2v[:, :, js]]
            wa_v, wc_v, wbp_v = wa2v[:, :, js], wc2v[:, :, js], wbp2v[:, :, js]
            for tbi in range(ntb):
                ts = slice(tbi * TBW, (tbi + 1) * TBW)
                acc = TMPH.get()
                for br in range(3):
                    pg = PF.get()
                    for kc in range(8):
                        O("pe", "matmul", [pg], [gate_t[br], hT], pg[:, 0:TBW], lhsT=gate_v[br][:, kc, :], rhs=hT[:, kc, ts], start=(kc == 0), stop=(kc == 7))
                    pb = PF.get()
                    if br == 0:
                        for kc in range(4):
                            O("pe", "matmul", [pb], [wgB, retT], pb[:, 0:TBW], lhsT=wa_v[:, kc, :], rhs=retT3[:, kc, ts], start=(kc == 0), stop=(kc == 3))
                    elif br == 1:
                        for kc in range(8):
                            O("pe", "matmul", [pb], [wgC, ZT_t], pb[:, 0:TBW], lhsT=wbp_v[:, kc, :], rhs=ZT3[:, kc, ts], start=(kc == 0), stop=(kc == 7))
                    else:
                        for kc in range(2):
                            O("pe", "matmul", [pb], [wgC if False else wgB, sguT], pb[:, 0:TBW], lhsT=wc_v[:, kc, :], rhs=sguT3[:, kc, ts], start=(kc == 0), stop=(kc == 1))
                    sg = TMPH.get()
                    O("act", "activation", [sg], [pg], out=sg[:, 0:TBW], in_=pg[:, 0:TBW], func=AF.Sigmoid)
                    if br == 0:
                        O("dve", "tensor_tensor", [acc], [sg, pb], out=acc[:, 0:TBW], in0=sg[:, 0:TBW], in1=pb[:, 0:TBW], op=ALU.mult)
                    else:
                        O("dve", "tensor_tensor", [sg], [sg, pb], out=sg[:, 0:TBW], in0=sg[:, 0:TBW], in1=pb[:, 0:TBW], op=ALU.mult)
                        if br == 1:
                            O("dve", "tensor_tensor", [acc], [acc, sg], out=acc[:, 0:TBW], in0=acc[:, 0:TBW], in1=sg[:, 0:TBW], op=ALU.add)
                        else:
                            O("dve", "tensor_tensor", [mT], [acc, sg], out=mT[:, db, ts], in0=acc[:, 0:TBW], in1=sg[:, 0:TBW], op=ALU.add)
        dbg("mT_" + tag, mT, mT[:, :, 0:128], [128, 8, 128], BF16)
        wo = [WP.get(), WP.get()]
        wov = []
        for half in range(2):
            v = wo[half][:].rearrange("p (kc n) -> p kc n", kc=8)
            wload(wo[half], v, W["w_out"], W["w_out"].h[l, :, half * 512:(half + 1) * 512].rearrange("(kc p) n -> p kc n", p=128))
            wov.append(v)
        def b6_a(ci):
            xt = XIN.get()
            load(xt, xt[:], xsrc, x_ap(ci))
            pss = []
            for half in range(2):
                ps = PF.get()
                for kc in range(8):
                    O("pe", "matmul", [ps], [mT, wo[half]], ps[:], lhsT=mT[:, kc, ci * 128:(ci + 1) * 128], rhs=wov[half][:, kc, :],
                      start=(kc == 0), stop=(kc == 7))
                pss.append(ps)
            return xt, pss

        def g_b6_b(ci, xt, pss):
            st = STAT.get()
            yield from g_post_norm_residual(xt, pss, [pss[0][:], pss[1][:]], st)
            if ci == 0:
                dbg("xmid_" + tag, xt, xt[:], [128, D])
            cx.dma("sp", "st", xmid, xm_ap(ci), xt, xt[:])
            yield
            yield from g_norm_mod_T(xt, hT, ci)

        for c2 in range(0, nch, 2):
            cis = list(range(c2, min(c2 + 2, nch)))
            As = [b6_a(ci) for ci in cis]
            interleave([g_b6_b(ci, *a) for ci, a in zip(cis, As)])
        aT3 = r4[:]
        for fb in range(8):
            wu = WP.get()
            wuv = wu[:].rearrange("p (kc n) -> p kc n", kc=8)
            wload(wu, wuv, W["w_up"], W["w_up"].h[l, :, fb * 512:(fb + 1) * 512].rearrange("(kc p) n -> p kc n", p=128))
            wd = WP.get()
            wdv = wd[:].rearrange("p (f n) -> p f n", f=4)
            wload(wd, wdv, W["w_down"], W["w_down"].h[l, fb * 512:(fb + 1) * 512, :].rearrange("(f p) n -> p f n", p=128))
            for tbi in range(ntb):
                ts = slice(tbi * TBW, (tbi + 1) * TBW)
                for f in range(4):
                    ps = PF.get()
                    for kc in range(8):
                        O("pe", "matmul", [ps], [wu, hT], ps[:, 0:TBW], lhsT=wuv[:, kc, f * 128:(f + 1) * 128], rhs=hT[:, kc, ts],
                          start=(kc == 0), stop=(kc == 7))
                    r = TMPH.get()
                    O("act", "activation", [r], [ps], out=r[:, 0:TBW], in_=ps[:, 0:TBW], func=AF.Relu)
                    O("act", "activation", [aT], [r], out=aT3[:, f, ts], in_=r[:, 0:TBW], func=AF.Square)
            if fb == 7 and grp.get("next") is not None:
                prefetch_inputs(grp["next"])
                grp["next"]["prefetched"] = True
            for ci in range(nch):
                for half in range(2):
                    ps = PF.get()
                    for f in range(4):
                        O("pe", "matmul", [ps], [aT, wd], ps[:], lhsT=aT3[:, f, ci * 128:(ci + 1) * 128], rhs=wdv[:, f, half * 512:(half + 1) * 512],
                          start=(f == 0), stop=(f == 3))
                    ya = y2v[:, ci, half * 512:(half + 1) * 512]
                    if fb == 0:
                        evac(big16, ya, ps, ps[:])
                    else:
                        O("dve", "tensor_tensor", [big16], [big16, ps], out=ya, in0=ya, in1=ps[:], op=ALU.add)
        def g_fin(ci):
            xt = XIN.get()
            load(xt, xt[:], xmid, xm_ap(ci))
            st = STAT.get()
            yield from g_post_norm_residual(xt, big16, [y2v[:, ci, 0:512], y2v[:, ci, 512:1024]], st, gt=MODG2)
            if ci == 0:
                dbg("xout_" + tag, xt, xt[:], [128, D])
            cx.dma("sp", "st", xdst, xd_ap(ci), xt, xt[:])
            yield
        return [lambda c2=c2: interleave([g_fin(ci) for ci in range(c2, min(c2 + 2, nch))]) for c2 in range(0, nch, 2)]

    def row_ap(t, c0):
        return lambda ci: t.h[(c0 + ci) * 128:(c0 + ci + 1) * 128, :]

    for l in range(nlayers):
        last = (l == DEPTH - 1)
        if l == 0:
            setup_mod(0)
        layer_setup(l)
        csrc = ctx_in if l == 0 else cs_d
        cgrp = dict(nch=CTXCH, xsrc=csrc, x_ap=row_ap(csrc, 0), is_ctx=True, c0=0, last=last,
                    xmid=cs_d, xm_ap=row_ap(cs_d, 0), xdst=cs_d, xd_ap=row_ap(cs_d, 0))
        pass_A(l, cgrp)
        recur(Uc_d, Sc_d, CTXCH, None, store=not last)
        for d in range(2):
            O("dve", "tensor_copy", [SCTX[d]], [SCUR[d]], out=SCTX[d][:], in_=SCUR[d][:])
        dbg("sctx_l%d" % l, SCTX[0], SCTX[0][:], [128, 512])
        xsrc = x_in if l == 0 else xs_d
        xdst = out_d if last else xs_d
        groups = []
        for g in range(NCH // G):
            c0 = g * G
            groups.append(dict(nch=G, xsrc=xsrc, x_ap=row_ap(xsrc, c0), is_ctx=False, c0=c0,
                               xmid=xs_d if not last else xs_d, xm_ap=row_ap(xs_d, c0), xdst=xdst, xd_ap=row_ap(xdst, c0)))
        esA = ExitStack()
        bufsA = alloc_passA(esA)
        for gi, grp in enumerate(groups):
            pass_A(l, grp, bufsA)
            if gi == 1 and l + 1 < nlayers:
                setup_mod(l + 1)
        cx.barrier()
        esA.close()
        if stop_after == "A":
            break
        fourier_gather()
        exchange_issue()
        if not last:
            esB = ExitStack()
            for f in pass_B(l, cgrp, last, alloc_passB(esB)):
                f()
            cx.barrier()
            esB.close()
        exchange_combine()
        dbg("sstart_l%d" % l, SSTART[0], SSTART[0][:], [128, 512])
        es_f = fourier_latent()
        recur(U_d, S_d, NCH, SSTART, store=True)
        cx.barrier()
        es_f.close()
        if stop_after == "F":
            break
        esB = ExitStack()
        bufsB = alloc_passB(esB)
        for gi, grp in enumerate(groups):
            grp["next"] = groups[gi + 1] if gi + 1 < len(groups) else None
            grp["load_mod"] = (gi == 0)
            grp["prefetched"] = False
        fin = None
        for grp in groups:
            fin = pass_B(l, grp, last, bufsB, prev_fin=fin)
        for f in fin:
            f()
        cx.barrier()
        esB.close()

    cx.barrier(full=True)
    cx.emit()
    DEBUG['min_free'] = cx.min_free
    DEBUG['ops'] = {k: len(v.ops) for k, v in cx.engs.items()}
    cx.close()
    return nc, dbg_out


_CACHE = {}


def make_in_maps(inputs):
    in_maps = []
    f32 = np.float32
    shared = {}
    for k in WEIGHT_SPECS:
        a = np.ascontiguousarray(np.asarray(inputs[k], dtype=f32))
        shared[k] = a.reshape(WEIGHT_SPECS[k])
    x = np.asarray(inputs["x"], dtype=f32)
    ctx = np.asarray(inputs["ctx"], dtype=f32)
    c = np.asarray(inputs["c"], dtype=f32)
    c_ctx = np.asarray(inputs["c_ctx"], dtype=f32)
    for core in range(8):
        b, j = core // 4, core % 4
        m = dict(shared)
        m["x"] = np.ascontiguousarray(x[b, 2048 * j:2048 * (j + 1), :])
        m["ctx"] = np.ascontiguousarray(ctx[b])
        m["c2"] = np.ascontiguousarray(np.stack([c[b], c_ctx], 0))
        m.update(host_consts(core))
        in_maps.append(m)
    return in_maps


def kernel(**inputs):
    if "nc" not in _CACHE:
        _CACHE["nc"] = build()[0]
    nc = _CACHE["nc"]
    in_maps = make_in_maps(inputs)
    res = run_bass_kernel_spmd(nc, in_maps, core_ids=list(range(8)))
    out = np.zeros((2, 8192, D), np.float32)
    for core in range(8):
        b, j = core // 4, core % 4
        out[b, 2048 * j:2048 * (j + 1), :] = res.results[core]["out"]
    return out
```

```python
from contextlib import ExitStack
import math
import numpy as np
import ml_dtypes
import concourse.bass as bass
import concourse.mybir as mybir
from concourse.bass_utils import run_bass_kernel_spmd

F32 = mybir.dt.float32
BF16 = mybir.dt.bfloat16
AF = mybir.ActivationFunctionType
ALU = mybir.AluOpType

D = 1024
DEPTH = 2
NCH = 16
G = 4
CTXCH = 2
IN_W = 5888
EPS = 1e-6
SK = 128 ** -0.5
DEBUG = {}


class Tile:
    __slots__ = ("name", "h", "w", "r")

    def __init__(self, name, h):
        self.name = name
        self.h = h
        self.w = {}
        self.r = {}

    def __getitem__(self, k):
        return self.h[k]


class Eng:
    def __init__(self, name, sem):
        self.name = name
        self.sem = sem
        self.count = 0
        self.ops = []
        self.waited = {}


class Stream:
    K = 8

    def __init__(self, name, sems):
        self.name = name
        self.sems = sems
        self.n = 0
        self.sem = sems[0]
        self.count = 0


class Ctx:
    ENG_NAMES = ("pe", "act", "dve", "pool", "sp")

    def __init__(self, nc):
        self.nc = nc
        self.stack = ExitStack()
        self.engs = {}
        for n in self.ENG_NAMES:
            sem = self.stack.enter_context(nc.semaphore("sem_" + n))
            self.engs[n] = Eng(n, sem)
        self.streams = {}
        self.ntiles = 0

    def stream(self, name, k=None):
        if name not in self.streams:
            k = k or Stream.K
            sems = [self.stack.enter_context(self.nc.semaphore("dq_%s%d" % (name, i))) for i in range(k)]
            self.streams[name] = Stream(name, sems)
        return self.streams[name]

    def sbuf(self, name, shape, dtype, stack=None):
        self.ntiles += 1
        h = (stack or self.stack).enter_context(self.nc.sbuf_tensor(f"{name}_{self.ntiles}", list(shape), dtype))
        self.min_free = min(getattr(self, "min_free", 1 << 30), self.nc.sbuf_bytes_remaining)
        return Tile(name, h)

    def psum(self, name, shape, dtype=F32):
        self.ntiles += 1
        h = self.stack.enter_context(self.nc.psum_tensor(f"{name}_{self.ntiles}", list(shape), dtype))
        return Tile(name, h)

    def dram(self, name, shape, dtype, kind="Internal"):
        t = self.nc.dram_tensor(name, list(shape), dtype, kind=kind)
        return Tile(name, t.ap())

    def _collect(self, eng, reads, writes):
        need = {}

        def add(d):
            for s, v in d.items():
                if need.get(s, 0) < v:
                    need[s] = v
        for t in reads:
            add(t.w)
        for t in writes:
            add(t.w)
            add(t.r)
        waits = []
        for s, v in need.items():
            if s is eng.sem and eng.name == "pe":
                continue
            if eng.waited.get(s, 0) >= v:
                continue
            eng.waited[s] = v
            waits.append((s, v))
        return waits

    def _mark(self, ev, reads, writes):
        for t in reads:
            if t.r.get(ev[0], 0) < ev[1]:
                t.r[ev[0]] = ev[1]
        for t in writes:
            t.w = {ev[0]: ev[1]}
            t.r = {}

    def op(self, engname, fn, reads=(), writes=()):
        eng = self.engs[engname]
        waits = self._collect(eng, reads, writes)
        eng.count += 1
        ev = (eng.sem, eng.count)
        eng.ops.append((waits, fn, (eng.sem, 1)))
        self._mark(ev, reads, writes)
        return ev

    def dma(self, engname, streamname, out_t, out_ap, in_t, in_ap, **kw):
        eng = self.engs[engname]
        st = self.stream(streamname)
        waits = self._collect(eng, [in_t], [out_t])
        k = len(st.sems)
        idx = st.n
        st.n += 1
        sem = st.sems[idx % k]
        if idx >= k:
            pv = 16 * (idx // k)
            if eng.waited.get(sem, 0) < pv:
                eng.waited[sem] = pv
                waits.append((sem, pv))
        ev = (sem, 16 * (idx // k + 1))

        def fn(e, out_ap=out_ap, in_ap=in_ap, kw=kw):
            return e.dma_start(out=out_ap, in_=in_ap, **kw)
        eng.ops.append((waits, fn, (sem, 16)))
        self._mark(ev, [in_t], [out_t])
        return ev

    def custom(self, engname, fn, reads, writes, st, inc):
        eng = self.engs[engname]
        waits = self._collect(eng, reads, writes)
        if st.count and eng.waited.get(st.sem, 0) < st.count:
            eng.waited[st.sem] = st.count
            waits.append((st.sem, st.count))
        st.count += inc
        ev = (st.sem, st.count)
        eng.ops.append((waits, fn, (st.sem, inc)))
        self._mark(ev, reads, writes)
        return ev

    def barrier(self, full=False, pool=False):
        evs = {}
        for e in self.engs.values():
            if e.count:
                evs[e.sem] = e.count
        for s in self.streams.values():
            if not full and s.name in ("w", "cc"):
                continue
            if s.count:
                evs[s.sem] = s.count
            k = len(s.sems)
            for i in range(min(k, s.n)):
                evs[s.sems[i]] = 16 * ((s.n - 1 - i) // k + 1)
        for e in self.engs.values():
            if e.name == "pool" and not (full or pool):
                continue
            waits = []
            for s, v in evs.items():
                if s is e.sem:
                    continue
                if e.waited.get(s, 0) >= v:
                    continue
                e.waited[s] = v
                waits.append((s, v))
            if waits:
                e.ops.append((waits, None, None))

    def emit(self):
        engs = self.engs

        def replay(handle, ops):
            for waits, fn, inc in ops:
                for s, v in waits:
                    handle.wait_ge(s, v)
                if fn is None:
                    continue
                fn(handle).then_inc(inc[0], inc[1])

        with self.nc.allow_non_contiguous_dma(reason="strided layout DMAs (small)"), self.nc.Block() as block:
            @block.tensor
            def _(e):
                replay(e, engs["pe"].ops)

            @block.scalar
            def _(e):
                replay(e, engs["act"].ops)

            @block.vector
            def _(e):
                replay(e, engs["dve"].ops)

            @block.gpsimd
            def _(e):
                replay(e, engs["pool"].ops)

            @block.sync
            def _(e):
                replay(e, engs["sp"].ops)

    def close(self):
        self.stack.close()


class Pool:
    def __init__(self, tiles):
        self.tiles = tiles
        self.i = 0

    def get(self):
        t = self.tiles[self.i % len(self.tiles)]
        self.i += 1
        return t


def host_consts(core):
    j = core % 4
    bf = ml_dtypes.bfloat16
    c = {}
    t = np.arange(2048, dtype=np.float64) + 2048 * j
    row = np.floor(t / 64.0)
    col = t - 64.0 * row
    freqs = (10000.0 ** (-np.arange(32, dtype=np.float32) / np.float32(32))).astype(np.float64)
    ang = np.concatenate([row[:, None] * freqs, col[:, None] * freqs], -1)
    ang32 = np.concatenate([(row.astype(np.float32)[:, None] * freqs.astype(np.float32)),
                            (col.astype(np.float32)[:, None] * freqs.astype(np.float32))], -1).astype(np.float64)
    c["rope_cos"] = np.cos(ang32).astype(np.float32)
    c["rope_sin"] = np.sin(ang32).astype(np.float32)
    a = np.arange(64)[:, None]
    kl = np.arange(64)[None, :]
    th = 2 * np.pi * a * kl / 64.0
    c["L1"] = np.concatenate([np.cos(th), -np.sin(th)], 1).astype(bf)
    s = 1.0 / math.sqrt(8192 * 64)
    b = np.arange(128)[:, None, None]
    klo = np.arange(64)[None, :, None]
    kh = (32 * j + np.arange(32))[None, None, :]
    k = 64 * kh + klo
    th3 = 2 * np.pi * ((b * k) % 8192) / 8192.0
    Ere = s * np.cos(th3)
    Eim = -s * np.sin(th3)
    c["E3"] = np.concatenate([-Eim, Ere, Eim], 2).astype(bf)
    s2 = 1.0 / math.sqrt(256 * 64)
    n = np.arange(256)[:, None]
    kk = np.arange(256)[None, :]
    th2 = 2 * np.pi * ((n * kk) % 256) / 256.0
    d256 = np.concatenate([s2 * np.cos(th2), -s2 * np.sin(th2)], 1)
    c["D256"] = d256.reshape(2, 128, 512).transpose(1, 0, 2).astype(bf).copy()
    m = np.arange(64)[:, None]
    jj = np.arange(64)[None, :]
    thc = 2 * np.pi * ((m * jj) % 64) / 64.0
    c["CS64"] = np.concatenate([np.cos(thc), np.sin(thc)], 1).astype(bf)
    sidx = np.arange(128)[:, None].astype(np.float32)
    cidx = np.arange(128)[None, :].astype(np.float32)
    rc = np.zeros((128, 6, 128), np.float32)
    rc[:, 0, :] = np.maximum(cidx - sidx, 0)
    rc[:, 1, :] = np.maximum(sidx - cidx, 0)
    rc[:, 2, :] = (cidx >= sidx) * SK
    rc[:, 3, :] = (sidx > cidx) * SK
    rc[:, 4, :] = cidx + 1.0
    rc[:, 5, :] = 128.0 - cidx
    c["RC"] = rc
    wc = np.zeros((128, 2), np.float32)
    wc[:, 0] = 127.0 - np.arange(128)
    wc[:, 1] = np.arange(128)
    c["WC"] = wc
    mexp = np.zeros((2, 5), np.float32)
    mask = np.zeros((2, 5), np.float32)
    for jp in range(4):
        if jp < j:
            mexp[0, jp] = j - 1 - jp
            mask[0, jp] = 1
        if jp > j:
            mexp[1, jp] = jp - j - 1
            mask[1, jp] = 1
    mexp[0, 4] = j
    mask[0, 4] = 1
    mexp[1, 4] = 3 - j
    mask[1, 4] = 1
    cm = np.zeros((128, 2, 2, 5), np.float32)
    cm[:, 0] = mexp[None] * 2048.0
    cm[:, 1] = mask[None]
    c["CMX"] = cm.reshape(128, 20)
    c["IDENT"] = np.eye(128, dtype=np.float32).astype(bf)
    return c


CONST_SPECS = {
    "rope_cos": ([2048, 64], F32), "rope_sin": ([2048, 64], F32),
    "L1": ([64, 128], BF16), "E3": ([128, 64, 96], BF16), "D256": ([128, 2, 512], BF16),
    "CS64": ([64, 128], BF16), "RC": ([128, 6, 128], F32), "WC": ([128, 2], F32),
    "CMX": ([128, 20], F32), "IDENT": ([128, 128], BF16),
}

WEIGHT_SPECS = {
    "w_mod": [DEPTH, D, 6 * D], "b_mod": [DEPTH, 6 * D], "g_pre_mix": [DEPTH, D], "g_post_mix": [DEPTH, D],
    "g_pre_mlp": [DEPTH, D], "g_post_mlp": [DEPTH, D], "w_in": [DEPTH, D, IN_W],
    "ret_decay_logit": [DEPTH, 8], "sgu_w_s": [DEPTH, 4, 128, 128], "sgu_b_s": [DEPTH, 4, 128],
    "sgu_norm": [DEPTH, 256], "w_branch_a": [DEPTH, 512, D], "w_branch_b": [DEPTH, 256, D],
    "w_branch_c": [DEPTH, 256, D], "w_out": [DEPTH, D, D], "w_up": [DEPTH, D, 4 * D], "w_down": [DEPTH, 4 * D, D],
}


def build(debug=(), nlayers=DEPTH, stop_after=None):
    nc = bass.Bass("TRN2", target_bir_lowering=False)
    cx = Ctx(nc)
    dbg_out = {}

    x_in = cx.dram("x", [2048, D], F32, kind="ExternalInput")
    ctx_in = cx.dram("ctx", [256, D], F32, kind="ExternalInput")
    c2_in = cx.dram("c2", [2, D], F32, kind="ExternalInput")
    W = {k: cx.dram(k, shp, F32, kind="ExternalInput") for k, shp in WEIGHT_SPECS.items()}
    K = {k: cx.dram(k, shp, dt, kind="ExternalInput") for k, (shp, dt) in CONST_SPECS.items()}
    out_d = cx.dram("out", [2048, D], F32, kind="ExternalOutput")
    xs_d = cx.dram("xs_d", [2048, D], F32)
    cs_d = cx.dram("cs_d", [256, D], F32)
    modraw_d = cx.dram("modraw_d", [DEPTH, 2, 6 * D], F32)
    modv_d = cx.dram("modv_d", [DEPTH, 2, 2, 2, D], F32)
    hT_d = cx.dram("hT_d", [NCH // G, 128, 8, G * 128], BF16)
    hTc_d = cx.dram("hTc_d", [128, 8, G * 128], BF16)
    kv_d = cx.dram("kv_d", [NCH // G, 2, 128, G, 512], BF16)
    kvc_d = cx.dram("kvc_d", [2, 128, G, 512], BF16)
    U_d = cx.dram("U_d", [2, NCH, 128, 512], F32)
    Uc_d = cx.dram("Uc_d", [2, CTXCH, 128, 512], F32)
    S_d = cx.dram("S_d", [2, NCH, 128, 512], BF16)
    Sc_d = cx.dram("Sc_d", [2, CTXCH, 128, 512], BF16)
    st_loc = cx.dram("st_loc", [2 * 128, 512], F32)
    st_all = cx.dram("st_all", [4 * 2 * 128, 512], F32)
    f_loc = cx.dram("f_loc", [4 * 2048, 64], BF16)
    f_all = cx.dram("f_all", [4 * 4 * 2048, 64], BF16)
    zt_d = cx.dram("zt_d", [64, 8, 2048], BF16)
    wbp_d = cx.dram("wbp_d", [64, 8, D], BF16)
    cc = cx.stream("cc", k=1)

    def dbg(name, t, ap, shape, dtype=F32):
        if name not in debug:
            return
        o = cx.dram("dbg_" + name, list(shape), dtype, kind="ExternalOutput")
        cx.dma("sp", "dbg", o, o.h, t, ap)
        dbg_out[name] = (shape, dtype)

    WP = Pool([cx.sbuf(f"wp{i}", [128, 4096], BF16) for i in range(6)])
    XIN = Pool([cx.sbuf(f"xin{i}", [128, D], F32) for i in range(3)])
    JUNK = Pool([cx.sbuf(f"junk{i}", [128, D], BF16) for i in range(1)])
    TMPF = Pool([cx.sbuf(f"tmpf{i}", [128, D], F32) for i in range(2)])
    TMPH = Pool([cx.sbuf(f"tmph{i}", [128, 512], F32) for i in range(6)])
    HB = Pool([cx.sbuf(f"hb{i}", [128, D], BF16) for i in range(2)])
    TB = Pool([cx.sbuf(f"tb{i}", [128, 512], BF16) for i in range(16)])
    SFB = Pool([cx.sbuf(f"sfb{i}", [128, 2, 512], BF16) for i in range(4)])
    STAT = Pool([cx.sbuf(f"stat{i}", [128, 8], F32) for i in range(6)])
    MODA = cx.sbuf("modA", [128, D], F32)
    MODB = cx.sbuf("modB", [128, D], F32)
    MODG = cx.sbuf("modG", [128, D], F32)
    MODG2 = cx.sbuf("modG2", [128, D], F32)
    ROPE = cx.sbuf("rope", [128, 2, G, 64], F32)
    ROPE1 = cx.sbuf("rope1", [128, 2, CTXCH, 64], F32)
    ident = cx.sbuf("ident", [128, 128], BF16)
    RC = cx.sbuf("RC", [128, 6, 128], F32)
    WC = cx.sbuf("WC", [128, 2], F32)
    CMX = cx.sbuf("CMX", [128, 20], F32)
    LG = cx.sbuf("LG", [128, 8], F32)
    DT = cx.sbuf("DT", [128, 4, 128], F32)
    QF = cx.sbuf("QF", [128, 4, 128], F32)
    QB = cx.sbuf("QB", [128, 4, 128], F32)
    WFB = cx.sbuf("WFB", [128, 2, 4], F32)
    G128 = cx.sbuf("G128", [128, 8], F32)
    COEF = cx.sbuf("COEF", [128, 2, 5, 4], F32)
    epsb = cx.sbuf("epsb", [128, 2], F32)
    SCUR = [cx.sbuf(f"scur{i}", [128, 512], F32) for i in range(2)]
    SCTX = [cx.sbuf(f"sctx{i}", [128, 512], F32) for i in range(2)]
    SSTART = [cx.sbuf(f"sstart{i}", [128, 512], F32) for i in range(2)]
    c2T = cx.sbuf("c2T", [128, 8, 2], F32)
    c2Tb = cx.sbuf("c2Tb", [128, 8, 2], BF16)
    SGW = cx.sbuf("SGW", [128, 4, 128], BF16)
    SGWt = cx.sbuf("SGWt", [128, 4, 128], BF16)
    SGB = cx.sbuf("SGB", [128, 4], F32)
    SGN = cx.sbuf("SGN", [128, 256], F32)
    L1 = cx.sbuf("L1", [64, 128], BF16)
    CS64 = cx.sbuf("CS64", [64, 128], BF16)
    ZTC = cx.sbuf("ZTC", [64, 8, 256], BF16)

    PF = Pool([cx.psum(f"pf{i}", [128, 512], F32) for i in range(6)])
    PBT = Pool([cx.psum(f"pb{i}", [128, 1024], BF16) for i in range(2)])

    def O(eng, meth, writes, reads, *a, **k):
        return cx.op(eng, lambda e: getattr(e, meth)(*a, **k), reads=reads, writes=writes)

    def load(t, ap, src, sap, eng="sp", stream="ld"):
        cx.dma(eng, stream, t, ap, src, sap)

    def wload(t, ap, src, sap):
        cx.dma("pool", "w", t, ap, src, sap)

    cpy_i = [0]

    def evac(out_t, out_ap, in_t, in_ap, eng=None):
        cpy_i[0] += 1
        use_act = (cpy_i[0] % 3 != 0) if eng is None else (eng == "act")
        if use_act:
            O("act", "activation", [out_t], [in_t], out=out_ap, in_=in_ap, func=AF.Copy)
        else:
            O("dve", "tensor_copy", [out_t], [in_t], out=out_ap, in_=in_ap)

    def transpose_blocks(src_t, src_aps, dst_t, dst_ap, eng=None):
        n = len(src_aps)
        pb = PBT.get()
        for i, sap in enumerate(src_aps):
            O("pe", "transpose", [pb], [src_t, ident], out=pb[:, i * 128:(i + 1) * 128], in_=sap, identity=ident[:])
        evac(dst_t, dst_ap, pb, pb[:, 0:n * 128].rearrange("p (n c) -> p n c", n=n), eng=eng)

    for name, t in (("IDENT", ident), ("RC", RC), ("WC", WC), ("CMX", CMX), ("L1", L1), ("CS64", CS64)):
        load(t, t[:], K[name], K[name].h)
    O("dve", "memset", [epsb], [], epsb[:, 0:1], EPS)
    O("dve", "memset", [epsb], [], epsb[:, 1:2], 1.0)
    O("dve", "memset", [ROPE1], [], ROPE1[:, 0], 1.0)
    O("dve", "memset", [ROPE1], [], ROPE1[:, 1], 0.0)
    for r in range(2):
        load(c2T, c2T[:, :, r], c2_in, c2_in.h[r].rearrange("(kc p) -> p kc", p=128))
    O("act", "activation", [c2Tb], [c2T], out=c2Tb[:], in_=c2T[:], func=AF.Silu)

    def layer_setup(l):
        load(LG, LG[:], W["ret_decay_logit"], W["ret_decay_logit"].h[l:l + 1, :].broadcast_to([128, 8]))
        O("act", "activation", [LG], [LG], out=LG[:], in_=LG[:], func=AF.Exp, scale=-1.0)
        O("act", "activation", [LG], [LG, epsb], out=LG[:], in_=LG[:], func=AF.Ln, bias=epsb[:, 1:2])
        O("dve", "tensor_scalar", [LG], [LG], out=LG[:], in0=LG[:], scalar1=-1.0, scalar2=None, op0=ALU.mult)
        tA = TMPH.get()
        tB = TMPH.get()
        for h in range(4):
            O("act", "activation", [tA], [RC, LG], out=tA[:, 0:128], in_=RC[:, 0, :], func=AF.Exp, scale=LG[:, h:h + 1])
            O("act", "activation", [tB], [RC, LG], out=tB[:, 0:128], in_=RC[:, 1, :], func=AF.Exp, scale=LG[:, 4 + h:5 + h])
            O("dve", "tensor_tensor", [tA], [tA, RC], out=tA[:, 0:128], in0=tA[:, 0:128], in1=RC[:, 2, :], op=ALU.mult)
            O("dve", "tensor_tensor", [tB], [tB, RC], out=tB[:, 0:128], in0=tB[:, 0:128], in1=RC[:, 3, :], op=ALU.mult)
            O("dve", "tensor_tensor", [DT], [tA, tB], out=DT[:, h, :], in0=tA[:, 0:128], in1=tB[:, 0:128], op=ALU.add)
            O("act", "activation", [QF], [RC, LG], out=QF[:, h, :], in_=RC[:, 4, :], func=AF.Exp, scale=LG[:, h:h + 1])
            O("act", "activation", [QB], [RC, LG], out=QB[:, h, :], in_=RC[:, 5, :], func=AF.Exp, scale=LG[:, 4 + h:5 + h])
        for d in range(2):
            O("act", "activation", [WFB], [LG, WC], out=WFB[:, d, :], in_=LG[:, 4 * d:4 * d + 4], func=AF.Exp,
              scale=WC[:, d:d + 1])
        O("dve", "tensor_scalar", [WFB], [WFB], out=WFB[:], in0=WFB[:], scalar1=SK, scalar2=None, op0=ALU.mult)
        O("act", "activation", [G128], [LG], out=G128[:], in_=LG[:], func=AF.Exp, scale=128.0)
        cmv = CMX[:].rearrange("p (a d s) -> p a d s", a=2, d=2)
        for d in range(2):
            for s in range(5):
                O("act", "activation", [COEF], [LG, CMX], out=COEF[:, d, s, :], in_=LG[:, 4 * d:4 * d + 4], func=AF.Exp,
                  scale=cmv[:, 0, d, s:s + 1])
                O("dve", "tensor_scalar", [COEF], [COEF, CMX], out=COEF[:, d, s, :], in0=COEF[:, d, s, :],
                  scalar1=cmv[:, 1, d, s:s + 1], scalar2=None, op0=ALU.mult)

        wt = WP.get()
        wf = wt[:, 0:512].rearrange("p (g s) -> p g s", g=4)
        wload(wt, wf, W["sgu_w_s"], W["sgu_w_s"].h[l].rearrange("g t s -> t g s"))
        O("dve", "tensor_copy", [SGWt], [wt], out=SGWt[:], in_=wf)
        transpose_blocks(SGWt, [SGWt[:, g, :] for g in range(4)], SGW, SGW[:])
        with nc.allow_non_contiguous_dma(reason="tiny transposed bias load"):
            load(SGB, SGB[:], W["sgu_b_s"], W["sgu_b_s"].h[l].rearrange("g t -> t g"))
        load(SGN, SGN[:], W["sgu_norm"], W["sgu_norm"].h[l:l + 1, :].broadcast_to([128, 256]))

        WBR = WP.get()
        WBRv = WBR[0:64, :].rearrange("p (g d) -> p g d", g=4)
        wload(WBR, WBRv, W["w_branch_b"], W["w_branch_b"].h[l].rearrange("(g j) d -> j g d", j=64))
        for g in range(4):
            for c in range(2):
                tb = TB.get()
                tb2 = TB.get()
                for half, tt in enumerate((tb, tb2)):
                    ps = PF.get()
                    O("pe", "matmul", [ps], [CS64, WBR], ps[0:64, :], lhsT=CS64[:, c * 64:(c + 1) * 64],
                      rhs=WBRv[:, g, half * 512:(half + 1) * 512], start=True, stop=True)
                    evac(tt, tt[0:64, :], ps, ps[0:64, :])
                    cx.dma("sp", "st", wbp_d, wbp_d.h[:, g * 2 + c, half * 512:(half + 1) * 512], tt, tt[0:64, :])


    def setup_mod(l, cbs=range(12), sel=None):
        for cb in cbs:
            wt = WP.get()
            wv = wt[:].rearrange("p (kc n) -> p kc n", kc=8)
            wload(wt, wv, W["w_mod"], W["w_mod"].h[l, :, cb * 512:(cb + 1) * 512].rearrange("(kc p) n -> p kc n", p=128))
            bch = TMPH.get()
            load(bch, bch[0:2, :], W["b_mod"], W["b_mod"].h[l:l + 1, cb * 512:(cb + 1) * 512].broadcast_to([2, 512]))
            ps = PF.get()
            for kc in range(8):
                O("pe", "matmul", [ps], [c2Tb, wt], ps[0:2, :], lhsT=c2Tb[:, kc, :], rhs=wv[:, kc, :],
                  start=(kc == 0), stop=(kc == 7))
            O("dve", "tensor_tensor", [bch], [ps, bch], out=bch[0:2, :], in0=ps[0:2, :], in1=bch[0:2, :], op=ALU.add)
            cx.dma("sp", "st", modraw_d, modraw_d.h[l][:, cb * 512:(cb + 1) * 512], bch, bch[0:2, :])

        setup_modv(l, sel)

    def setup_modv(l, sel=None):
        def bc(t, ap, src, sap):
            load(t, ap, src, sap.broadcast_to([128, D]))
        for row in range(2):
            for which in range(2):
                o = 3 * which
                gpre = W["g_pre_mix"] if which == 0 else W["g_pre_mlp"]
                gpost = W["g_post_mix"] if which == 0 else W["g_post_mlp"]
                if sel is None or (which, "A") in sel:
                    ta = TMPF.get()
                    t1 = TMPF.get()
                    bc(ta, ta[:], modraw_d, modraw_d.h[l][row:row + 1, (o + 1) * D:(o + 2) * D])
                    bc(t1, t1[:], gpre, gpre.h[l:l + 1, :])
                    O("dve", "scalar_tensor_tensor", [ta], [ta, t1], out=ta[:], in0=ta[:], scalar=1.0, in1=t1[:], op0=ALU.add, op1=ALU.mult)
                    cx.dma("sp", "st", modv_d, modv_d.h[l][row, which, 0:1, :], ta, ta[0:1, :])
                if sel is None or (which, "G") in sel:
                    tg = TMPF.get()
                    t2 = TMPF.get()
                    bc(tg, tg[:], modraw_d, modraw_d.h[l][row:row + 1, (o + 2) * D:(o + 3) * D])
                    bc(t2, t2[:], gpost, gpost.h[l:l + 1, :])
                    O("dve", "tensor_tensor", [tg], [tg, t2], out=tg[:], in0=tg[:], in1=t2[:], op=ALU.mult)
                    cx.dma("sp", "st", modv_d, modv_d.h[l][row, which, 1:2, :], tg, tg[0:1, :])

    def load_vec(t, row, which, kind, l):
        if kind == "B":
            src, sap = modraw_d, modraw_d.h[l][row:row + 1, (3 * which) * D:(3 * which + 1) * D]
        else:
            i = 0 if kind == "A" else 1
            src, sap = modv_d, modv_d.h[l][row, which, i:i + 1, :]
        load(t, t[:], src, sap.broadcast_to([128, D]))

    def g_rms_rstd(src_t, src_aps, n, st):
        srcs = src_t if isinstance(src_t, (list, tuple)) else [src_t] * len(src_aps)
        O("dve", "memset", [st], [], st[:], 0.0)
        yield
        single = len(src_aps) == 1
        for i, sap in enumerate(src_aps):
            junk = JUNK.get()
            jv = junk[:, 0:sap.shape[-1]]
            col = 0 if single else 4 + i
            O("act", "activation", [junk, st], [srcs[i], st], out=jv, in_=sap, func=AF.Square, accum_out=st[:, col:col + 1])
            yield
        if not single:
            O("dve", "tensor_tensor", [st], [st], out=st[:, 0:1], in0=st[:, 4:5], in1=st[:, 5:6], op=ALU.add)
            yield
        O("act", "activation", [st], [st, epsb], out=st[:, 1:2], in_=st[:, 0:1], func=AF.Sqrt, scale=1.0 / n, bias=epsb[:, 0:1])
        yield
        O("dve", "reciprocal", [st], [st], out=st[:, 2:3], in_=st[:, 1:2])
        yield

    def run(gen):
        for _ in gen:
            pass

    def interleave(gens):
        gens = list(gens)
        while gens:
            for g in list(gens):
                try:
                    next(g)
                except StopIteration:
                    gens.remove(g)

    def rms_rstd(src_t, src_aps, n, st):
        run(g_rms_rstd(src_t, src_aps, n, st))

    def g_norm_mod_T(xt, dstT, ci, out=None):
        st = STAT.get()
        yield from g_rms_rstd(xt, [xt[:]], D, st)
        tmp = TMPF.get()
        O("dve", "scalar_tensor_tensor", [tmp], [xt, st, MODA], out=tmp[:], in0=xt[:], scalar=st[:, 2:3], in1=MODA[:],
          op0=ALU.mult, op1=ALU.mult)
        yield
        hb = HB.get()
        O("dve", "tensor_tensor", [hb], [tmp, MODB], out=hb[:], in0=tmp[:], in1=MODB[:], op=ALU.add)
        if out is not None:
            out.append(hb)
        yield
        transpose_blocks(hb, [hb[:, kc * 128:(kc + 1) * 128] for kc in range(8)], dstT, dstT[:, :, ci * 128:(ci + 1) * 128])
        yield

    def norm_mod_T(xt, dstT, ci, add_eng="dve"):
        out = []
        run(g_norm_mod_T(xt, dstT, ci, out))
        return out[0]

    def zblock(hT, ci, wv, ncols, col0=0):
        ps = PF.get()
        for kc in range(8):
            O("pe", "matmul", [ps], [hT, wv.tile], ps[:, 0:ncols], lhsT=hT[:, kc, ci * 128:(ci + 1) * 128],
              rhs=wv.ap[:, kc, col0:col0 + ncols], start=(kc == 0), stop=(kc == 7))
        return ps

    class WV:
        def __init__(self, tile, ap):
            self.tile = tile
            self.ap = ap

    def load_win(l, c0, ncols):
        wt = WP.get()
        wv = wt[:, 0:8 * ncols].rearrange("p (kc n) -> p kc n", kc=8)
        wload(wt, wv, W["w_in"], W["w_in"].h[l, :, c0:c0 + ncols].rearrange("(kc p) n -> p kc n", p=128))
        return WV(wt, wv)

    def rope(ps, ropet, ci, dst_t, dst_ap, comb_eng="dve"):
        pv = ps[:].rearrange("p (h t i) -> p h t i", h=4, t=2)
        cosb = ropet[:, 0, ci, :].unsqueeze(1).broadcast_to([128, 4, 64])
        sinb = ropet[:, 1, ci, :].unsqueeze(1).broadcast_to([128, 4, 64])
        t1 = TMPH.get()
        t2 = TMPH.get()
        t1v = t1[:].rearrange("p (h t i) -> p h t i", h=4, t=2)
        t2v = t2[:].rearrange("p (h t i) -> p h t i", h=4, t=2)
        dv = dst_ap.rearrange("p (h t i) -> p h t i", h=4, t=2)
        O("dve", "tensor_tensor", [t1], [ps, ropet], out=t1v[:, :, 0, :], in0=pv[:, :, 0, :], in1=cosb, op=ALU.mult)
        O("dve", "tensor_tensor", [t1], [ps, ropet], out=t1v[:, :, 1, :], in0=pv[:, :, 1, :], in1=cosb, op=ALU.mult)
        O("dve", "tensor_tensor", [t2], [ps, ropet], out=t2v[:, :, 0, :], in0=pv[:, :, 1, :], in1=sinb, op=ALU.mult)
        O("dve", "tensor_tensor", [t2], [ps, ropet], out=t2v[:, :, 1, :], in0=pv[:, :, 0, :], in1=sinb, op=ALU.mult)
        O(comb_eng, "tensor_tensor", [dst_t], [t1, t2], out=dv[:, :, 0, :], in0=t1v[:, :, 0, :], in1=t2v[:, :, 0, :], op=ALU.subtract)
        O(comb_eng, "tensor_tensor", [dst_t], [t1, t2], out=dv[:, :, 1, :], in0=t1v[:, :, 1, :], in1=t2v[:, :, 1, :], op=ALU.add)

    def alloc_passA(es):
        return dict(hT=cx.sbuf("hT", [128, 8, G * 128], BF16, stack=es), k_tok=cx.sbuf("k_tok", [128, G, 512], BF16, stack=es),
                    vf=cx.sbuf("vf", [128, G, 512], BF16, stack=es), vb=cx.sbuf("vb", [128, G, 512], BF16, stack=es),
                    f_tok=cx.sbuf("f_tok", [128, G, 256], BF16, stack=es), v_pl=cx.sbuf("v_pl", [128, G, 512], BF16, stack=es))

    def pass_A(l, grp, bufs=None):
        nch, xsrc, x_ap, is_ctx = grp["nch"], grp["xsrc"], grp["x_ap"], grp["is_ctx"]
        ropet = ROPE1 if is_ctx else ROPE
        es = None
        if bufs is None:
            es = ExitStack()
            bufs = alloc_passA(es)
        hT, k_tok, vf, vb, f_tok, v_pl = bufs["hT"], bufs["k_tok"], bufs["vf"], bufs["vb"], bufs["f_tok"], bufs["v_pl"]
        kvd_ap = kvc_d.h if is_ctx else kv_d.h[grp["c0"] // G]
        kvd = kvc_d if is_ctx else kv_d
        k3, vf3, vb3, f3 = k_tok[:], vf[:], vb[:], f_tok[:]
        row = 1 if is_ctx else 0
        load_vec(MODA, row, 0, "A", l)
        load_vec(MODB, row, 0, "B", l)
        if not is_ctx:
            c0 = grp["c0"]
            for i, nm in enumerate(("rope_cos", "rope_sin")):
                load(ROPE, ROPE[:, i], K[nm], K[nm].h[c0 * 128:(c0 + nch) * 128, :].rearrange("(c p) i -> p c i", p=128))
        for c2 in range(0, nch, 2):
            gens = []
            for ci in range(c2, min(c2 + 2, nch)):
                xt = XIN.get()
                load(xt, xt[:], xsrc, x_ap(ci))
                gens.append(g_norm_mod_T(xt, hT, ci))
            interleave(gens)
        hTd = hTc_d if is_ctx else hT_d
        hTd_ap = hTc_d.h if is_ctx else hT_d.h[grp["c0"] // G]
        cx.dma("sp", "st", hTd, hTd_ap[:, :, 0:nch * 128], hT, hT[:, :, 0:nch * 128])
        wv = load_win(l, 512, 512)
        for ci in range(nch):
            ps = zblock(hT, ci, wv, 512)
            rope(ps, ropet, ci, k_tok, k3[:, ci, :])
        cx.dma("sp", "st", kvd, kvd_ap[0, :, 0:nch, :], k_tok, k3[:, 0:nch, :])
        wv = load_win(l, 1024, 512)
        for ci in range(nch):
            ps = zblock(hT, ci, wv, 512)
            pv = ps[:].rearrange("p (h e) -> p h e", h=4)
            O("dve", "tensor_tensor", [vf], [ps, WFB], out=vf3[:, ci, :].rearrange("p (h e) -> p h e", h=4), in0=pv,
              in1=WFB[:, 0, :].unsqueeze(2).broadcast_to([128, 4, 128]), op=ALU.mult)
            O("dve", "tensor_tensor", [vb], [ps, WFB], out=vb3[:, ci, :].rearrange("p (h e) -> p h e", h=4), in0=pv,
              in1=WFB[:, 1, :].unsqueeze(2).broadcast_to([128, 4, 128]), op=ALU.mult)
            O("dve", "tensor_copy", [v_pl], [ps], out=v_pl[:, ci, :], in_=ps[:])
        cx.dma("sp", "st", kvd, kvd_ap[1, :, 0:nch, :], v_pl, v_pl[:, 0:nch, :])
        wv = load_win(l, 2048, 256)
        for ci in range(nch):
            ps = zblock(hT, ci, wv, 256)
            evac(f_tok, f3[:, ci, :], ps, ps[:, 0:256])
            if not is_ctx:
                tok0 = (grp["c0"] + ci) * 128
                with nc.allow_non_contiguous_dma(reason="quarter-major f layout for the Fourier exchange"):
                    cx.dma("sp", "st", f_loc,
                           f_loc.h.rearrange("(q t) c -> t q c", q=4)[tok0:tok0 + 128, :, :], f_tok,
                           f3[:, ci, :].rearrange("p (q c) -> p q c", q=4))
        Ud = Uc_d if is_ctx else U_d
        for ci in range(nch):
            gci = ci if is_ctx else grp["c0"] + ci
            for d, vw in enumerate((vf3, vb3)):
                ps = PF.get()
                for h in range(4):
                    O("pe", "matmul", [ps], [k_tok, vf if d == 0 else vb], ps[:, h * 128:(h + 1) * 128],
                      lhsT=k3[:, ci, h * 128:(h + 1) * 128], rhs=vw[:, ci, h * 128:(h + 1) * 128], start=True, stop=True)
                tmp = TMPH.get()
                evac(tmp, tmp[:], ps, ps[:])
                cx.dma("sp", "st", Ud, Ud.h[d, gci], tmp, tmp[:])
                if not is_ctx:
                    pw = STAT.get()
                    O("act", "activation", [pw], [LG], out=pw[:, 0:4], in_=LG[:, 4 * d:4 * d + 4], func=AF.Exp,
                      scale=float(128 * ((NCH - 1 - gci) if d == 0 else gci)))
                    t2 = TMPH.get()
                    O("dve", "tensor_tensor", [t2], [tmp, pw], out=t2[:].rearrange("p (h e) -> p h e", h=4),
                      in0=tmp[:].rearrange("p (h e) -> p h e", h=4), in1=pw[:, 0:4].unsqueeze(2).broadcast_to([128, 4, 128]), op=ALU.mult)
                    if gci == 0:
                        O("dve", "tensor_copy", [SCUR[d]], [t2], out=SCUR[d][:], in_=t2[:])
                    else:
                        O("dve", "tensor_tensor", [SCUR[d]], [SCUR[d], t2], out=SCUR[d][:], in0=SCUR[d][:], in1=t2[:], op=ALU.add)
        if is_ctx:
            dbg("kctx_l%d" % l, k_tok, k3[:, 0, :], [128, 512], BF16)
            if not grp["last"]:
                fourier_ctx(f_tok)
        else:
            if grp["c0"] == 0:
                dbg("k0_l%d" % l, k_tok, k3[:, 0, :], [128, 512], BF16)
        if es is not None:
            cx.barrier()
            es.close()

    def decay_mul(S, d):
        sv = S[:].rearrange("p (h e) -> p h e", h=4)
        O("dve", "tensor_tensor", [S], [S, G128], out=sv, in0=sv,
          in1=G128[:, 4 * d:4 * d + 4].unsqueeze(2).broadcast_to([128, 4, 128]), op=ALU.mult)

    def recur_steps(Ud, Sd, nch, start, store):
        steps = []

        def init(d):
            S = SCUR[d]
            if start is None:
                O("dve", "memset", [S], [], S[:], 0.0)
            else:
                O("dve", "tensor_copy", [S], [start[d]], out=S[:], in_=start[d][:])

        def step(d, ci):
            S = SCUR[d]
            if store:
                sb = TB.get()
                O("dve", "tensor_copy", [sb], [S], out=sb[:], in_=S[:])
                cx.dma("sp", "st", Sd, Sd.h[d, ci], sb, sb[:])
            u = TMPH.get()
            load(u, u[:], Ud, Ud.h[d, ci])
            decay_mul(S, d)
            O("dve", "tensor_tensor", [S], [S, u], out=S[:], in0=S[:], in1=u[:], op=ALU.add)

        steps.append(lambda: (init(0), init(1)))
        for k in range(nch):
            steps.append(lambda k=k: step(0, k))
            steps.append(lambda k=k: step(1, nch - 1 - k))
        return steps

    def recur(Ud, Sd, nch, start, store):
        for f in recur_steps(Ud, Sd, nch, start, store):
            f()

    def exchange_states():
        for d in range(2):
            cx.dma("sp", "st", st_loc, st_loc.h[d * 128:(d + 1) * 128, :], SCUR[d], SCUR[d][:])
        cx.custom("pool", lambda e: e.collective_compute("AllGather", ALU.bypass, replica_groups=[[0, 1, 2, 3], [4, 5, 6, 7]],
                                                         ins=[st_loc.h.opt()], outs=[st_all.h.opt()]),
                  reads=[st_loc], writes=[st_all], st=cc, inc=1)
        for d in range(2):
            S = SSTART[d]
            sv = S[:].rearrange("p (h e) -> p h e", h=4)
            O("dve", "tensor_tensor", [S], [SCTX[d], COEF], out=sv, in0=SCTX[d][:].rearrange("p (h e) -> p h e", h=4),
              in1=COEF[:, d, 4, :].unsqueeze(2).broadcast_to([128, 4, 128]), op=ALU.mult)
            for r in range(4):
                u = TMPH.get()
                load(u, u[:], st_all, st_all.h[(r * 2 + d) * 128:(r * 2 + d + 1) * 128, :])
                O("dve", "tensor_tensor", [u], [u, COEF], out=u[:].rearrange("p (h e) -> p h e", h=4),
                  in0=u[:].rearrange("p (h e) -> p h e", h=4),
                  in1=COEF[:, d, r, :].unsqueeze(2).broadcast_to([128, 4, 128]), op=ALU.mult)
                O("dve", "tensor_tensor", [S], [S, u], out=S[:], in0=S[:], in1=u[:], op=ALU.add)

    def fourier_ctx(f_tok):
        f3 = f_tok[:]
        dt_ = WP.get()
        D256 = dt_[:, 0:1024].rearrange("p (n k) -> p n k", n=2)
        load(dt_, D256, K["D256"], K["D256"].h)
        for q in range(4):
            ps = PF.get()
            for nchk in range(2):
                O("pe", "matmul", [ps], [f_tok, dt_], ps[0:64, :], lhsT=f3[:, nchk, q * 64:(q + 1) * 64],
                  rhs=D256[:, nchk, :], start=(nchk == 0), stop=(nchk == 1))
            evac(ZTC, ZTC[:, 2 * q:2 * q + 2, :], ps, ps[0:64, :].rearrange("p (c k) -> p c k", c=2))

    def fourier_gather():
        cx.custom("pool", lambda e: e.collective_compute("AllGather", ALU.bypass, replica_groups=[[0, 1, 2, 3], [4, 5, 6, 7]],
                                                         ins=[f_loc.h.opt()], outs=[f_all.h.opt()]),
                  reads=[f_loc], writes=[f_all], st=cc, inc=1)

    def fourier_latent(side_steps=None):
        fav = f_all.h.rearrange("(r q a b) c -> r q a (b c)", r=4, q=4, a=16)
        cx.barrier(pool=True)
        es = ExitStack()
        X_t = cx.sbuf("fX", [64, 8192], BF16, stack=es)
        TT_t = cx.sbuf("fTT", [128, 8192], BF16, stack=es)
        E3 = cx.sbuf("E3", [128, 64, 96], BF16, stack=es)
        load(E3, E3[:], K["E3"], K["E3"].h, eng="pool", stream="fld")
        X, TT = X_t[:], TT_t[:]
        for q in range(4):
            for r in range(4):
                load(X_t, X[r * 16:(r + 1) * 16, 0:128 * 64], f_all, fav[r, q], eng="pool", stream="fld")
            Xv = X[0:64, :].rearrange("p (b c) -> p c b", c=64)
            TTv = TT[:, 0:64 * 128].rearrange("p (c k) -> p c k", c=64)
            for cg in range(16):
                if side_steps and cg % 2 == 0:
                    side_steps.pop(0)()
                ps = PF.get()
                for i in range(4):
                    O("pe", "matmul", [ps], [L1, X_t], ps[:, i * 128:(i + 1) * 128], lhsT=Xv[:, cg * 4 + i, :], rhs=L1[:], start=True, stop=True)
                evac(TT_t, TTv[:, cg * 4:(cg + 1) * 4, :], ps, ps[:].rearrange("p (c k) -> p c k", c=4), eng="act")
            zq = WP.get()
            zqv = zq[0:64, :].rearrange("p (c kh kl) -> p c kl kh", c=2, kl=64)
            for kg in range(8):
                ps = PF.get()
                for i in range(8):
                    klo = kg * 8 + i
                    O("pe", "matmul", [ps], [TT_t, E3], ps[0:64, i * 64:(i + 1) * 64], lhsT=TTv[:, :, klo], rhs=E3[:, klo, 32:96],
                      start=True, stop=False)
                    O("pe", "matmul", [ps], [TT_t, E3], ps[0:64, i * 64:(i + 1) * 64], lhsT=TTv[:, :, 64 + klo], rhs=E3[:, klo, 0:64],
                      start=False, stop=True)
                psv = ps[0:64, :].rearrange("p (kl c kh) -> p kl c kh", kl=8, c=2)
                for c in range(2):
                    evac(zq, zqv[:, c, kg * 8:(kg + 1) * 8, :], ps, psv[:, :, c, :], eng="act")
            cx.dma("pool", "fst", zt_d, zt_d.h[:, 2 * q:2 * q + 2, :], zq, zq[0:64, :].rearrange("p (c t) -> p c t", c=2))
        return es

    def g_post_norm_residual(xt, ysrc_t, y_aps, st, gt=None):
        srcs = ysrc_t if isinstance(ysrc_t, (list, tuple)) else [ysrc_t] * len(y_aps)
        yield from g_rms_rstd(srcs, y_aps, D, st)
        gtt = gt or MODG
        for i, yap in enumerate(y_aps):
            tmp = TMPH.get()
            O("dve", "scalar_tensor_tensor", [tmp], [srcs[i], st, gtt], out=tmp[:], in0=yap, scalar=st[:, 2:3],
              in1=gtt[:, i * 512:(i + 1) * 512], op0=ALU.mult, op1=ALU.mult)
            yield
            O("dve", "tensor_tensor", [xt], [tmp, xt], out=xt[:, i * 512:(i + 1) * 512], in0=tmp[:],
              in1=xt[:, i * 512:(i + 1) * 512], op=ALU.add)
            yield

    def post_norm_residual(xt, ysrc_t, y_aps, st, gt=None):
        run(g_post_norm_residual(xt, ysrc_t, y_aps, st, gt))

    def alloc_passB(es):
        return dict(mT=cx.sbuf("mT", [128, 8, G * 128], BF16, stack=es), hT=cx.sbuf("hT", [128, 8, G * 128], BF16, stack=es),
                    big16=cx.sbuf("big16", [128, G * D], F32, stack=es), r4=cx.sbuf("r4", [128, 4, G * 128], BF16, stack=es),
                    sguT=cx.sbuf("sguT", [128, 2, G * 128], BF16, stack=es), ZTL=cx.sbuf("ZTL", [64, 8, G * 128], BF16, stack=es))

    def pass_B(l, grp, last, bufs, prev_fin=None):
        nch, xsrc, x_ap, is_ctx = grp["nch"], grp["xsrc"], grp["x_ap"], grp["is_ctx"]
        xdst, xd_ap = grp["xdst"], grp["xd_ap"]
        xmid, xm_ap = grp["xmid"], grp["xm_ap"]
        T = nch * 128
        ropet = ROPE1 if is_ctx else ROPE
        Sd = Sc_d if is_ctx else S_d
        mT, hT, big16, r4, sguT, ZTL = bufs["mT"], bufs["hT"], bufs["big16"], bufs["r4"], bufs["sguT"], bufs["ZTL"]
        qT = kT = v_tok = gs = big16
        retT = aT = r4
        B16 = big16[:].bitcast(BF16)
        qT3 = B16[:, 0:2048].rearrange("p (h t) -> p h t", h=4)
        kT3 = B16[:, 2048:4096].rearrange("p (h t) -> p h t", h=4)
        v3 = B16[:, 4096:6144].rearrange("p (c n) -> p c n", c=G)
        gs3 = B16[:, 6144:8192].rearrange("p (c n) -> p c n", c=G)
        y2v = big16[:].rearrange("p (c d) -> p c d", c=G)
        retT3, sguT3 = r4[:], sguT[:]
        tag = "l%d_%s" % (l, "c" if is_ctx else "x%d" % grp["c0"])
        row = 1 if is_ctx else 0

        def prefetch_inputs(g):
            gctx = g["is_ctx"]
            Tg = g["nch"] * 128
            hTd = hTc_d if gctx else hT_d
            hTd_ap = hTc_d.h if gctx else hT_d.h[g["c0"] // G]
            load(hT, hT[:, :, 0:Tg], hTd, hTd_ap[:, :, 0:Tg])
            if not gctx:
                c0g = g["c0"]
                for i, nm in enumerate(("rope_cos", "rope_sin")):
                    load(ROPE, ROPE[:, i], K[nm], K[nm].h[c0g * 128:(c0g + g["nch"]) * 128, :].rearrange("(c p) i -> p c i", p=128))

        if not grp.get("prefetched"):
            prefetch_inputs(grp)
        if grp.get("load_mod", True):
            load_vec(MODG, row, 0, "G", l)
            load_vec(MODA, row, 1, "A", l)
            load_vec(MODB, row, 1, "B", l)
            load_vec(MODG2, row, 1, "G", l)
        wv = load_win(l, 2304, 512)

        def g_uvs_b(ci, ps):
            gu = TMPH.get()
            tt = TMPH.get()
            O("act", "activation", [gu], [ps], out=gu[:], in_=ps[:], func=AF.Gelu_apprx_tanh)
            yield
            st = STAT.get()
            O("act", "activation", [tt], [gu], out=tt[:, 0:256], in_=gu[:, 256:512], func=AF.Square)
            yield
            O("dve", "tensor_reduce", [st], [tt], out=st[:, 0:4], in_=tt[:, 0:256].rearrange("p (g c) -> p g c", g=4),
              axis=mybir.AxisListType.X, op=ALU.add)
            yield
            O("act", "activation", [st], [st, epsb], out=st[:, 0:4], in_=st[:, 0:4], func=AF.Sqrt, scale=1.0 / 64, bias=epsb[:, 0:1])
            yield
            O("dve", "reciprocal", [st], [st], out=st[:, 4:8], in_=st[:, 0:4])
            yield
            O("dve", "tensor_tensor", [tt], [gu, st], out=tt[:, 0:256].rearrange("p (g c) -> p g c", g=4),
              in0=gu[:, 256:512].rearrange("p (g c) -> p g c", g=4), in1=st[:, 4:8].unsqueeze(2).broadcast_to([128, 4, 64]), op=ALU.mult)
            yield
            vnb = TB.get()
            O("dve", "tensor_tensor", [vnb], [tt, SGN], out=vnb[:, 0:256], in0=tt[:, 0:256], in1=SGN[:], op=ALU.mult)
            yield
            ps2 = PF.get()
            for g in range(4):
                O("pe", "matmul", [ps2], [SGW, vnb], ps2[:, g * 64:(g + 1) * 64], lhsT=SGW[:, g, :], rhs=vnb[:, g * 64:(g + 1) * 64],
                  start=True, stop=True)
            O("dve", "tensor_tensor", [tt], [ps2, SGB], out=tt[:, 256:512].rearrange("p (g c) -> p g c", g=4),
              in0=ps2[:, 0:256].rearrange("p (g c) -> p g c", g=4), in1=SGB[:].unsqueeze(2).broadcast_to([128, 4, 64]), op=ALU.add)
            yield
            sgb = TB.get()
            O("dve", "tensor_tensor", [sgb], [tt, gu], out=sgb[:, 0:256], in0=tt[:, 256:512], in1=gu[:, 0:256], op=ALU.mult)
            if ci == 0:
                dbg("sgu_" + tag, sgb, sgb[:, 0:256], [128, 256], BF16)
            yield
            transpose_blocks(sgb, [sgb[:, h * 128:(h + 1) * 128] for h in range(2)], sguT, sguT3[:, :, ci * 128:(ci + 1) * 128])
            yield

        prev_fin = list(prev_fin or [])
        for c2 in range(0, nch, 2):
            cis = list(range(c2, min(c2 + 2, nch)))
            pss = [zblock(hT, ci, wv, 512) for ci in cis]
            interleave([g_uvs_b(ci, ps) for ci, ps in zip(cis, pss)])
            if prev_fin:
                prev_fin.pop(0)()
        while prev_fin:
            prev_fin.pop(0)()
        wv = load_win(l, 0, 512)

        def q_b(ci, ps):
            tb = TB.get()
            rope(ps, ropet, ci, tb, tb[:])
            transpose_blocks(tb, [tb[:, h * 128:(h + 1) * 128] for h in range(4)], qT, qT3[:, :, ci * 128:(ci + 1) * 128])
        pend = zblock(hT, 0, wv, 512)
        for ci in range(nch):
            nxt = zblock(hT, ci + 1, wv, 512) if ci + 1 < nch else None
            q_b(ci, pend)
            pend = nxt
        kvd_ap = kvc_d.h if is_ctx else kv_d.h[grp["c0"] // G]
        kvd = kvc_d if is_ctx else kv_d
        load(r4, r4[:].rearrange("p h t -> p (h t)").rearrange("p (c n) -> p c n", c=G)[:, 0:nch, :], kvd, kvd_ap[0, :, 0:nch, :])
        load(big16, v3[:, 0:nch, :], kvd, kvd_ap[1, :, 0:nch, :])
        kst = r4[:].rearrange("p h t -> p (h t)").rearrange("p (c n) -> p c n", c=G)
        for ci in range(nch):
            transpose_blocks(r4, [kst[:, ci, h * 128:(h + 1) * 128] for h in range(4)], kT, kT3[:, :, ci * 128:(ci + 1) * 128])
        wv = load_win(l, 1536, 512)
        for ci in range(nch):
            ps = zblock(hT, ci, wv, 512)
            O("act", "activation", [gs], [ps], out=gs3[:, ci, :], in_=ps[:], func=AF.Silu)
        def b3_s1(ci):
            gci = ci if is_ctx else grp["c0"] + ci
            sl = slice(ci * 128, (ci + 1) * 128)
            sfb = SFB.get()
            load(sfb, sfb[:], Sd, Sd.h[:, gci].rearrange("d p n -> p d n"))
            Sf = Sb = sfb
            ps = PF.get()
            for h in range(4):
                O("pe", "matmul", [ps], [kT, qT], ps[:, h * 128:(h + 1) * 128], lhsT=kT3[:, h, sl], rhs=qT3[:, h, sl], start=True, stop=True)
            scm = TB.get()
            O("dve", "tensor_tensor", [scm], [ps, DT], out=scm[:], in0=ps[:], in1=DT[:].rearrange("p h c -> p (h c)"), op=ALU.mult)
            qf = TB.get()
            qb = TB.get()
            O("dve", "tensor_tensor", [qf], [qT, QF], out=qf[:].rearrange("p (h c) -> p h c", h=4), in0=qT3[:, :, sl], in1=QF[:], op=ALU.mult)
            O("dve", "tensor_tensor", [qb], [qT, QB], out=qb[:].rearrange("p (h c) -> p h c", h=4), in0=qT3[:, :, sl], in1=QB[:], op=ALU.mult)
            return Sf, Sb, scm, qf, qb

        def g_b3_s2(ci, Sf, Sb, scm, qf, qb):
            sl = slice(ci * 128, (ci + 1) * 128)
            po = PF.get()
            for h in range(4):
                hs = slice(h * 128, (h + 1) * 128)
                O("pe", "matmul", [po], [scm, v_tok], po[:, hs], lhsT=scm[:, hs], rhs=v3[:, ci, hs], start=True, stop=False)
                O("pe", "matmul", [po], [qf, Sf], po[:, hs], lhsT=qf[:, hs], rhs=Sf[:, 0, hs], start=False, stop=False)
                O("pe", "matmul", [po], [qb, Sb], po[:, hs], lhsT=qb[:, hs], rhs=Sb[:, 1, hs], start=False, stop=True)
            st = STAT.get()
            O("dve", "memset", [st], [], st[:], 0.0)
            yield
            for h in range(4):
                junk = JUNK.get()
                O("act", "activation", [junk, st], [po, st], out=junk[:, 0:128], in_=po[:, h * 128:(h + 1) * 128], func=AF.Square,
                  accum_out=st[:, h:h + 1])
            yield
            O("act", "activation", [st], [st, epsb], out=st[:, 0:4], in_=st[:, 0:4], func=AF.Sqrt, scale=1.0 / 128, bias=epsb[:, 0:1])
            yield
            O("dve", "reciprocal", [st], [st], out=st[:, 4:8], in_=st[:, 0:4])
            yield
            rt = TB.get()
            for h in range(4):
                hs = slice(h * 128, (h + 1) * 128)
                O("dve", "scalar_tensor_tensor", [rt], [po, st, gs], out=rt[:, hs], in0=po[:, hs], scalar=st[:, 4 + h:5 + h],
                  in1=gs3[:, ci, hs], op0=ALU.mult, op1=ALU.mult)
            if ci == 0:
                dbg("ret_" + tag, rt, rt[:], [128, 512], BF16)
            yield
            transpose_blocks(rt, [rt[:, h * 128:(h + 1) * 128] for h in range(4)], retT, retT3[:, :, sl])
            yield

        s1 = [b3_s1(ci) for ci in range(nch)]
        for c2 in range(0, nch, 2):
            interleave([g_b3_s2(ci, *s1[ci]) for ci in range(c2, min(c2 + 2, nch))])
        if is_ctx:
            ZT_t, ZT3 = ZTC, ZTC[:]
        else:
            ZT_t = ZTL
            ZT3 = ZTL[:]
            load(ZTL, ZT3, zt_d, zt_d.h[:, :, grp["c0"] * 128:grp["c0"] * 128 + T])
        TBW = min(T, 512)
        ntb = T // TBW
        for db in range(8):
            j = db % 2
            if j == 0:
                dbp = db // 2
                wgA = WP.get()
                wgAv = wgA[:].rearrange("p (b kc n) -> p b kc n", b=2, kc=8)
                for br in range(2):
                    c0w = 2816 + br * 1024 + dbp * 256
                    wload(wgA, wgAv[:, br], W["w_in"], W["w_in"].h[l, :, c0w:c0w + 256].rearrange("(kc p) n -> p kc n", p=128))
                wgB = WP.get()
                wg2v = wgB[:, 0:2048].rearrange("p (kc n) -> p kc n", kc=8)
                wa2v = wgB[:, 2048:3072].rearrange("p (kc n) -> p kc n", kc=4)
                wc2v = wgB[:, 3072:3584].rearrange("p (kc n) -> p kc n", kc=2)
                c0w = 2816 + 2 * 1024 + dbp * 256
                wload(wgB, wg2v, W["w_in"], W["w_in"].h[l, :, c0w:c0w + 256].rearrange("(kc p) n -> p kc n", p=128))
                wload(wgB, wa2v, W["w_branch_a"], W["w_branch_a"].h[l, :, dbp * 256:(dbp + 1) * 256].rearrange("(kc p) n -> p kc n", p=128))
                wload(wgB, wc2v, W["w_branch_c"], W["w_branch_c"].h[l, :, dbp * 256:(dbp + 1) * 256].rearrange("(kc p) n -> p kc n", p=128))
                wgC = WP.get()
                wbp2v = wgC[0:64, 0:2048].rearrange("p (kc n) -> p kc n", kc=8)
                load(wgC, wbp2v, wbp_d, wbp_d.h[:, :, dbp * 256:(dbp + 1) * 256], eng="pool", stream="w")
            js = slice(j * 128, (j + 1) * 128)
            gate_t = [wgA, wgA, wgB]
            gate_v = [wgAv[:, 0, :, js], wgAv[:, 1, :, js], wg2v[:, :, js]]
            wa_v, wc_v, wbp_v = wa2v[:, :, js], wc2v[:, :, js], wbp2v[:, :, js]
            for tbi in range(ntb):
                ts = slice(tbi * TBW, (tbi + 1) * TBW)
                acc = TMPH.get()
                for br in range(3):
                    pg = PF.get()
                    for kc in range(8):
                        O("pe", "matmul", [pg], [gate_t[br], hT], pg[:, 0:TBW], lhsT=gate_v[br][:, kc, :], rhs=hT[:, kc, ts], start=(kc == 0), stop=(kc == 7))
                    pb = PF.get()
                    if br == 0:
                        for kc in range(4):
                            O("pe", "matmul", [pb], [wgB, retT], pb[:, 0:TBW], lhsT=wa_v[:, kc, :], rhs=retT3[:, kc, ts], start=(kc == 0), stop=(kc == 3))
                    elif br == 1:
                        for kc in range(8):
                            O("pe", "matmul", [pb], [wgC, ZT_t], pb[:, 0:TBW], lhsT=wbp_v[:, kc, :], rhs=ZT3[:, kc, ts], start=(kc == 0), stop=(kc == 7))
                    else:
                        for kc in range(2):
                            O("pe", "matmul", [pb], [wgC if False else wgB, sguT], pb[:, 0:TBW], lhsT=wc_v[:, kc, :], rhs=sguT3[:, kc, ts], start=(kc == 0), stop=(kc == 1))
                    sg = TMPH.get()
                    O("act", "activation", [sg], [pg], out=sg[:, 0:TBW], in_=pg[:, 0:TBW], func=AF.Sigmoid)
                    if br == 0:
                        O("dve", "tensor_tensor", [acc], [sg, pb], out=acc[:, 0:TBW], in0=sg[:, 0:TBW], in1=pb[:, 0:TBW], op=ALU.mult)
                    else:
                        O("dve", "tensor_tensor", [sg], [sg, pb], out=sg[:, 0:TBW], in0=sg[:, 0:TBW], in1=pb[:, 0:TBW], op=ALU.mult)
                        if br == 1:
                            O("dve", "tensor_tensor", [acc], [acc, sg], out=acc[:, 0:TBW], in0=acc[:, 0:TBW], in1=sg[:, 0:TBW], op=ALU.add)
                        else:
                            O("dve", "tensor_tensor", [mT], [acc, sg], out=mT[:, db, ts], in0=acc[:, 0:TBW], in1=sg[:, 0:TBW], op=ALU.add)
        dbg("mT_" + tag, mT, mT[:, :, 0:128], [128, 8, 128], BF16)
        wo = [WP.get(), WP.get()]
        wov = []
        for half in range(2):
            v = wo[half][:].rearrange("p (kc n) -> p kc n", kc=8)
            wload(wo[half], v, W["w_out"], W["w_out"].h[l, :, half * 512:(half + 1) * 512].rearrange("(kc p) n -> p kc n", p=128))
            wov.append(v)
        def b6_a(ci):
            xt = XIN.get()
            load(xt, xt[:], xsrc, x_ap(ci))
            pss = []
            for half in range(2):
                ps = PF.get()
                for kc in range(8):
                    O("pe", "matmul", [ps], [mT, wo[half]], ps[:], lhsT=mT[:, kc, ci * 128:(ci + 1) * 128], rhs=wov[half][:, kc, :],
                      start=(kc == 0), stop=(kc == 7))
                pss.append(ps)
            return xt, pss

        def g_b6_b(ci, xt, pss):
            st = STAT.get()
            yield from g_post_norm_residual(xt, pss, [pss[0][:], pss[1][:]], st)
            if ci == 0:
                dbg("xmid_" + tag, xt, xt[:], [128, D])
            cx.dma("sp", "st", xmid, xm_ap(ci), xt, xt[:])
            yield
            yield from g_norm_mod_T(xt, hT, ci)

        for c2 in range(0, nch, 2):
            cis = list(range(c2, min(c2 + 2, nch)))
            As = [b6_a(ci) for ci in cis]
            interleave([g_b6_b(ci, *a) for ci, a in zip(cis, As)])
        aT3 = r4[:]
        for fb in range(8):
            wu = WP.get()
            wuv = wu[:].rearrange("p (kc n) -> p kc n", kc=8)
            wload(wu, wuv, W["w_up"], W["w_up"].h[l, :, fb * 512:(fb + 1) * 512].rearrange("(kc p) n -> p kc n", p=128))
            wd = WP.get()
            wdv = wd[:].rearrange("p (f n) -> p f n", f=4)
            wload(wd, wdv, W["w_down"], W["w_down"].h[l, fb * 512:(fb + 1) * 512, :].rearrange("(f p) n -> p f n", p=128))
            for tbi in range(ntb):
                ts = slice(tbi * TBW, (tbi + 1) * TBW)
                for f in range(4):
                    ps = PF.get()
                    for kc in range(8):
                        O("pe", "matmul", [ps], [wu, hT], ps[:, 0:TBW], lhsT=wuv[:, kc, f * 128:(f + 1) * 128], rhs=hT[:, kc, ts],
                          start=(kc == 0), stop=(kc == 7))
                    r = TMPH.get()
                    O("act", "activation", [r], [ps], out=r[:, 0:TBW], in_=ps[:, 0:TBW], func=AF.Relu)
                    O("act", "activation", [aT], [r], out=aT3[:, f, ts], in_=r[:, 0:TBW], func=AF.Square)
            if fb == 7 and grp.get("next") is not None:
                prefetch_inputs(grp["next"])
                grp["next"]["prefetched"] = True
            for ci in range(nch):
                for half in range(2):
                    ps = PF.get()
                    for f in range(4):
                        O("pe", "matmul", [ps], [aT, wd], ps[:], lhsT=aT3[:, f, ci * 128:(ci + 1) * 128], rhs=wdv[:, f, half * 512:(half + 1) * 512],
                          start=(f == 0), stop=(f == 3))
                    ya = y2v[:, ci, half * 512:(half + 1) * 512]
                    if fb == 0:
                        evac(big16, ya, ps, ps[:])
                    else:
                        O("dve", "tensor_tensor", [big16], [big16, ps], out=ya, in0=ya, in1=ps[:], op=ALU.add)
        def g_fin(ci):
            xt = XIN.get()
            load(xt, xt[:], xmid, xm_ap(ci))
            st = STAT.get()
            yield from g_post_norm_residual(xt, big16, [y2v[:, ci, 0:512], y2v[:, ci, 512:1024]], st, gt=MODG2)
            if ci == 0:
                dbg("xout_" + tag, xt, xt[:], [128, D])
            cx.dma("sp", "st", xdst, xd_ap(ci), xt, xt[:])
            yield
        return [lambda c2=c2: interleave([g_fin(ci) for ci in range(c2, min(c2 + 2, nch))]) for c2 in range(0, nch, 2)]

    def row_ap(t, c0):
        return lambda ci: t.h[(c0 + ci) * 128:(c0 + ci + 1) * 128, :]

    for l in range(nlayers):
        last = (l == DEPTH - 1)
        if l == 0:
            setup_mod(0, cbs=range(0, 4), sel=[(0, "A")])
        layer_setup(l)
        csrc = ctx_in if l == 0 else cs_d
        cgrp = dict(nch=CTXCH, xsrc=csrc, x_ap=row_ap(csrc, 0), is_ctx=True, c0=0, last=last,
                    xmid=cs_d, xm_ap=row_ap(cs_d, 0), xdst=cs_d, xd_ap=row_ap(cs_d, 0))
        pass_A(l, cgrp)
        if l == 0:
            setup_mod(0, cbs=range(4, 12), sel=[(0, "G"), (1, "A"), (1, "G")])
        recur(Uc_d, Sc_d, CTXCH, None, store=not last)
        for d in range(2):
            O("dve", "tensor_copy", [SCTX[d]], [SCUR[d]], out=SCTX[d][:], in_=SCUR[d][:])
        dbg("sctx_l%d" % l, SCTX[0], SCTX[0][:], [128, 512])
        if not last:
            esB = ExitStack()
            for f in pass_B(l, cgrp, last, alloc_passB(esB)):
                f()
            cx.barrier()
            esB.close()
        xsrc = x_in if l == 0 else xs_d
        xdst = out_d if last else xs_d
        groups = []
        for g in range(NCH // G):
            c0 = g * G
            groups.append(dict(nch=G, xsrc=xsrc, x_ap=row_ap(xsrc, c0), is_ctx=False, c0=c0,
                               xmid=xs_d if not last else xs_d, xm_ap=row_ap(xs_d, c0), xdst=xdst, xd_ap=row_ap(xdst, c0)))
        esA = ExitStack()
        bufsA = alloc_passA(esA)
        for gi, grp in enumerate(groups):
            pass_A(l, grp, bufsA)
            if gi == 1 and l + 1 < nlayers:
                setup_mod(l + 1)
        cx.barrier()
        esA.close()
        if stop_after == "A":
            break
        fourier_gather()
        exchange_states()
        dbg("sstart_l%d" % l, SSTART[0], SSTART[0][:], [128, 512])
        es_f = fourier_latent()
        recur(U_d, S_d, NCH, SSTART, store=True)
        cx.barrier()
        es_f.close()
        if stop_after == "F":
            break
        esB = ExitStack()
        bufsB = alloc_passB(esB)
        for gi, grp in enumerate(groups):
            grp["next"] = groups[gi + 1] if gi + 1 < len(groups) else None
            grp["load_mod"] = (gi == 0)
            grp["prefetched"] = False
        fin = None
        for grp in groups:
            fin = pass_B(l, grp, last, bufsB, prev_fin=fin)
        for f in fin:
            f()
        cx.barrier()
        esB.close()

    cx.barrier(full=True)
    cx.emit()
    DEBUG['min_free'] = cx.min_free
    DEBUG['ops'] = {k: len(v.ops) for k, v in cx.engs.items()}
    cx.close()
    return nc, dbg_out


_CACHE = {}


def make_in_maps(inputs):
    in_maps = []
    f32 = np.float32
    shared = {}
    for k in WEIGHT_SPECS:
        a = np.ascontiguousarray(np.asarray(inputs[k], dtype=f32))
        shared[k] = a.reshape(WEIGHT_SPECS[k])
    x = np.asarray(inputs["x"], dtype=f32)
    ctx = np.asarray(inputs["ctx"], dtype=f32)
    c = np.asarray(inputs["c"], dtype=f32)
    c_ctx = np.asarray(inputs["c_ctx"], dtype=f32)
    for core in range(8):
        b, j = core // 4, core % 4
        m = dict(shared)
        m["x"] = np.ascontiguousarray(x[b, 2048 * j:2048 * (j + 1), :])
        m["ctx"] = np.ascontiguousarray(ctx[b])
        m["c2"] = np.ascontiguousarray(np.stack([c[b], c_ctx], 0))
        m.update(host_consts(core))
        in_maps.append(m)
    return in_maps


def kernel(**inputs):
    if "nc" not in _CACHE:
        _CACHE["nc"] = build()[0]
    nc = _CACHE["nc"]
    in_maps = make_in_maps(inputs)
    res = run_bass_kernel_spmd(nc, in_maps, core_ids=list(range(8)))
    out = np.zeros((2, 8192, D), np.float32)
    for core in range(8):
        b, j = core // 4, core % 4
        out[b, 2048 * j:2048 * (j + 1), :] = res.results[core]["out"]
    return out
```

```python
from contextlib import ExitStack
import math
import numpy as np
import ml_dtypes
import concourse.bass as bass
import concourse.mybir as mybir
from concourse.bass_utils import run_bass_kernel_spmd

F32 = mybir.dt.float32
BF16 = mybir.dt.bfloat16
AF = mybir.ActivationFunctionType
ALU = mybir.AluOpType

D = 1024
DEPTH = 2
NCH = 16
G = 4
CTXCH = 2
IN_W = 5888
EPS = 1e-6
SK = 128 ** -0.5
DEBUG = {}


class Tile:
    __slots__ = ("name", "h", "w", "r")

    def __init__(self, name, h):
        self.name = name
        self.h = h
        self.w = {}
        self.r = {}

    def __getitem__(self, k):
        return self.h[k]


class Eng:
    def __init__(self, name, sem):
        self.name = name
        self.sem = sem
        self.count = 0
        self.ops = []
        self.waited = {}


class Stream:
    K = 8

    def __init__(self, name, sems):
        self.name = name
        self.sems = sems
        self.n = 0
        self.sem = sems[0]
        self.count = 0


class Ctx:
    ENG_NAMES = ("pe", "act", "dve", "pool", "sp")

    def __init__(self, nc):
        self.nc = nc
        self.stack = ExitStack()
        self.engs = {}
        for n in self.ENG_NAMES:
            sem = self.stack.enter_context(nc.semaphore("sem_" + n))
            self.engs[n] = Eng(n, sem)
        self.streams = {}
        self.ntiles = 0

    def stream(self, name, k=None):
        if name not in self.streams:
            k = k or Stream.K
            sems = [self.stack.enter_context(self.nc.semaphore("dq_%s%d" % (name, i))) for i in range(k)]
            self.streams[name] = Stream(name, sems)
        return self.streams[name]

    def sbuf(self, name, shape, dtype, stack=None):
        self.ntiles += 1
        h = (stack or self.stack).enter_context(self.nc.sbuf_tensor(f"{name}_{self.ntiles}", list(shape), dtype))
        self.min_free = min(getattr(self, "min_free", 1 << 30), self.nc.sbuf_bytes_remaining)
        return Tile(name, h)

    def psum(self, name, shape, dtype=F32):
        self.ntiles += 1
        h = self.stack.enter_context(self.nc.psum_tensor(f"{name}_{self.ntiles}", list(shape), dtype))
        return Tile(name, h)

    def dram(self, name, shape, dtype, kind="Internal"):
        t = self.nc.dram_tensor(name, list(shape), dtype, kind=kind)
        return Tile(name, t.ap())

    def _collect(self, eng, reads, writes):
        need = {}

        def add(d):
            for s, v in d.items():
                if need.get(s, 0) < v:
                    need[s] = v
        for t in reads:
            add(t.w)
        for t in writes:
            add(t.w)
            add(t.r)
        waits = []
        for s, v in need.items():
            if s is eng.sem and eng.name == "pe":
                continue
            if eng.waited.get(s, 0) >= v:
                continue
            eng.waited[s] = v
            waits.append((s, v))
        return waits

    def _mark(self, ev, reads, writes):
        for t in reads:
            if t.r.get(ev[0], 0) < ev[1]:
                t.r[ev[0]] = ev[1]
        for t in writes:
            t.w = {ev[0]: ev[1]}
            t.r = {}

    def op(self, engname, fn, reads=(), writes=()):
        eng = self.engs[engname]
        waits = self._collect(eng, reads, writes)
        eng.count += 1
        ev = (eng.sem, eng.count)
        eng.ops.append((waits, fn, (eng.sem, 1)))
        self._mark(ev, reads, writes)
        return ev

    def dma(self, engname, streamname, out_t, out_ap, in_t, in_ap, **kw):
        eng = self.engs[engname]
        st = self.stream(streamname)
        waits = self._collect(eng, [in_t], [out_t])
        k = len(st.sems)
        idx = st.n
        st.n += 1
        sem = st.sems[idx % k]
        if idx >= k:
            pv = 16 * (idx // k)
            if eng.waited.get(sem, 0) < pv:
                eng.waited[sem] = pv
                waits.append((sem, pv))
        ev = (sem, 16 * (idx // k + 1))

        def fn(e, out_ap=out_ap, in_ap=in_ap, kw=kw):
            return e.dma_start(out=out_ap, in_=in_ap, **kw)
        eng.ops.append((waits, fn, (sem, 16)))
        self._mark(ev, [in_t], [out_t])
        return ev

    def custom(self, engname, fn, reads, writes, st, inc):
        eng = self.engs[engname]
        waits = self._collect(eng, reads, writes)
        if st.count and eng.waited.get(st.sem, 0) < st.count:
            eng.waited[st.sem] = st.count
            waits.append((st.sem, st.count))
        st.count += inc
        ev = (st.sem, st.count)
        eng.ops.append((waits, fn, (st.sem, inc)))
        self._mark(ev, reads, writes)
        return ev

    def barrier(self, full=False, pool=False):
        evs = {}
        for e in self.engs.values():
            if e.count:
                evs[e.sem] = e.count
        for s in self.streams.values():
            if not full and s.name in ("w", "cc"):
                continue
            if s.count:
                evs[s.sem] = s.count
            k = len(s.sems)
            for i in range(min(k, s.n)):
                evs[s.sems[i]] = 16 * ((s.n - 1 - i) // k + 1)
        for e in self.engs.values():
            if e.name == "pool" and not (full or pool):
                continue
            waits = []
            for s, v in evs.items():
                if s is e.sem:
                    continue
                if e.waited.get(s, 0) >= v:
                    continue
                e.waited[s] = v
                waits.append((s, v))
            if waits:
                e.ops.append((waits, None, None))

    def emit(self):
        engs = self.engs

        def replay(handle, ops):
            for waits, fn, inc in ops:
                for s, v in waits:
                    handle.wait_ge(s, v)
                if fn is None:
                    continue
                fn(handle).then_inc(inc[0], inc[1])

        with self.nc.allow_non_contiguous_dma(reason="strided layout DMAs (small)"), self.nc.Block() as block:
            @block.tensor
            def _(e):
                replay(e, engs["pe"].ops)

            @block.scalar
            def _(e):
                replay(e, engs["act"].ops)

            @block.vector
            def _(e):
                replay(e, engs["dve"].ops)

            @block.gpsimd
            def _(e):
                replay(e, engs["pool"].ops)

            @block.sync
            def _(e):
                replay(e, engs["sp"].ops)

    def close(self):
        self.stack.close()


class Pool:
    def __init__(self, tiles):
        self.tiles = tiles
        self.i = 0

    def get(self):
        t = self.tiles[self.i % len(self.tiles)]
        self.i += 1
        return t


def host_consts(core):
    j = core % 4
    bf = ml_dtypes.bfloat16
    c = {}
    t = np.arange(2048, dtype=np.float64) + 2048 * j
    row = np.floor(t / 64.0)
    col = t - 64.0 * row
    freqs = (10000.0 ** (-np.arange(32, dtype=np.float32) / np.float32(32))).astype(np.float64)
    ang = np.concatenate([row[:, None] * freqs, col[:, None] * freqs], -1)
    ang32 = np.concatenate([(row.astype(np.float32)[:, None] * freqs.astype(np.float32)),
                            (col.astype(np.float32)[:, None] * freqs.astype(np.float32))], -1).astype(np.float64)
    c["rope_cos"] = np.cos(ang32).astype(np.float32)
    c["rope_sin"] = np.sin(ang32).astype(np.float32)
    a = np.arange(64)[:, None]
    kl = np.arange(64)[None, :]
    th = 2 * np.pi * a * kl / 64.0
    c["L1"] = np.concatenate([np.cos(th), -np.sin(th)], 1).astype(bf)
    s = 1.0 / math.sqrt(8192 * 64)
    b = np.arange(128)[:, None, None]
    klo = np.arange(64)[None, :, None]
    kh = (32 * j + np.arange(32))[None, None, :]
    k = 64 * kh + klo
    th3 = 2 * np.pi * ((b * k) % 8192) / 8192.0
    Ere = s * np.cos(th3)
    Eim = -s * np.sin(th3)
    c["E3"] = np.concatenate([-Eim, Ere, Eim], 2).astype(bf)
    s2 = 1.0 / math.sqrt(256 * 64)
    n = np.arange(256)[:, None]
    kk = np.arange(256)[None, :]
    th2 = 2 * np.pi * ((n * kk) % 256) / 256.0
    d256 = np.concatenate([s2 * np.cos(th2), -s2 * np.sin(th2)], 1)
    c["D256"] = d256.reshape(2, 128, 512).transpose(1, 0, 2).astype(bf).copy()
    m = np.arange(64)[:, None]
    jj = np.arange(64)[None, :]
    thc = 2 * np.pi * ((m * jj) % 64) / 64.0
    c["CS64"] = np.concatenate([np.cos(thc), np.sin(thc)], 1).astype(bf)
    sidx = np.arange(128)[:, None].astype(np.float32)
    cidx = np.arange(128)[None, :].astype(np.float32)
    rc = np.zeros((128, 6, 128), np.float32)
    rc[:, 0, :] = np.maximum(cidx - sidx, 0)
    rc[:, 1, :] = np.maximum(sidx - cidx, 0)
    rc[:, 2, :] = (cidx >= sidx) * SK
    rc[:, 3, :] = (sidx > cidx) * SK
    rc[:, 4, :] = cidx + 1.0
    rc[:, 5, :] = 128.0 - cidx
    c["RC"] = rc
    wc = np.zeros((128, 2), np.float32)
    wc[:, 0] = 127.0 - np.arange(128)
    wc[:, 1] = np.arange(128)
    c["WC"] = wc
    mexp = np.zeros((2, 5), np.float32)
    mask = np.zeros((2, 5), np.float32)
    for jp in range(4):
        if jp < j:
            mexp[0, jp] = j - 1 - jp
            mask[0, jp] = 1
        if jp > j:
            mexp[1, jp] = jp - j - 1
            mask[1, jp] = 1
    mexp[0, 4] = j
    mask[0, 4] = 1
    mexp[1, 4] = 3 - j
    mask[1, 4] = 1
    cm = np.zeros((128, 2, 2, 5), np.float32)
    cm[:, 0] = mexp[None] * 2048.0
    cm[:, 1] = mask[None]
    c["CMX"] = cm.reshape(128, 20)
    c["IDENT"] = np.eye(128, dtype=np.float32).astype(bf)
    return c


CONST_SPECS = {
    "rope_cos": ([2048, 64], F32), "rope_sin": ([2048, 64], F32),
    "L1": ([64, 128], BF16), "E3": ([128, 64, 96], BF16), "D256": ([128, 2, 512], BF16),
    "CS64": ([64, 128], BF16), "RC": ([128, 6, 128], F32), "WC": ([128, 2], F32),
    "CMX": ([128, 20], F32), "IDENT": ([128, 128], BF16),
}

WEIGHT_SPECS = {
    "w_mod": [DEPTH, D, 6 * D], "b_mod": [DEPTH, 6 * D], "g_pre_mix": [DEPTH, D], "g_post_mix": [DEPTH, D],
    "g_pre_mlp": [DEPTH, D], "g_post_mlp": [DEPTH, D], "w_in": [DEPTH, D, IN_W],
    "ret_decay_logit": [DEPTH, 8], "sgu_w_s": [DEPTH, 4, 128, 128], "sgu_b_s": [DEPTH, 4, 128],
    "sgu_norm": [DEPTH, 256], "w_branch_a": [DEPTH, 512, D], "w_branch_b": [DEPTH, 256, D],
    "w_branch_c": [DEPTH, 256, D], "w_out": [DEPTH, D, D], "w_up": [DEPTH, D, 4 * D], "w_down": [DEPTH, 4 * D, D],
}


def build(debug=(), nlayers=DEPTH, stop_after=None):
    nc = bass.Bass("TRN2", target_bir_lowering=False)
    cx = Ctx(nc)
    dbg_out = {}

    x_in = cx.dram("x", [2048, D], F32, kind="ExternalInput")
    ctx_in = cx.dram("ctx", [256, D], F32, kind="ExternalInput")
    c2_in = cx.dram("c2", [2, D], F32, kind="ExternalInput")
    W = {k: cx.dram(k, shp, F32, kind="ExternalInput") for k, shp in WEIGHT_SPECS.items()}
    K = {k: cx.dram(k, shp, dt, kind="ExternalInput") for k, (shp, dt) in CONST_SPECS.items()}
    out_d = cx.dram("out", [2048, D], F32, kind="ExternalOutput")
    xs_d = cx.dram("xs_d", [2048, D], F32)
    cs_d = cx.dram("cs_d", [256, D], F32)
    modraw_d = cx.dram("modraw_d", [DEPTH, 2, 6 * D], F32)
    modv_d = cx.dram("modv_d", [DEPTH, 2, 2, 2, D], F32)
    hT_d = cx.dram("hT_d", [NCH // G, 128, 8, G * 128], BF16)
    hTc_d = cx.dram("hTc_d", [128, 8, G * 128], BF16)
    kv_d = cx.dram("kv_d", [NCH // G, 2, 128, G, 512], BF16)
    kvc_d = cx.dram("kvc_d", [2, 128, G, 512], BF16)
    U_d = cx.dram("U_d", [2, NCH, 128, 512], F32)
    Uc_d = cx.dram("Uc_d", [2, CTXCH, 128, 512], F32)
    S_d = cx.dram("S_d", [2, NCH, 128, 512], BF16)
    Sc_d = cx.dram("Sc_d", [2, CTXCH, 128, 512], BF16)
    st_loc = cx.dram("st_loc", [2 * 128, 512], F32)
    st_all = cx.dram("st_all", [4 * 2 * 128, 512], F32)
    f_loc = cx.dram("f_loc", [4 * 2048, 64], BF16)
    f_all = cx.dram("f_all", [4 * 4 * 2048, 64], BF16)
    zt_d = cx.dram("zt_d", [64, 8, 2048], BF16)
    wbp_d = cx.dram("wbp_d", [64, 8, D], BF16)
    cc = cx.stream("cc", k=1)

    def dbg(name, t, ap, shape, dtype=F32):
        if name not in debug:
            return
        o = cx.dram("dbg_" + name, list(shape), dtype, kind="ExternalOutput")
        cx.dma("sp", "dbg", o, o.h, t, ap)
        dbg_out[name] = (shape, dtype)

    WP = Pool([cx.sbuf(f"wp{i}", [128, 4096], BF16) for i in range(6)])
    XIN = Pool([cx.sbuf(f"xin{i}", [128, D], F32) for i in range(3)])
    JUNK = Pool([cx.sbuf(f"junk{i}", [128, D], BF16) for i in range(1)])
    TMPF = Pool([cx.sbuf(f"tmpf{i}", [128, D], F32) for i in range(2)])
    TMPH = Pool([cx.sbuf(f"tmph{i}", [128, 512], F32) for i in range(6)])
    HB = Pool([cx.sbuf(f"hb{i}", [128, D], BF16) for i in range(2)])
    TB = Pool([cx.sbuf(f"tb{i}", [128, 512], BF16) for i in range(16)])
    SFB = Pool([cx.sbuf(f"sfb{i}", [128, 2, 512], BF16) for i in range(4)])
    STAT = Pool([cx.sbuf(f"stat{i}", [128, 8], F32) for i in range(6)])
    MODA = cx.sbuf("modA", [128, D], F32)
    MODB = cx.sbuf("modB", [128, D], F32)
    MODG = cx.sbuf("modG", [128, D], F32)
    MODG2 = cx.sbuf("modG2", [128, D], F32)
    ROPE = cx.sbuf("rope", [128, 2, G, 64], F32)
    ROPE1 = cx.sbuf("rope1", [128, 2, CTXCH, 64], F32)
    ident = cx.sbuf("ident", [128, 128], BF16)
    RC = cx.sbuf("RC", [128, 6, 128], F32)
    WC = cx.sbuf("WC", [128, 2], F32)
    CMX = cx.sbuf("CMX", [128, 20], F32)
    LG = cx.sbuf("LG", [128, 8], F32)
    DT = cx.sbuf("DT", [128, 4, 128], F32)
    QF = cx.sbuf("QF", [128, 4, 128], F32)
    QB = cx.sbuf("QB", [128, 4, 128], F32)
    WFB = cx.sbuf("WFB", [128, 2, 4], F32)
    G128 = cx.sbuf("G128", [128, 8], F32)
    COEF = cx.sbuf("COEF", [128, 2, 5, 4], F32)
    epsb = cx.sbuf("epsb", [128, 2], F32)
    SCUR = [cx.sbuf(f"scur{i}", [128, 512], F32) for i in range(2)]
    SCTX = [cx.sbuf(f"sctx{i}", [128, 512], F32) for i in range(2)]
    SSTART = [cx.sbuf(f"sstart{i}", [128, 512], F32) for i in range(2)]
    c2T = cx.sbuf("c2T", [128, 8, 2], F32)
    c2Tb = cx.sbuf("c2Tb", [128, 8, 2], BF16)
    SGW = cx.sbuf("SGW", [128, 4, 128], BF16)
    SGWt = cx.sbuf("SGWt", [128, 4, 128], BF16)
    SGB = cx.sbuf("SGB", [128, 4], F32)
    SGN = cx.sbuf("SGN", [128, 256], F32)
    L1 = cx.sbuf("L1", [64, 128], BF16)
    CS64 = cx.sbuf("CS64", [64, 128], BF16)
    ZTC = cx.sbuf("ZTC", [64, 8, 256], BF16)

    PF = Pool([cx.psum(f"pf{i}", [128, 512], F32) for i in range(6)])
    PBT = Pool([cx.psum(f"pb{i}", [128, 1024], BF16) for i in range(2)])

    def O(eng, meth, writes, reads, *a, **k):
        return cx.op(eng, lambda e: getattr(e, meth)(*a, **k), reads=reads, writes=writes)

    def load(t, ap, src, sap, eng="sp", stream="ld"):
        cx.dma(eng, stream, t, ap, src, sap)

    def wload(t, ap, src, sap):
        cx.dma("pool", "w", t, ap, src, sap)

    cpy_i = [0]

    def evac(out_t, out_ap, in_t, in_ap, eng=None):
        cpy_i[0] += 1
        use_act = (cpy_i[0] % 3 != 0) if eng is None else (eng == "act")
        if use_act:
            O("act", "activation", [out_t], [in_t], out=out_ap, in_=in_ap, func=AF.Copy)
        else:
            O("dve", "tensor_copy", [out_t], [in_t], out=out_ap, in_=in_ap)

    def transpose_blocks(src_t, src_aps, dst_t, dst_ap, eng=None):
        n = len(src_aps)
        pb = PBT.get()
        for i, sap in enumerate(src_aps):
            O("pe", "transpose", [pb], [src_t, ident], out=pb[:, i * 128:(i + 1) * 128], in_=sap, identity=ident[:])
        evac(dst_t, dst_ap, pb, pb[:, 0:n * 128].rearrange("p (n c) -> p n c", n=n), eng=eng)

    for name, t in (("IDENT", ident), ("RC", RC), ("WC", WC), ("CMX", CMX), ("L1", L1), ("CS64", CS64)):
        load(t, t[:], K[name], K[name].h)
    O("dve", "memset", [epsb], [], epsb[:, 0:1], EPS)
    O("dve", "memset", [epsb], [], epsb[:, 1:2], 1.0)
    O("dve", "memset", [ROPE1], [], ROPE1[:, 0], 1.0)
    O("dve", "memset", [ROPE1], [], ROPE1[:, 1], 0.0)
    for r in range(2):
        load(c2T, c2T[:, :, r], c2_in, c2_in.h[r].rearrange("(kc p) -> p kc", p=128))
    O("act", "activation", [c2Tb], [c2T], out=c2Tb[:], in_=c2T[:], func=AF.Silu)

    def layer_setup(l):
        load(LG, LG[:], W["ret_decay_logit"], W["ret_decay_logit"].h[l:l + 1, :].broadcast_to([128, 8]))
        O("act", "activation", [LG], [LG], out=LG[:], in_=LG[:], func=AF.Exp, scale=-1.0)
        O("act", "activation", [LG], [LG, epsb], out=LG[:], in_=LG[:], func=AF.Ln, bias=epsb[:, 1:2])
        O("dve", "tensor_scalar", [LG], [LG], out=LG[:], in0=LG[:], scalar1=-1.0, scalar2=None, op0=ALU.mult)
        tA = TMPH.get()
        tB = TMPH.get()
        for h in range(4):
            O("act", "activation", [tA], [RC, LG], out=tA[:, 0:128], in_=RC[:, 0, :], func=AF.Exp, scale=LG[:, h:h + 1])
            O("act", "activation", [tB], [RC, LG], out=tB[:, 0:128], in_=RC[:, 1, :], func=AF.Exp, scale=LG[:, 4 + h:5 + h])
            O("dve", "tensor_tensor", [tA], [tA, RC], out=tA[:, 0:128], in0=tA[:, 0:128], in1=RC[:, 2, :], op=ALU.mult)
            O("dve", "tensor_tensor", [tB], [tB, RC], out=tB[:, 0:128], in0=tB[:, 0:128], in1=RC[:, 3, :], op=ALU.mult)
            O("dve", "tensor_tensor", [DT], [tA, tB], out=DT[:, h, :], in0=tA[:, 0:128], in1=tB[:, 0:128], op=ALU.add)
            O("act", "activation", [QF], [RC, LG], out=QF[:, h, :], in_=RC[:, 4, :], func=AF.Exp, scale=LG[:, h:h + 1])
            O("act", "activation", [QB], [RC, LG], out=QB[:, h, :], in_=RC[:, 5, :], func=AF.Exp, scale=LG[:, 4 + h:5 + h])
        for d in range(2):
            O("act", "activation", [WFB], [LG, WC], out=WFB[:, d, :], in_=LG[:, 4 * d:4 * d + 4], func=AF.Exp,
              scale=WC[:, d:d + 1])
        O("dve", "tensor_scalar", [WFB], [WFB], out=WFB[:], in0=WFB[:], scalar1=SK, scalar2=None, op0=ALU.mult)
        O("act", "activation", [G128], [LG], out=G128[:], in_=LG[:], func=AF.Exp, scale=128.0)
        cmv = CMX[:].rearrange("p (a d s) -> p a d s", a=2, d=2)
        for d in range(2):
            for s in range(5):
                O("act", "activation", [COEF], [LG, CMX], out=COEF[:, d, s, :], in_=LG[:, 4 * d:4 * d + 4], func=AF.Exp,
                  scale=cmv[:, 0, d, s:s + 1])
                O("dve", "tensor_scalar", [COEF], [COEF, CMX], out=COEF[:, d, s, :], in0=COEF[:, d, s, :],
                  scalar1=cmv[:, 1, d, s:s + 1], scalar2=None, op0=ALU.mult)

        wt = WP.get()
        wf = wt[:, 0:512].rearrange("p (g s) -> p g s", g=4)
        wload(wt, wf, W["sgu_w_s"], W["sgu_w_s"].h[l].rearrange("g t s -> t g s"))
        O("dve", "tensor_copy", [SGWt], [wt], out=SGWt[:], in_=wf)
        transpose_blocks(SGWt, [SGWt[:, g, :] for g in range(4)], SGW, SGW[:])
        with nc.allow_non_contiguous_dma(reason="tiny transposed bias load"):
            load(SGB, SGB[:], W["sgu_b_s"], W["sgu_b_s"].h[l].rearrange("g t -> t g"))
        load(SGN, SGN[:], W["sgu_norm"], W["sgu_norm"].h[l:l + 1, :].broadcast_to([128, 256]))

        WBR = WP.get()
        WBRv = WBR[0:64, :].rearrange("p (g d) -> p g d", g=4)
        wload(WBR, WBRv, W["w_branch_b"], W["w_branch_b"].h[l].rearrange("(g j) d -> j g d", j=64))
        for g in range(4):
            for c in range(2):
                tb = TB.get()
                tb2 = TB.get()
                for half, tt in enumerate((tb, tb2)):
                    ps = PF.get()
                    O("pe", "matmul", [ps], [CS64, WBR], ps[0:64, :], lhsT=CS64[:, c * 64:(c + 1) * 64],
                      rhs=WBRv[:, g, half * 512:(half + 1) * 512], start=True, stop=True)
                    evac(tt, tt[0:64, :], ps, ps[0:64, :])
                    cx.dma("sp", "st", wbp_d, wbp_d.h[:, g * 2 + c, half * 512:(half + 1) * 512], tt, tt[0:64, :])


    def setup_mod(l):
        for cb in range(12):
            wt = WP.get()
            wv = wt[:].rearrange("p (kc n) -> p kc n", kc=8)
            wload(wt, wv, W["w_mod"], W["w_mod"].h[l, :, cb * 512:(cb + 1) * 512].rearrange("(kc p) n -> p kc n", p=128))
            bch = TMPH.get()
            load(bch, bch[0:2, :], W["b_mod"], W["b_mod"].h[l:l + 1, cb * 512:(cb + 1) * 512].broadcast_to([2, 512]))
            ps = PF.get()
            for kc in range(8):
                O("pe", "matmul", [ps], [c2Tb, wt], ps[0:2, :], lhsT=c2Tb[:, kc, :], rhs=wv[:, kc, :],
                  start=(kc == 0), stop=(kc == 7))
            O("dve", "tensor_tensor", [bch], [ps, bch], out=bch[0:2, :], in0=ps[0:2, :], in1=bch[0:2, :], op=ALU.add)
            cx.dma("sp", "st", modraw_d, modraw_d.h[l][:, cb * 512:(cb + 1) * 512], bch, bch[0:2, :])

        setup_modv(l)

    def setup_modv(l):
        def bc(t, ap, src, sap):
            load(t, ap, src, sap.broadcast_to([128, D]))
        for row in range(2):
            for which in range(2):
                o = 3 * which
                gpre = W["g_pre_mix"] if which == 0 else W["g_pre_mlp"]
                gpost = W["g_post_mix"] if which == 0 else W["g_post_mlp"]
                ta = TMPF.get()
                t1 = TMPF.get()
                bc(ta, ta[:], modraw_d, modraw_d.h[l][row:row + 1, (o + 1) * D:(o + 2) * D])
                bc(t1, t1[:], gpre, gpre.h[l:l + 1, :])
                O("dve", "scalar_tensor_tensor", [ta], [ta, t1], out=ta[:], in0=ta[:], scalar=1.0, in1=t1[:], op0=ALU.add, op1=ALU.mult)
                cx.dma("sp", "st", modv_d, modv_d.h[l][row, which, 0:1, :], ta, ta[0:1, :])
                tg = TMPF.get()
                t2 = TMPF.get()
                bc(tg, tg[:], modraw_d, modraw_d.h[l][row:row + 1, (o + 2) * D:(o + 3) * D])
                bc(t2, t2[:], gpost, gpost.h[l:l + 1, :])
                O("dve", "tensor_tensor", [tg], [tg, t2], out=tg[:], in0=tg[:], in1=t2[:], op=ALU.mult)
                cx.dma("sp", "st", modv_d, modv_d.h[l][row, which, 1:2, :], tg, tg[0:1, :])

    def load_vec(t, row, which, kind, l):
        if kind == "B":
            src, sap = modraw_d, modraw_d.h[l][row:row + 1, (3 * which) * D:(3 * which + 1) * D]
        else:
            i = 0 if kind == "A" else 1
            src, sap = modv_d, modv_d.h[l][row, which, i:i + 1, :]
        load(t, t[:], src, sap.broadcast_to([128, D]))

    def g_rms_rstd(src_t, src_aps, n, st):
        srcs = src_t if isinstance(src_t, (list, tuple)) else [src_t] * len(src_aps)
        O("dve", "memset", [st], [], st[:], 0.0)
        yield
        single = len(src_aps) == 1
        for i, sap in enumerate(src_aps):
            junk = JUNK.get()
            jv = junk[:, 0:sap.shape[-1]]
            col = 0 if single else 4 + i
            O("act", "activation", [junk, st], [srcs[i], st], out=jv, in_=sap, func=AF.Square, accum_out=st[:, col:col + 1])
            yield
        if not single:
            O("dve", "tensor_tensor", [st], [st], out=st[:, 0:1], in0=st[:, 4:5], in1=st[:, 5:6], op=ALU.add)
            yield
        O("act", "activation", [st], [st, epsb], out=st[:, 1:2], in_=st[:, 0:1], func=AF.Sqrt, scale=1.0 / n, bias=epsb[:, 0:1])
        yield
        O("dve", "reciprocal", [st], [st], out=st[:, 2:3], in_=st[:, 1:2])
        yield

    def run(gen):
        for _ in gen:
            pass

    def interleave(gens):
        gens = list(gens)
        while gens:
            for g in list(gens):
                try:
                    next(g)
                except StopIteration:
                    gens.remove(g)

    def rms_rstd(src_t, src_aps, n, st):
        run(g_rms_rstd(src_t, src_aps, n, st))

    def g_norm_mod_T(xt, dstT, ci, out=None):
        st = STAT.get()
        yield from g_rms_rstd(xt, [xt[:]], D, st)
        tmp = TMPF.get()
        O("dve", "scalar_tensor_tensor", [tmp], [xt, st, MODA], out=tmp[:], in0=xt[:], scalar=st[:, 2:3], in1=MODA[:],
          op0=ALU.mult, op1=ALU.mult)
        yield
        hb = HB.get()
        O("dve", "tensor_tensor", [hb], [tmp, MODB], out=hb[:], in0=tmp[:], in1=MODB[:], op=ALU.add)
        if out is not None:
            out.append(hb)
        yield
        transpose_blocks(hb, [hb[:, kc * 128:(kc + 1) * 128] for kc in range(8)], dstT, dstT[:, :, ci * 128:(ci + 1) * 128])
        yield

    def norm_mod_T(xt, dstT, ci, add_eng="dve"):
        out = []
        run(g_norm_mod_T(xt, dstT, ci, out))
        return out[0]

    def zblock(hT, ci, wv, ncols, col0=0):
        ps = PF.get()
        for kc in range(8):
            O("pe", "matmul", [ps], [hT, wv.tile], ps[:, 0:ncols], lhsT=hT[:, kc, ci * 128:(ci + 1) * 128],
              rhs=wv.ap[:, kc, col0:col0 + ncols], start=(kc == 0), stop=(kc == 7))
        return ps

    class WV:
        def __init__(self, tile, ap):
            self.tile = tile
            self.ap = ap

    def load_win(l, c0, ncols):
        wt = WP.get()
        wv = wt[:, 0:8 * ncols].rearrange("p (kc n) -> p kc n", kc=8)
        wload(wt, wv, W["w_in"], W["w_in"].h[l, :, c0:c0 + ncols].rearrange("(kc p) n -> p kc n", p=128))
        return WV(wt, wv)

    def rope(ps, ropet, ci, dst_t, dst_ap, comb_eng="dve"):
        pv = ps[:].rearrange("p (h t i) -> p h t i", h=4, t=2)
        cosb = ropet[:, 0, ci, :].unsqueeze(1).broadcast_to([128, 4, 64])
        sinb = ropet[:, 1, ci, :].unsqueeze(1).broadcast_to([128, 4, 64])
        t1 = TMPH.get()
        t2 = TMPH.get()
        t1v = t1[:].rearrange("p (h t i) -> p h t i", h=4, t=2)
        t2v = t2[:].rearrange("p (h t i) -> p h t i", h=4, t=2)
        dv = dst_ap.rearrange("p (h t i) -> p h t i", h=4, t=2)
        O("dve", "tensor_tensor", [t1], [ps, ropet], out=t1v[:, :, 0, :], in0=pv[:, :, 0, :], in1=cosb, op=ALU.mult)
        O("dve", "tensor_tensor", [t1], [ps, ropet], out=t1v[:, :, 1, :], in0=pv[:, :, 1, :], in1=cosb, op=ALU.mult)
        O("dve", "tensor_tensor", [t2], [ps, ropet], out=t2v[:, :, 0, :], in0=pv[:, :, 1, :], in1=sinb, op=ALU.mult)
        O("dve", "tensor_tensor", [t2], [ps, ropet], out=t2v[:, :, 1, :], in0=pv[:, :, 0, :], in1=sinb, op=ALU.mult)
        O(comb_eng, "tensor_tensor", [dst_t], [t1, t2], out=dv[:, :, 0, :], in0=t1v[:, :, 0, :], in1=t2v[:, :, 0, :], op=ALU.subtract)
        O(comb_eng, "tensor_tensor", [dst_t], [t1, t2], out=dv[:, :, 1, :], in0=t1v[:, :, 1, :], in1=t2v[:, :, 1, :], op=ALU.add)

    def alloc_passA(es):
        return dict(hT=cx.sbuf("hT", [128, 8, G * 128], BF16, stack=es), k_tok=cx.sbuf("k_tok", [128, G, 512], BF16, stack=es),
                    vf=cx.sbuf("vf", [128, G, 512], BF16, stack=es), vb=cx.sbuf("vb", [128, G, 512], BF16, stack=es),
                    f_tok=cx.sbuf("f_tok", [128, G, 256], BF16, stack=es), v_pl=cx.sbuf("v_pl", [128, G, 512], BF16, stack=es))

    def pass_A(l, grp, bufs=None):
        nch, xsrc, x_ap, is_ctx = grp["nch"], grp["xsrc"], grp["x_ap"], grp["is_ctx"]
        ropet = ROPE1 if is_ctx else ROPE
        es = None
        if bufs is None:
            es = ExitStack()
            bufs = alloc_passA(es)
        hT, k_tok, vf, vb, f_tok, v_pl = bufs["hT"], bufs["k_tok"], bufs["vf"], bufs["vb"], bufs["f_tok"], bufs["v_pl"]
        kvd_ap = kvc_d.h if is_ctx else kv_d.h[grp["c0"] // G]
        kvd = kvc_d if is_ctx else kv_d
        k3, vf3, vb3, f3 = k_tok[:], vf[:], vb[:], f_tok[:]
        row = 1 if is_ctx else 0
        load_vec(MODA, row, 0, "A", l)
        load_vec(MODB, row, 0, "B", l)
        if not is_ctx:
            c0 = grp["c0"]
            for i, nm in enumerate(("rope_cos", "rope_sin")):
                load(ROPE, ROPE[:, i], K[nm], K[nm].h[c0 * 128:(c0 + nch) * 128, :].rearrange("(c p) i -> p c i", p=128))
        for c2 in range(0, nch, 2):
            gens = []
            for ci in range(c2, min(c2 + 2, nch)):
                xt = XIN.get()
                load(xt, xt[:], xsrc, x_ap(ci))
                gens.append(g_norm_mod_T(xt, hT, ci))
            interleave(gens)
        hTd = hTc_d if is_ctx else hT_d
        hTd_ap = hTc_d.h if is_ctx else hT_d.h[grp["c0"] // G]
        cx.dma("sp", "st", hTd, hTd_ap[:, :, 0:nch * 128], hT, hT[:, :, 0:nch * 128])
        wv = load_win(l, 512, 512)
        for ci in range(nch):
            ps = zblock(hT, ci, wv, 512)
            rope(ps, ropet, ci, k_tok, k3[:, ci, :])
        cx.dma("sp", "st", kvd, kvd_ap[0, :, 0:nch, :], k_tok, k3[:, 0:nch, :])
        wv = load_win(l, 1024, 512)
        for ci in range(nch):
            ps = zblock(hT, ci, wv, 512)
            pv = ps[:].rearrange("p (h e) -> p h e", h=4)
            O("dve", "tensor_tensor", [vf], [ps, WFB], out=vf3[:, ci, :].rearrange("p (h e) -> p h e", h=4), in0=pv,
              in1=WFB[:, 0, :].unsqueeze(2).broadcast_to([128, 4, 128]), op=ALU.mult)
            O("dve", "tensor_tensor", [vb], [ps, WFB], out=vb3[:, ci, :].rearrange("p (h e) -> p h e", h=4), in0=pv,
              in1=WFB[:, 1, :].unsqueeze(2).broadcast_to([128, 4, 128]), op=ALU.mult)
            O("dve", "tensor_copy", [v_pl], [ps], out=v_pl[:, ci, :], in_=ps[:])
        cx.dma("sp", "st", kvd, kvd_ap[1, :, 0:nch, :], v_pl, v_pl[:, 0:nch, :])
        wv = load_win(l, 2048, 256)
        for ci in range(nch):
            ps = zblock(hT, ci, wv, 256)
            evac(f_tok, f3[:, ci, :], ps, ps[:, 0:256])
            if not is_ctx:
                tok0 = (grp["c0"] + ci) * 128
                with nc.allow_non_contiguous_dma(reason="quarter-major f layout for the Fourier exchange"):
                    cx.dma("sp", "st", f_loc,
                           f_loc.h.rearrange("(q t) c -> t q c", q=4)[tok0:tok0 + 128, :, :], f_tok,
                           f3[:, ci, :].rearrange("p (q c) -> p q c", q=4))
        Ud = Uc_d if is_ctx else U_d
        for ci in range(nch):
            gci = ci if is_ctx else grp["c0"] + ci
            for d, vw in enumerate((vf3, vb3)):
                ps = PF.get()
                for h in range(4):
                    O("pe", "matmul", [ps], [k_tok, vf if d == 0 else vb], ps[:, h * 128:(h + 1) * 128],
                      lhsT=k3[:, ci, h * 128:(h + 1) * 128], rhs=vw[:, ci, h * 128:(h + 1) * 128], start=True, stop=True)
                tmp = TMPH.get()
                evac(tmp, tmp[:], ps, ps[:])
                cx.dma("sp", "st", Ud, Ud.h[d, gci], tmp, tmp[:])
                if not is_ctx:
                    pw = STAT.get()
                    O("act", "activation", [pw], [LG], out=pw[:, 0:4], in_=LG[:, 4 * d:4 * d + 4], func=AF.Exp,
                      scale=float(128 * ((NCH - 1 - gci) if d == 0 else gci)))
                    t2 = TMPH.get()
                    O("dve", "tensor_tensor", [t2], [tmp, pw], out=t2[:].rearrange("p (h e) -> p h e", h=4),
                      in0=tmp[:].rearrange("p (h e) -> p h e", h=4), in1=pw[:, 0:4].unsqueeze(2).broadcast_to([128, 4, 128]), op=ALU.mult)
                    if gci == 0:
                        O("dve", "tensor_copy", [SCUR[d]], [t2], out=SCUR[d][:], in_=t2[:])
                    else:
                        O("dve", "tensor_tensor", [SCUR[d]], [SCUR[d], t2], out=SCUR[d][:], in0=SCUR[d][:], in1=t2[:], op=ALU.add)
        if is_ctx:
            dbg("kctx_l%d" % l, k_tok, k3[:, 0, :], [128, 512], BF16)
            if not grp["last"]:
                fourier_ctx(f_tok)
        else:
            if grp["c0"] == 0:
                dbg("k0_l%d" % l, k_tok, k3[:, 0, :], [128, 512], BF16)
        if es is not None:
            cx.barrier()
            es.close()

    def decay_mul(S, d):
        sv = S[:].rearrange("p (h e) -> p h e", h=4)
        O("dve", "tensor_tensor", [S], [S, G128], out=sv, in0=sv,
          in1=G128[:, 4 * d:4 * d + 4].unsqueeze(2).broadcast_to([128, 4, 128]), op=ALU.mult)

    def recur_steps(Ud, Sd, nch, start, store):
        steps = []

        def init(d):
            S = SCUR[d]
            if start is None:
                O("dve", "memset", [S], [], S[:], 0.0)
            else:
                O("dve", "tensor_copy", [S], [start[d]], out=S[:], in_=start[d][:])

        def step(d, ci):
            S = SCUR[d]
            if store:
                sb = TB.get()
                O("dve", "tensor_copy", [sb], [S], out=sb[:], in_=S[:])
                cx.dma("sp", "st", Sd, Sd.h[d, ci], sb, sb[:])
            u = TMPH.get()
            load(u, u[:], Ud, Ud.h[d, ci])
            decay_mul(S, d)
            O("dve", "tensor_tensor", [S], [S, u], out=S[:], in0=S[:], in1=u[:], op=ALU.add)

        steps.append(lambda: (init(0), init(1)))
        for k in range(nch):
            steps.append(lambda k=k: step(0, k))
            steps.append(lambda k=k: step(1, nch - 1 - k))
        return steps

    def recur(Ud, Sd, nch, start, store):
        for f in recur_steps(Ud, Sd, nch, start, store):
            f()

    def exchange_states():
        for d in range(2):
            cx.dma("sp", "st", st_loc, st_loc.h[d * 128:(d + 1) * 128, :], SCUR[d], SCUR[d][:])
        cx.custom("pool", lambda e: e.collective_compute("AllGather", ALU.bypass, replica_groups=[[0, 1, 2, 3], [4, 5, 6, 7]],
                                                         ins=[st_loc.h.opt()], outs=[st_all.h.opt()]),
                  reads=[st_loc], writes=[st_all], st=cc, inc=1)
        for d in range(2):
            S = SSTART[d]
            sv = S[:].rearrange("p (h e) -> p h e", h=4)
            O("dve", "tensor_tensor", [S], [SCTX[d], COEF], out=sv, in0=SCTX[d][:].rearrange("p (h e) -> p h e", h=4),
              in1=COEF[:, d, 4, :].unsqueeze(2).broadcast_to([128, 4, 128]), op=ALU.mult)
            for r in range(4):
                u = TMPH.get()
                load(u, u[:], st_all, st_all.h[(r * 2 + d) * 128:(r * 2 + d + 1) * 128, :])
                O("dve", "tensor_tensor", [u], [u, COEF], out=u[:].rearrange("p (h e) -> p h e", h=4),
                  in0=u[:].rearrange("p (h e) -> p h e", h=4),
                  in1=COEF[:, d, r, :].unsqueeze(2).broadcast_to([128, 4, 128]), op=ALU.mult)
                O("dve", "tensor_tensor", [S], [S, u], out=S[:], in0=S[:], in1=u[:], op=ALU.add)

    def fourier_ctx(f_tok):
        f3 = f_tok[:]
        dt_ = WP.get()
        D256 = dt_[:, 0:1024].rearrange("p (n k) -> p n k", n=2)
        load(dt_, D256, K["D256"], K["D256"].h)
        for q in range(4):
            ps = PF.get()
            for nchk in range(2):
                O("pe", "matmul", [ps], [f_tok, dt_], ps[0:64, :], lhsT=f3[:, nchk, q * 64:(q + 1) * 64],
                  rhs=D256[:, nchk, :], start=(nchk == 0), stop=(nchk == 1))
            evac(ZTC, ZTC[:, 2 * q:2 * q + 2, :], ps, ps[0:64, :].rearrange("p (c k) -> p c k", c=2))

    def fourier_gather():
        cx.custom("pool", lambda e: e.collective_compute("AllGather", ALU.bypass, replica_groups=[[0, 1, 2, 3], [4, 5, 6, 7]],
                                                         ins=[f_loc.h.opt()], outs=[f_all.h.opt()]),
                  reads=[f_loc], writes=[f_all], st=cc, inc=1)

    def fourier_latent(side_steps=None):
        fav = f_all.h.rearrange("(r q a b) c -> r q a (b c)", r=4, q=4, a=16)
        cx.barrier(pool=True)
        es = ExitStack()
        X_t = cx.sbuf("fX", [64, 8192], BF16, stack=es)
        TT_t = cx.sbuf("fTT", [128, 8192], BF16, stack=es)
        E3 = cx.sbuf("E3", [128, 64, 96], BF16, stack=es)
        load(E3, E3[:], K["E3"], K["E3"].h, eng="pool", stream="fld")
        X, TT = X_t[:], TT_t[:]
        for q in range(4):
            for r in range(4):
                load(X_t, X[r * 16:(r + 1) * 16, 0:128 * 64], f_all, fav[r, q], eng="pool", stream="fld")
            Xv = X[0:64, :].rearrange("p (b c) -> p c b", c=64)
            TTv = TT[:, 0:64 * 128].rearrange("p (c k) -> p c k", c=64)
            for cg in range(16):
                if side_steps and cg % 2 == 0:
                    side_steps.pop(0)()
                ps = PF.get()
                for i in range(4):
                    O("pe", "matmul", [ps], [L1, X_t], ps[:, i * 128:(i + 1) * 128], lhsT=Xv[:, cg * 4 + i, :], rhs=L1[:], start=True, stop=True)
                evac(TT_t, TTv[:, cg * 4:(cg + 1) * 4, :], ps, ps[:].rearrange("p (c k) -> p c k", c=4), eng="act")
            zq = WP.get()
            zqv = zq[0:64, :].rearrange("p (c kh kl) -> p c kh kl", c=2, kl=64)
            for kg in range(8):
                ps = PF.get()
                for i in range(8):
                    klo = kg * 8 + i
                    O("pe", "matmul", [ps], [TT_t, E3], ps[0:64, i * 64:(i + 1) * 64], lhsT=TTv[:, :, klo], rhs=E3[:, klo, 32:96],
                      start=True, stop=False)
                    O("pe", "matmul", [ps], [TT_t, E3], ps[0:64, i * 64:(i + 1) * 64], lhsT=TTv[:, :, 64 + klo], rhs=E3[:, klo, 0:64],
                      start=False, stop=True)
                psv = ps[0:64, :].rearrange("p (kl c kh) -> p c kh kl", kl=8, c=2)
                for c in range(2):
                    evac(zq, zqv[:, c, :, kg * 8:(kg + 1) * 8], ps, psv[:, c, :, :], eng="act")
            cx.dma("pool", "fst", zt_d, zt_d.h[:, 2 * q:2 * q + 2, :], zq, zq[0:64, :].rearrange("p (c t) -> p c t", c=2))
        return es

    def g_post_norm_residual(xt, ysrc_t, y_aps, st, gt=None):
        srcs = ysrc_t if isinstance(ysrc_t, (list, tuple)) else [ysrc_t] * len(y_aps)
        yield from g_rms_rstd(srcs, y_aps, D, st)
        gtt = gt or MODG
        for i, yap in enumerate(y_aps):
            tmp = TMPH.get()
            O("dve", "scalar_tensor_tensor", [tmp], [srcs[i], st, gtt], out=tmp[:], in0=yap, scalar=st[:, 2:3],
              in1=gtt[:, i * 512:(i + 1) * 512], op0=ALU.mult, op1=ALU.mult)
            yield
            O("dve", "tensor_tensor", [xt], [tmp, xt], out=xt[:, i * 512:(i + 1) * 512], in0=tmp[:],
              in1=xt[:, i * 512:(i + 1) * 512], op=ALU.add)
            yield

    def post_norm_residual(xt, ysrc_t, y_aps, st, gt=None):
        run(g_post_norm_residual(xt, ysrc_t, y_aps, st, gt))

    def alloc_passB(es):
        return dict(mT=cx.sbuf("mT", [128, 8, G * 128], BF16, stack=es), hT=cx.sbuf("hT", [128, 8, G * 128], BF16, stack=es),
                    big16=cx.sbuf("big16", [128, G * D], F32, stack=es), r4=cx.sbuf("r4", [128, 4, G * 128], BF16, stack=es),
                    sguT=cx.sbuf("sguT", [128, 2, G * 128], BF16, stack=es), ZTL=cx.sbuf("ZTL", [64, 8, G * 128], BF16, stack=es))

    def pass_B(l, grp, last, bufs, prev_fin=None):
        nch, xsrc, x_ap, is_ctx = grp["nch"], grp["xsrc"], grp["x_ap"], grp["is_ctx"]
        xdst, xd_ap = grp["xdst"], grp["xd_ap"]
        xmid, xm_ap = grp["xmid"], grp["xm_ap"]
        T = nch * 128
        ropet = ROPE1 if is_ctx else ROPE
        Sd = Sc_d if is_ctx else S_d
        mT, hT, big16, r4, sguT, ZTL = bufs["mT"], bufs["hT"], bufs["big16"], bufs["r4"], bufs["sguT"], bufs["ZTL"]
        qT = kT = v_tok = gs = big16
        retT = aT = r4
        B16 = big16[:].bitcast(BF16)
        qT3 = B16[:, 0:2048].rearrange("p (h t) -> p h t", h=4)
        kT3 = B16[:, 2048:4096].rearrange("p (h t) -> p h t", h=4)
        v3 = B16[:, 4096:6144].rearrange("p (c n) -> p c n", c=G)
        gs3 = B16[:, 6144:8192].rearrange("p (c n) -> p c n", c=G)
        y2v = big16[:].rearrange("p (c d) -> p c d", c=G)
        retT3, sguT3 = r4[:], sguT[:]
        tag = "l%d_%s" % (l, "c" if is_ctx else "x%d" % grp["c0"])
        row = 1 if is_ctx else 0

        def prefetch_inputs(g):
            gctx = g["is_ctx"]
            Tg = g["nch"] * 128
            hTd = hTc_d if gctx else hT_d
            hTd_ap = hTc_d.h if gctx else hT_d.h[g["c0"] // G]
            load(hT, hT[:, :, 0:Tg], hTd, hTd_ap[:, :, 0:Tg])
            if not gctx:
                c0g = g["c0"]
                for i, nm in enumerate(("rope_cos", "rope_sin")):
                    load(ROPE, ROPE[:, i], K[nm], K[nm].h[c0g * 128:(c0g + g["nch"]) * 128, :].rearrange("(c p) i -> p c i", p=128))

        if not grp.get("prefetched"):
            prefetch_inputs(grp)
        if grp.get("load_mod", True):
            load_vec(MODG, row, 0, "G", l)
            load_vec(MODA, row, 1, "A", l)
            load_vec(MODB, row, 1, "B", l)
            load_vec(MODG2, row, 1, "G", l)
        wv = load_win(l, 2304, 512)

        def g_uvs_b(ci, ps):
            gu = TMPH.get()
            tt = TMPH.get()
            O("act", "activation", [gu], [ps], out=gu[:], in_=ps[:], func=AF.Gelu_apprx_tanh)
            yield
            st = STAT.get()
            O("act", "activation", [tt], [gu], out=tt[:, 0:256], in_=gu[:, 256:512], func=AF.Square)
            yield
            O("dve", "tensor_reduce", [st], [tt], out=st[:, 0:4], in_=tt[:, 0:256].rearrange("p (g c) -> p g c", g=4),
              axis=mybir.AxisListType.X, op=ALU.add)
            yield
            O("act", "activation", [st], [st, epsb], out=st[:, 0:4], in_=st[:, 0:4], func=AF.Sqrt, scale=1.0 / 64, bias=epsb[:, 0:1])
            yield
            O("dve", "reciprocal", [st], [st], out=st[:, 4:8], in_=st[:, 0:4])
            yield
            O("dve", "tensor_tensor", [tt], [gu, st], out=tt[:, 0:256].rearrange("p (g c) -> p g c", g=4),
              in0=gu[:, 256:512].rearrange("p (g c) -> p g c", g=4), in1=st[:, 4:8].unsqueeze(2).broadcast_to([128, 4, 64]), op=ALU.mult)
            yield
            vnb = TB.get()
            O("dve", "tensor_tensor", [vnb], [tt, SGN], out=vnb[:, 0:256], in0=tt[:, 0:256], in1=SGN[:], op=ALU.mult)
            yield
            ps2 = PF.get()
            for g in range(4):
                O("pe", "matmul", [ps2], [SGW, vnb], ps2[:, g * 64:(g + 1) * 64], lhsT=SGW[:, g, :], rhs=vnb[:, g * 64:(g + 1) * 64],
                  start=True, stop=True)
            O("dve", "tensor_tensor", [tt], [ps2, SGB], out=tt[:, 256:512].rearrange("p (g c) -> p g c", g=4),
              in0=ps2[:, 0:256].rearrange("p (g c) -> p g c", g=4), in1=SGB[:].unsqueeze(2).broadcast_to([128, 4, 64]), op=ALU.add)
            yield
            sgb = TB.get()
            O("dve", "tensor_tensor", [sgb], [tt, gu], out=sgb[:, 0:256], in0=tt[:, 256:512], in1=gu[:, 0:256], op=ALU.mult)
            if ci == 0:
                dbg("sgu_" + tag, sgb, sgb[:, 0:256], [128, 256], BF16)
            yield
            transpose_blocks(sgb, [sgb[:, h * 128:(h + 1) * 128] for h in range(2)], sguT, sguT3[:, :, ci * 128:(ci + 1) * 128])
            yield

        prev_fin = list(prev_fin or [])
        for c2 in range(0, nch, 2):
            cis = list(range(c2, min(c2 + 2, nch)))
            pss = [zblock(hT, ci, wv, 512) for ci in cis]
            interleave([g_uvs_b(ci, ps) for ci, ps in zip(cis, pss)])
            if prev_fin:
                prev_fin.pop(0)()
        while prev_fin:
            prev_fin.pop(0)()
        wv = load_win(l, 0, 512)

        def q_b(ci, ps):
            tb = TB.get()
            rope(ps, ropet, ci, tb, tb[:])
            transpose_blocks(tb, [tb[:, h * 128:(h + 1) * 128] for h in range(4)], qT, qT3[:, :, ci * 128:(ci + 1) * 128])
        pend = zblock(hT, 0, wv, 512)
        for ci in range(nch):
            nxt = zblock(hT, ci + 1, wv, 512) if ci + 1 < nch else None
            q_b(ci, pend)
            pend = nxt
        kvd_ap = kvc_d.h if is_ctx else kv_d.h[grp["c0"] // G]
        kvd = kvc_d if is_ctx else kv_d
        load(r4, r4[:].rearrange("p h t -> p (h t)").rearrange("p (c n) -> p c n", c=G)[:, 0:nch, :], kvd, kvd_ap[0, :, 0:nch, :])
        load(big16, v3[:, 0:nch, :], kvd, kvd_ap[1, :, 0:nch, :])
        kst = r4[:].rearrange("p h t -> p (h t)").rearrange("p (c n) -> p c n", c=G)
        for ci in range(nch):
            transpose_blocks(r4, [kst[:, ci, h * 128:(h + 1) * 128] for h in range(4)], kT, kT3[:, :, ci * 128:(ci + 1) * 128])
        wv = load_win(l, 1536, 512)
        for ci in range(nch):
            ps = zblock(hT, ci, wv, 512)
            O("act", "activation", [gs], [ps], out=gs3[:, ci, :], in_=ps[:], func=AF.Silu)
        def b3_s1(ci):
            gci = ci if is_ctx else grp["c0"] + ci
            sl = slice(ci * 128, (ci + 1) * 128)
            sfb = SFB.get()
            load(sfb, sfb[:], Sd, Sd.h[:, gci].rearrange("d p n -> p d n"))
            Sf = Sb = sfb
            ps = PF.get()
            for h in range(4):
                O("pe", "matmul", [ps], [kT, qT], ps[:, h * 128:(h + 1) * 128], lhsT=kT3[:, h, sl], rhs=qT3[:, h, sl], start=True, stop=True)
            scm = TB.get()
            O("dve", "tensor_tensor", [scm], [ps, DT], out=scm[:], in0=ps[:], in1=DT[:].rearrange("p h c -> p (h c)"), op=ALU.mult)
            qf = TB.get()
            qb = TB.get()
            O("dve", "tensor_tensor", [qf], [qT, QF], out=qf[:].rearrange("p (h c) -> p h c", h=4), in0=qT3[:, :, sl], in1=QF[:], op=ALU.mult)
            O("dve", "tensor_tensor", [qb], [qT, QB], out=qb[:].rearrange("p (h c) -> p h c", h=4), in0=qT3[:, :, sl], in1=QB[:], op=ALU.mult)
            return Sf, Sb, scm, qf, qb

        def g_b3_s2(ci, Sf, Sb, scm, qf, qb):
            sl = slice(ci * 128, (ci + 1) * 128)
            po = PF.get()
            for h in range(4):
                hs = slice(h * 128, (h + 1) * 128)
                O("pe", "matmul", [po], [scm, v_tok], po[:, hs], lhsT=scm[:, hs], rhs=v3[:, ci, hs], start=True, stop=False)
                O("pe", "matmul", [po], [qf, Sf], po[:, hs], lhsT=qf[:, hs], rhs=Sf[:, 0, hs], start=False, stop=False)
                O("pe", "matmul", [po], [qb, Sb], po[:, hs], lhsT=qb[:, hs], rhs=Sb[:, 1, hs], start=False, stop=True)
            st = STAT.get()
            O("dve", "memset", [st], [], st[:], 0.0)
            yield
            for h in range(4):
                junk = JUNK.get()
                O("act", "activation", [junk, st], [po, st], out=junk[:, 0:128], in_=po[:, h * 128:(h + 1) * 128], func=AF.Square,
                  accum_out=st[:, h:h + 1])
            yield
            O("act", "activation", [st], [st, epsb], out=st[:, 0:4], in_=st[:, 0:4], func=AF.Sqrt, scale=1.0 / 128, bias=epsb[:, 0:1])
            yield
            O("dve", "reciprocal", [st], [st], out=st[:, 4:8], in_=st[:, 0:4])
            yield
            rt = TB.get()
            for h in range(4):
                hs = slice(h * 128, (h + 1) * 128)
                O("dve", "scalar_tensor_tensor", [rt], [po, st, gs], out=rt[:, hs], in0=po[:, hs], scalar=st[:, 4 + h:5 + h],
                  in1=gs3[:, ci, hs], op0=ALU.mult, op1=ALU.mult)
            if ci == 0:
                dbg("ret_" + tag, rt, rt[:], [128, 512], BF16)
            yield
            transpose_blocks(rt, [rt[:, h * 128:(h + 1) * 128] for h in range(4)], retT, retT3[:, :, sl])
            yield

        s1 = [b3_s1(ci) for ci in range(nch)]
        for c2 in range(0, nch, 2):
            interleave([g_b3_s2(ci, *s1[ci]) for ci in range(c2, min(c2 + 2, nch))])
        if is_ctx:
            ZT_t, ZT3 = ZTC, ZTC[:]
        else:
            ZT_t = ZTL
            ZT3 = ZTL[:]
            load(ZTL, ZT3, zt_d, zt_d.h[:, :, grp["c0"] * 128:grp["c0"] * 128 + T])
        TBW = min(T, 512)
        ntb = T // TBW
        for db in range(8):
            j = db % 2
            if j == 0:
                dbp = db // 2
                wgA = WP.get()
                wgAv = wgA[:].rearrange("p (b kc n) -> p b kc n", b=2, kc=8)
                for br in range(2):
                    c0w = 2816 + br * 1024 + dbp * 256
                    wload(wgA, wgAv[:, br], W["w_in"], W["w_in"].h[l, :, c0w:c0w + 256].rearrange("(kc p) n -> p kc n", p=128))
                wgB = WP.get()
                wg2v = wgB[:, 0:2048].rearrange("p (kc n) -> p kc n", kc=8)
                wa2v = wgB[:, 2048:3072].rearrange("p (kc n) -> p kc n", kc=4)
                wc2v = wgB[:, 3072:3584].rearrange("p (kc n) -> p kc n", kc=2)
                c0w = 2816 + 2 * 1024 + dbp * 256
                wload(wgB, wg2v, W["w_in"], W["w_in"].h[l, :, c0w:c0w + 256].rearrange("(kc p) n -> p kc n", p=128))
                wload(wgB, wa2v, W["w_branch_a"], W["w_branch_a"].h[l, :, dbp * 256:(dbp + 1) * 256].rearrange("(kc p) n -> p kc n", p=128))
                wload(wgB, wc2v, W["w_branch_c"], W["w_branch_c"].h[l, :, dbp * 256:(dbp + 1) * 256].rearrange("(kc p) n -> p kc n", p=128))
                wgC = WP.get()
                wbp2v = wgC[0:64, 0:2048].rearrange("p (kc n) -> p kc n", kc=8)
                load(wgC, wbp2v, wbp_d, wbp_d.h[:, :, dbp * 256:(dbp + 1) * 256], eng="pool", stream="w")
            js = slice(j * 128, (j + 1) * 128)
            gate_t = [wgA, wgA, wgB]
            gate_v = [wgAv[:, 0, :, js], wgAv[:, 1, :, js], wg2v[:, :, js]]
            wa_v, wc_v, wbp_v = wa2v[:, :, js], wc2v[:, :, js], wbp2v[:, :, js]
            for tbi in range(ntb):
                ts = slice(tbi * TBW, (tbi + 1) * TBW)
                acc = TMPH.get()
                for br in range(3):
                    pg = PF.get()
                    for kc in range(8):
                        O("pe", "matmul", [pg], [gate_t[br], hT], pg[:, 0:TBW], lhsT=gate_v[br][:, kc, :], rhs=hT[:, kc, ts], start=(kc == 0), stop=(kc == 7))
                    pb = PF.get()
                    if br == 0:
                        for kc in range(4):
                            O("pe", "matmul", [pb], [wgB, retT], pb[:, 0:TBW], lhsT=wa_v[:, kc, :], rhs=retT3[:, kc, ts], start=(kc == 0), stop=(kc == 3))
                    elif br == 1:
                        for kc in range(8):
                            O("pe", "matmul", [pb], [wgC, ZT_t], pb[:, 0:TBW], lhsT=wbp_v[:, kc, :], rhs=ZT3[:, kc, ts], start=(kc == 0), stop=(kc == 7))
                    else:
                        for kc in range(2):
                            O("pe", "matmul", [pb], [wgC if False else wgB, sguT], pb[:, 0:TBW], lhsT=wc_v[:, kc, :], rhs=sguT3[:, kc, ts], start=(kc == 0), stop=(kc == 1))
                    sg = TMPH.get()
                    O("act", "activation", [sg], [pg], out=sg[:, 0:TBW], in_=pg[:, 0:TBW], func=AF.Sigmoid)
                    if br == 0:
                        O("dve", "tensor_tensor", [acc], [sg, pb], out=acc[:, 0:TBW], in0=sg[:, 0:TBW], in1=pb[:, 0:TBW], op=ALU.mult)
                    else:
                        O("dve", "tensor_tensor", [sg], [sg, pb], out=sg[:, 0:TBW], in0=sg[:, 0:TBW], in1=pb[:, 0:TBW], op=ALU.mult)
                        if br == 1:
                            O("dve", "tensor_tensor", [acc], [acc, sg], out=acc[:, 0:TBW], in0=acc[:, 0:TBW], in1=sg[:, 0:TBW], op=ALU.add)
                        else:
                            O("dve", "tensor_tensor", [mT], [acc, sg], out=mT[:, db, ts], in0=acc[:, 0:TBW], in1=sg[:, 0:TBW], op=ALU.add)
        dbg("mT_" + tag, mT, mT[:, :, 0:128], [128, 8, 128], BF16)
        wo = [WP.get(), WP.get()]
        wov = []
        for half in range(2):
            v = wo[half][:].rearrange("p (kc n) -> p kc n", kc=8)
            wload(wo[half], v, W["w_out"], W["w_out"].h[l, :, half * 512:(half + 1) * 512].rearrange("(kc p) n -> p kc n", p=128))
            wov.append(v)
        def b6_a(ci):
            xt = XIN.get()
            load(xt, xt[:], xsrc, x_ap(ci))
            pss = []
            for half in range(2):
                ps = PF.get()
                for kc in range(8):
                    O("pe", "matmul", [ps], [mT, wo[half]], ps[:], lhsT=mT[:, kc, ci * 128:(ci + 1) * 128], rhs=wov[half][:, kc, :],
                      start=(kc == 0), stop=(kc == 7))
                pss.append(ps)
            return xt, pss

        def g_b6_b(ci, xt, pss):
            st = STAT.get()
            yield from g_post_norm_residual(xt, pss, [pss[0][:], pss[1][:]], st)
            if ci == 0:
                dbg("xmid_" + tag, xt, xt[:], [128, D])
            cx.dma("sp", "st", xmid, xm_ap(ci), xt, xt[:])
            yield
            yield from g_norm_mod_T(xt, hT, ci)

        for c2 in range(0, nch, 2):
            cis = list(range(c2, min(c2 + 2, nch)))
            As = [b6_a(ci) for ci in cis]
            interleave([g_b6_b(ci, *a) for ci, a in zip(cis, As)])
        aT3 = r4[:]
        for fb in range(8):
            wu = WP.get()
            wuv = wu[:].rearrange("p (kc n) -> p kc n", kc=8)
            wload(wu, wuv, W["w_up"], W["w_up"].h[l, :, fb * 512:(fb + 1) * 512].rearrange("(kc p) n -> p kc n", p=128))
            wd = WP.get()
            wdv = wd[:].rearrange("p (f n) -> p f n", f=4)
            wload(wd, wdv, W["w_down"], W["w_down"].h[l, fb * 512:(fb + 1) * 512, :].rearrange("(f p) n -> p f n", p=128))
            for tbi in range(ntb):
                ts = slice(tbi * TBW, (tbi + 1) * TBW)
                for f in range(4):
                    ps = PF.get()
                    for kc in range(8):
                        O("pe", "matmul", [ps], [wu, hT], ps[:, 0:TBW], lhsT=wuv[:, kc, f * 128:(f + 1) * 128], rhs=hT[:, kc, ts],
                          start=(kc == 0), stop=(kc == 7))
                    r = TMPH.get()
                    O("act", "activation", [r], [ps], out=r[:, 0:TBW], in_=ps[:, 0:TBW], func=AF.Relu)
                    O("act", "activation", [aT], [r], out=aT3[:, f, ts], in_=r[:, 0:TBW], func=AF.Square)
            if fb == 7 and grp.get("next") is not None:
                prefetch_inputs(grp["next"])
                grp["next"]["prefetched"] = True
            for ci in range(nch):
                for half in range(2):
                    ps = PF.get()
                    for f in range(4):
                        O("pe", "matmul", [ps], [aT, wd], ps[:], lhsT=aT3[:, f, ci * 128:(ci + 1) * 128], rhs=wdv[:, f, half * 512:(half + 1) * 512],
                          start=(f == 0), stop=(f == 3))
                    ya = y2v[:, ci, half * 512:(half + 1) * 512]
                    if fb == 0:
                        evac(big16, ya, ps, ps[:])
                    else:
                        O("dve", "tensor_tensor", [big16], [big16, ps], out=ya, in0=ya, in1=ps[:], op=ALU.add)
        def g_fin(ci):
            xt = XIN.get()
            load(xt, xt[:], xmid, xm_ap(ci))
            st = STAT.get()
            yield from g_post_norm_residual(xt, big16, [y2v[:, ci, 0:512], y2v[:, ci, 512:1024]], st, gt=MODG2)
            if ci == 0:
                dbg("xout_" + tag, xt, xt[:], [128, D])
            cx.dma("sp", "st", xdst, xd_ap(ci), xt, xt[:])
            yield
        return [lambda c2=c2: interleave([g_fin(ci) for ci in range(c2, min(c2 + 2, nch))]) for c2 in range(0, nch, 2)]

    def row_ap(t, c0):
        return lambda ci: t.h[(c0 + ci) * 128:(c0 + ci + 1) * 128, :]

    for l in range(nlayers):
        last = (l == DEPTH - 1)
        if l == 0:
            setup_mod(0)
        layer_setup(l)
        csrc = ctx_in if l == 0 else cs_d
        cgrp = dict(nch=CTXCH, xsrc=csrc, x_ap=row_ap(csrc, 0), is_ctx=True, c0=0, last=last,
                    xmid=cs_d, xm_ap=row_ap(cs_d, 0), xdst=cs_d, xd_ap=row_ap(cs_d, 0))
        pass_A(l, cgrp)
        recur(Uc_d, Sc_d, CTXCH, None, store=not last)
        for d in range(2):
            O("dve", "tensor_copy", [SCTX[d]], [SCUR[d]], out=SCTX[d][:], in_=SCUR[d][:])
        dbg("sctx_l%d" % l, SCTX[0], SCTX[0][:], [128, 512])
        if not last:
            esB = ExitStack()
            for f in pass_B(l, cgrp, last, alloc_passB(esB)):
                f()
            cx.barrier()
            esB.close()
        xsrc = x_in if l == 0 else xs_d
        xdst = out_d if last else xs_d
        groups = []
        for g in range(NCH // G):
            c0 = g * G
            groups.append(dict(nch=G, xsrc=xsrc, x_ap=row_ap(xsrc, c0), is_ctx=False, c0=c0,
                               xmid=xs_d if not last else xs_d, xm_ap=row_ap(xs_d, c0), xdst=xdst, xd_ap=row_ap(xdst, c0)))
        esA = ExitStack()
        bufsA = alloc_passA(esA)
        for gi, grp in enumerate(groups):
            pass_A(l, grp, bufsA)
            if gi == 1 and l + 1 < nlayers:
                setup_mod(l + 1)
        cx.barrier()
        esA.close()
        if stop_after == "A":
            break
        fourier_gather()
        exchange_states()
        dbg("sstart_l%d" % l, SSTART[0], SSTART[0][:], [128, 512])
        es_f = fourier_latent()
        recur(U_d, S_d, NCH, SSTART, store=True)
        cx.barrier()
        es_f.close()
        if stop_after == "F":
            break
        esB = ExitStack()
        bufsB = alloc_passB(esB)
        for gi, grp in enumerate(groups):
            grp["next"] = groups[gi + 1] if gi + 1 < len(groups) else None
            grp["load_mod"] = (gi == 0)
            grp["prefetched"] = False
        fin = None
        for grp in groups:
            fin = pass_B(l, grp, last, bufsB, prev_fin=fin)
        for f in fin:
            f()
        cx.barrier()
        esB.close()

    cx.barrier(full=True)
    cx.emit()
    DEBUG['min_free'] = cx.min_free
    DEBUG['ops'] = {k: len(v.ops) for k, v in cx.engs.items()}
    cx.close()
    return nc, dbg_out


_CACHE = {}


def make_in_maps(inputs):
    in_maps = []
    f32 = np.float32
    shared = {}
    for k in WEIGHT_SPECS:
        a = np.ascontiguousarray(np.asarray(inputs[k], dtype=f32))
        shared[k] = a.reshape(WEIGHT_SPECS[k])
    x = np.asarray(inputs["x"], dtype=f32)
    ctx = np.asarray(inputs["ctx"], dtype=f32)
    c = np.asarray(inputs["c"], dtype=f32)
    c_ctx = np.asarray(inputs["c_ctx"], dtype=f32)
    for core in range(8):
        b, j = core // 4, core % 4
        m = dict(shared)
        m["x"] = np.ascontiguousarray(x[b, 2048 * j:2048 * (j + 1), :])
        m["ctx"] = np.ascontiguousarray(ctx[b])
        m["c2"] = np.ascontiguousarray(np.stack([c[b], c_ctx], 0))
        m.update(host_consts(core))
        in_maps.append(m)
    return in_maps


def kernel(**inputs):
    if "nc" not in _CACHE:
        _CACHE["nc"] = build()[0]
    nc = _CACHE["nc"]
    in_maps = make_in_maps(inputs)
    res = run_bass_kernel_spmd(nc, in_maps, core_ids=list(range(8)))
    out = np.zeros((2, 8192, D), np.float32)
    for core in range(8):
        b, j = core // 4, core % 4
        out[b, 2048 * j:2048 * (j + 1), :] = res.results[core]["out"]
    return out
```

```python
from contextlib import ExitStack
import math
import numpy as np
import ml_dtypes
import concourse.bass as bass
import concourse.mybir as mybir
from concourse.bass_utils import run_bass_kernel_spmd

F32 = mybir.dt.float32
BF16 = mybir.dt.bfloat16
AF = mybir.ActivationFunctionType
ALU = mybir.AluOpType

D = 1024
DEPTH = 2
NCH = 16
G = 4
CTXCH = 2
IN_W = 5888
EPS = 1e-6
SK = 128 ** -0.5
DEBUG = {}


class Tile:
    __slots__ = ("name", "h", "w", "r")

    def __init__(self, name, h):
        self.name = name
        self.h = h
        self.w = {}
        self.r = {}

    def __getitem__(self, k):
        return self.h[k]


class Eng:
    def __init__(self, name, sem):
        self.name = name
        self.sem = sem
        self.count = 0
        self.ops = []
        self.waited = {}


class Stream:
    K = 8

    def __init__(self, name, sems):
        self.name = name
        self.sems = sems
        self.n = 0
        self.sem = sems[0]
        self.count = 0


class Ctx:
    ENG_NAMES = ("pe", "act", "dve", "pool", "sp")

    def __init__(self, nc):
        self.nc = nc
        self.stack = ExitStack()
        self.engs = {}
        for n in self.ENG_NAMES:
            sem = self.stack.enter_context(nc.semaphore("sem_" + n))
            self.engs[n] = Eng(n, sem)
        self.streams = {}
        self.ntiles = 0

    def stream(self, name, k=None):
        if name not in self.streams:
            k = k or Stream.K
            sems = [self.stack.enter_context(self.nc.semaphore("dq_%s%d" % (name, i))) for i in range(k)]
            self.streams[name] = Stream(name, sems)
        return self.streams[name]

    def sbuf(self, name, shape, dtype, stack=None):
        self.ntiles += 1
        h = (stack or self.stack).enter_context(self.nc.sbuf_tensor(f"{name}_{self.ntiles}", list(shape), dtype))
        self.min_free = min(getattr(self, "min_free", 1 << 30), self.nc.sbuf_bytes_remaining)
        return Tile(name, h)

    def psum(self, name, shape, dtype=F32):
        self.ntiles += 1
        h = self.stack.enter_context(self.nc.psum_tensor(f"{name}_{self.ntiles}", list(shape), dtype))
        return Tile(name, h)

    def dram(self, name, shape, dtype, kind="Internal"):
        t = self.nc.dram_tensor(name, list(shape), dtype, kind=kind)
        return Tile(name, t.ap())

    def _collect(self, eng, reads, writes):
        need = {}

        def add(d):
            for s, v in d.items():
                if need.get(s, 0) < v:
                    need[s] = v
        for t in reads:
            add(t.w)
        for t in writes:
            add(t.w)
            add(t.r)
        waits = []
        for s, v in need.items():
            if s is eng.sem and eng.name == "pe":
                continue
            if eng.waited.get(s, 0) >= v:
                continue
            eng.waited[s] = v
            waits.append((s, v))
        return waits

    def _mark(self, ev, reads, writes):
        for t in reads:
            if t.r.get(ev[0], 0) < ev[1]:
                t.r[ev[0]] = ev[1]
        for t in writes:
            t.w = {ev[0]: ev[1]}
            t.r = {}

    def op(self, engname, fn, reads=(), writes=()):
        eng = self.engs[engname]
        waits = self._collect(eng, reads, writes)
        eng.count += 1
        ev = (eng.sem, eng.count)
        eng.ops.append((waits, fn, (eng.sem, 1)))
        self._mark(ev, reads, writes)
        return ev

    def dma(self, engname, streamname, out_t, out_ap, in_t, in_ap, **kw):
        eng = self.engs[engname]
        st = self.stream(streamname)
        waits = self._collect(eng, [in_t], [out_t])
        k = len(st.sems)
        idx = st.n
        st.n += 1
        sem = st.sems[idx % k]
        if idx >= k:
            pv = 16 * (idx // k)
            if eng.waited.get(sem, 0) < pv:
                eng.waited[sem] = pv
                waits.append((sem, pv))
        ev = (sem, 16 * (idx // k + 1))

        def fn(e, out_ap=out_ap, in_ap=in_ap, kw=kw):
            return e.dma_start(out=out_ap, in_=in_ap, **kw)
        eng.ops.append((waits, fn, (sem, 16)))
        self._mark(ev, [in_t], [out_t])
        return ev

    def custom(self, engname, fn, reads, writes, st, inc):
        eng = self.engs[engname]
        waits = self._collect(eng, reads, writes)
        if st.count and eng.waited.get(st.sem, 0) < st.count:
            eng.waited[st.sem] = st.count
            waits.append((st.sem, st.count))
        st.count += inc
        ev = (st.sem, st.count)
        eng.ops.append((waits, fn, (st.sem, inc)))
        self._mark(ev, reads, writes)
        return ev

    def barrier(self, full=False, pool=False):
        evs = {}
        for e in self.engs.values():
            if e.count:
                evs[e.sem] = e.count
        for s in self.streams.values():
            if not full and s.name in ("w", "cc"):
                continue
            if s.count:
                evs[s.sem] = s.count
            k = len(s.sems)
            for i in range(min(k, s.n)):
                evs[s.sems[i]] = 16 * ((s.n - 1 - i) // k + 1)
        for e in self.engs.values():
            if e.name == "pool" and not (full or pool):
                continue
            waits = []
            for s, v in evs.items():
                if s is e.sem:
                    continue
                if e.waited.get(s, 0) >= v:
                    continue
                e.waited[s] = v
                waits.append((s, v))
            if waits:
                e.ops.append((waits, None, None))

    def emit(self):
        engs = self.engs

        def replay(handle, ops):
            for waits, fn, inc in ops:
                for s, v in waits:
                    handle.wait_ge(s, v)
                if fn is None:
                    continue
                fn(handle).then_inc(inc[0], inc[1])

        with self.nc.allow_non_contiguous_dma(reason="strided layout DMAs (small)"), self.nc.Block() as block:
            @block.tensor
            def _(e):
                replay(e, engs["pe"].ops)

            @block.scalar
            def _(e):
                replay(e, engs["act"].ops)

            @block.vector
            def _(e):
                replay(e, engs["dve"].ops)

            @block.gpsimd
            def _(e):
                replay(e, engs["pool"].ops)

            @block.sync
            def _(e):
                replay(e, engs["sp"].ops)

    def close(self):
        self.stack.close()


class Pool:
    def __init__(self, tiles):
        self.tiles = tiles
        self.i = 0

    def get(self):
        t = self.tiles[self.i % len(self.tiles)]
        self.i += 1
        return t


def host_consts(core):
    j = core % 4
    bf = ml_dtypes.bfloat16
    c = {}
    t = np.arange(2048, dtype=np.float64) + 2048 * j
    row = np.floor(t / 64.0)
    col = t - 64.0 * row
    freqs = (10000.0 ** (-np.arange(32, dtype=np.float32) / np.float32(32))).astype(np.float64)
    ang = np.concatenate([row[:, None] * freqs, col[:, None] * freqs], -1)
    ang32 = np.concatenate([(row.astype(np.float32)[:, None] * freqs.astype(np.float32)),
                            (col.astype(np.float32)[:, None] * freqs.astype(np.float32))], -1).astype(np.float64)
    c["rope_cos"] = np.cos(ang32).astype(np.float32)
    c["rope_sin"] = np.sin(ang32).astype(np.float32)
    a = np.arange(64)[:, None]
    kl = np.arange(64)[None, :]
    th = 2 * np.pi * a * kl / 64.0
    c["L1"] = np.concatenate([np.cos(th), -np.sin(th)], 1).astype(bf)
    s = 1.0 / math.sqrt(8192 * 64)
    b = np.arange(128)[:, None, None]
    klo = np.arange(64)[None, :, None]
    kh = (32 * j + np.arange(32))[None, None, :]
    k = 64 * kh + klo
    th3 = 2 * np.pi * ((b * k) % 8192) / 8192.0
    Ere = s * np.cos(th3)
    Eim = -s * np.sin(th3)
    c["E3"] = np.concatenate([-Eim, Ere, Eim], 2).astype(bf)
    s2 = 1.0 / math.sqrt(256 * 64)
    n = np.arange(256)[:, None]
    kk = np.arange(256)[None, :]
    th2 = 2 * np.pi * ((n * kk) % 256) / 256.0
    d256 = np.concatenate([s2 * np.cos(th2), -s2 * np.sin(th2)], 1)
    c["D256"] = d256.reshape(2, 128, 512).transpose(1, 0, 2).astype(bf).copy()
    m = np.arange(64)[:, None]
    jj = np.arange(64)[None, :]
    thc = 2 * np.pi * ((m * jj) % 64) / 64.0
    c["CS64"] = np.concatenate([np.cos(thc), np.sin(thc)], 1).astype(bf)
    sidx = np.arange(128)[:, None].astype(np.float32)
    cidx = np.arange(128)[None, :].astype(np.float32)
    rc = np.zeros((128, 6, 128), np.float32)
    rc[:, 0, :] = np.maximum(cidx - sidx, 0)
    rc[:, 1, :] = np.maximum(sidx - cidx, 0)
    rc[:, 2, :] = (cidx >= sidx) * SK
    rc[:, 3, :] = (sidx > cidx) * SK
    rc[:, 4, :] = cidx + 1.0
    rc[:, 5, :] = 128.0 - cidx
    c["RC"] = rc
    wc = np.zeros((128, 2), np.float32)
    wc[:, 0] = 127.0 - np.arange(128)
    wc[:, 1] = np.arange(128)
    c["WC"] = wc
    mexp = np.zeros((2, 5), np.float32)
    mask = np.zeros((2, 5), np.float32)
    for jp in range(4):
        if jp < j:
            mexp[0, jp] = j - 1 - jp
            mask[0, jp] = 1
        if jp > j:
            mexp[1, jp] = jp - j - 1
            mask[1, jp] = 1
    mexp[0, 4] = j
    mask[0, 4] = 1
    mexp[1, 4] = 3 - j
    mask[1, 4] = 1
    cm = np.zeros((128, 2, 2, 5), np.float32)
    cm[:, 0] = mexp[None] * 2048.0
    cm[:, 1] = mask[None]
    c["CMX"] = cm.reshape(128, 20)
    c["IDENT"] = np.eye(128, dtype=np.float32).astype(bf)
    return c


CONST_SPECS = {
    "rope_cos": ([2048, 64], F32), "rope_sin": ([2048, 64], F32),
    "L1": ([64, 128], BF16), "E3": ([128, 64, 96], BF16), "D256": ([128, 2, 512], BF16),
    "CS64": ([64, 128], BF16), "RC": ([128, 6, 128], F32), "WC": ([128, 2], F32),
    "CMX": ([128, 20], F32), "IDENT": ([128, 128], BF16),
}

WEIGHT_SPECS = {
    "w_mod": [DEPTH, D, 6 * D], "b_mod": [DEPTH, 6 * D], "g_pre_mix": [DEPTH, D], "g_post_mix": [DEPTH, D],
    "g_pre_mlp": [DEPTH, D], "g_post_mlp": [DEPTH, D], "w_in": [DEPTH, D, IN_W],
    "ret_decay_logit": [DEPTH, 8], "sgu_w_s": [DEPTH, 4, 128, 128], "sgu_b_s": [DEPTH, 4, 128],
    "sgu_norm": [DEPTH, 256], "w_branch_a": [DEPTH, 512, D], "w_branch_b": [DEPTH, 256, D],
    "w_branch_c": [DEPTH, 256, D], "w_out": [DEPTH, D, D], "w_up": [DEPTH, D, 4 * D], "w_down": [DEPTH, 4 * D, D],
}


def build(debug=(), nlayers=DEPTH, stop_after=None):
    nc = bass.Bass("TRN2", target_bir_lowering=False)
    cx = Ctx(nc)
    dbg_out = {}

    x_in = cx.dram("x", [2048, D], F32, kind="ExternalInput")
    ctx_in = cx.dram("ctx", [256, D], F32, kind="ExternalInput")
    c2_in = cx.dram("c2", [2, D], F32, kind="ExternalInput")
    W = {k: cx.dram(k, shp, F32, kind="ExternalInput") for k, shp in WEIGHT_SPECS.items()}
    K = {k: cx.dram(k, shp, dt, kind="ExternalInput") for k, (shp, dt) in CONST_SPECS.items()}
    out_d = cx.dram("out", [2048, D], F32, kind="ExternalOutput")
    xs_d = cx.dram("xs_d", [2048, D], F32)
    cs_d = cx.dram("cs_d", [256, D], F32)
    modraw_d = cx.dram("modraw_d", [DEPTH, 2, 6 * D], F32)
    modv_d = cx.dram("modv_d", [DEPTH, 2, 2, 2, D], F32)
    hT_d = cx.dram("hT_d", [NCH // G, 128, 8, G * 128], BF16)
    hTc_d = cx.dram("hTc_d", [128, 8, G * 128], BF16)
    kv_d = cx.dram("kv_d", [NCH // G, 2, 128, G, 512], BF16)
    kvc_d = cx.dram("kvc_d", [2, 128, G, 512], BF16)
    U_d = cx.dram("U_d", [2, NCH, 128, 512], F32)
    Uc_d = cx.dram("Uc_d", [2, CTXCH, 128, 512], F32)
    S_d = cx.dram("S_d", [2, NCH, 128, 512], BF16)
    Sc_d = cx.dram("Sc_d", [2, CTXCH, 128, 512], BF16)
    st_loc = cx.dram("st_loc", [2 * 128, 512], F32)
    st_all = cx.dram("st_all", [4 * 2 * 128, 512], F32)
    f_loc = cx.dram("f_loc", [4 * 2048, 64], BF16)
    f_all = cx.dram("f_all", [4 * 4 * 2048, 64], BF16)
    zt_d = cx.dram("zt_d", [64, 8, 2048], BF16)
    wbp_d = cx.dram("wbp_d", [64, 8, D], BF16)
    cc = cx.stream("cc", k=1)

    def dbg(name, t, ap, shape, dtype=F32):
        if name not in debug:
            return
        o = cx.dram("dbg_" + name, list(shape), dtype, kind="ExternalOutput")
        cx.dma("sp", "dbg", o, o.h, t, ap)
        dbg_out[name] = (shape, dtype)

    WP = Pool([cx.sbuf(f"wp{i}", [128, 4096], BF16) for i in range(6)])
    XIN = Pool([cx.sbuf(f"xin{i}", [128, D], F32) for i in range(3)])
    JUNK = Pool([cx.sbuf(f"junk{i}", [128, D], BF16) for i in range(1)])
    TMPF = Pool([cx.sbuf(f"tmpf{i}", [128, D], F32) for i in range(2)])
    TMPH = Pool([cx.sbuf(f"tmph{i}", [128, 512], F32) for i in range(6)])
    HB = Pool([cx.sbuf(f"hb{i}", [128, D], BF16) for i in range(2)])
    TB = Pool([cx.sbuf(f"tb{i}", [128, 512], BF16) for i in range(16)])
    SFB = Pool([cx.sbuf(f"sfb{i}", [128, 2, 512], BF16) for i in range(4)])
    STAT = Pool([cx.sbuf(f"stat{i}", [128, 8], F32) for i in range(6)])
    MODA = cx.sbuf("modA", [128, D], F32)
    MODB = cx.sbuf("modB", [128, D], F32)
    MODG = cx.sbuf("modG", [128, D], F32)
    MODG2 = cx.sbuf("modG2", [128, D], F32)
    ROPE = cx.sbuf("rope", [128, 2, G, 64], F32)
    ROPE1 = cx.sbuf("rope1", [128, 2, CTXCH, 64], F32)
    ident = cx.sbuf("ident", [128, 128], BF16)
    RC = cx.sbuf("RC", [128, 6, 128], F32)
    WC = cx.sbuf("WC", [128, 2], F32)
    CMX = cx.sbuf("CMX", [128, 20], F32)
    LG = cx.sbuf("LG", [128, 8], F32)
    DT = cx.sbuf("DT", [128, 4, 128], F32)
    QF = cx.sbuf("QF", [128, 4, 128], F32)
    QB = cx.sbuf("QB", [128, 4, 128], F32)
    WFB = cx.sbuf("WFB", [128, 2, 4], F32)
    G128 = cx.sbuf("G128", [128, 8], F32)
    COEF = cx.sbuf("COEF", [128, 2, 5, 4], F32)
    epsb = cx.sbuf("epsb", [128, 2], F32)
    SCUR = [cx.sbuf(f"scur{i}", [128, 512], F32) for i in range(2)]
    SCTX = [cx.sbuf(f"sctx{i}", [128, 512], F32) for i in range(2)]
    SSTART = [cx.sbuf(f"sstart{i}", [128, 512], F32) for i in range(2)]
    c2T = cx.sbuf("c2T", [128, 8, 2], F32)
    c2Tb = cx.sbuf("c2Tb", [128, 8, 2], BF16)
    SGW = cx.sbuf("SGW", [128, 4, 128], BF16)
    SGWt = cx.sbuf("SGWt", [128, 4, 128], BF16)
    SGB = cx.sbuf("SGB", [128, 4], F32)
    SGN = cx.sbuf("SGN", [128, 256], F32)
    L1 = cx.sbuf("L1", [64, 128], BF16)
    CS64 = cx.sbuf("CS64", [64, 128], BF16)
    ZTC = cx.sbuf("ZTC", [64, 8, 256], BF16)

    PF = Pool([cx.psum(f"pf{i}", [128, 512], F32) for i in range(6)])
    PBT = Pool([cx.psum(f"pb{i}", [128, 1024], BF16) for i in range(2)])

    def O(eng, meth, writes, reads, *a, **k):
        return cx.op(eng, lambda e: getattr(e, meth)(*a, **k), reads=reads, writes=writes)

    def load(t, ap, src, sap, eng="sp", stream="ld"):
        cx.dma(eng, stream, t, ap, src, sap)

    def wload(t, ap, src, sap):
        cx.dma("pool", "w", t, ap, src, sap)

    cpy_i = [0]

    def evac(out_t, out_ap, in_t, in_ap, eng=None):
        cpy_i[0] += 1
        use_act = (cpy_i[0] % 4 != 0) if eng is None else (eng == "act")
        if use_act:
            O("act", "activation", [out_t], [in_t], out=out_ap, in_=in_ap, func=AF.Copy)
        else:
            O("dve", "tensor_copy", [out_t], [in_t], out=out_ap, in_=in_ap)

    def transpose_blocks(src_t, src_aps, dst_t, dst_ap, eng=None):
        n = len(src_aps)
        pb = PBT.get()
        for i, sap in enumerate(src_aps):
            O("pe", "transpose", [pb], [src_t, ident], out=pb[:, i * 128:(i + 1) * 128], in_=sap, identity=ident[:])
        evac(dst_t, dst_ap, pb, pb[:, 0:n * 128].rearrange("p (n c) -> p n c", n=n), eng=eng)

    for name, t in (("IDENT", ident), ("RC", RC), ("WC", WC), ("CMX", CMX), ("L1", L1), ("CS64", CS64)):
        load(t, t[:], K[name], K[name].h)
    O("dve", "memset", [epsb], [], epsb[:, 0:1], EPS)
    O("dve", "memset", [epsb], [], epsb[:, 1:2], 1.0)
    O("dve", "memset", [ROPE1], [], ROPE1[:, 0], 1.0)
    O("dve", "memset", [ROPE1], [], ROPE1[:, 1], 0.0)
    for r in range(2):
        load(c2T, c2T[:, :, r], c2_in, c2_in.h[r].rearrange("(kc p) -> p kc", p=128))
    O("act", "activation", [c2Tb], [c2T], out=c2Tb[:], in_=c2T[:], func=AF.Silu)

    def layer_setup(l):
        load(LG, LG[:], W["ret_decay_logit"], W["ret_decay_logit"].h[l:l + 1, :].broadcast_to([128, 8]))
        O("act", "activation", [LG], [LG], out=LG[:], in_=LG[:], func=AF.Exp, scale=-1.0)
        O("act", "activation", [LG], [LG, epsb], out=LG[:], in_=LG[:], func=AF.Ln, bias=epsb[:, 1:2])
        O("dve", "tensor_scalar", [LG], [LG], out=LG[:], in0=LG[:], scalar1=-1.0, scalar2=None, op0=ALU.mult)
        tA = TMPH.get()
        tB = TMPH.get()
        for h in range(4):
            O("act", "activation", [tA], [RC, LG], out=tA[:, 0:128], in_=RC[:, 0, :], func=AF.Exp, scale=LG[:, h:h + 1])
            O("act", "activation", [tB], [RC, LG], out=tB[:, 0:128], in_=RC[:, 1, :], func=AF.Exp, scale=LG[:, 4 + h:5 + h])
            O("dve", "tensor_tensor", [tA], [tA, RC], out=tA[:, 0:128], in0=tA[:, 0:128], in1=RC[:, 2, :], op=ALU.mult)
            O("dve", "tensor_tensor", [tB], [tB, RC], out=tB[:, 0:128], in0=tB[:, 0:128], in1=RC[:, 3, :], op=ALU.mult)
            O("dve", "tensor_tensor", [DT], [tA, tB], out=DT[:, h, :], in0=tA[:, 0:128], in1=tB[:, 0:128], op=ALU.add)
            O("act", "activation", [QF], [RC, LG], out=QF[:, h, :], in_=RC[:, 4, :], func=AF.Exp, scale=LG[:, h:h + 1])
            O("act", "activation", [QB], [RC, LG], out=QB[:, h, :], in_=RC[:, 5, :], func=AF.Exp, scale=LG[:, 4 + h:5 + h])
        for d in range(2):
            O("act", "activation", [WFB], [LG, WC], out=WFB[:, d, :], in_=LG[:, 4 * d:4 * d + 4], func=AF.Exp,
              scale=WC[:, d:d + 1])
        O("dve", "tensor_scalar", [WFB], [WFB], out=WFB[:], in0=WFB[:], scalar1=SK, scalar2=None, op0=ALU.mult)
        O("act", "activation", [G128], [LG], out=G128[:], in_=LG[:], func=AF.Exp, scale=128.0)
        cmv = CMX[:].rearrange("p (a d s) -> p a d s", a=2, d=2)
        for d in range(2):
            for s in range(5):
                O("act", "activation", [COEF], [LG, CMX], out=COEF[:, d, s, :], in_=LG[:, 4 * d:4 * d + 4], func=AF.Exp,
                  scale=cmv[:, 0, d, s:s + 1])
                O("dve", "tensor_scalar", [COEF], [COEF, CMX], out=COEF[:, d, s, :], in0=COEF[:, d, s, :],
                  scalar1=cmv[:, 1, d, s:s + 1], scalar2=None, op0=ALU.mult)

        wt = WP.get()
        wf = wt[:, 0:512].rearrange("p (g s) -> p g s", g=4)
        wload(wt, wf, W["sgu_w_s"], W["sgu_w_s"].h[l].rearrange("g t s -> t g s"))
        O("dve", "tensor_copy", [SGWt], [wt], out=SGWt[:], in_=wf)
        transpose_blocks(SGWt, [SGWt[:, g, :] for g in range(4)], SGW, SGW[:])
        with nc.allow_non_contiguous_dma(reason="tiny transposed bias load"):
            load(SGB, SGB[:], W["sgu_b_s"], W["sgu_b_s"].h[l].rearrange("g t -> t g"))
        load(SGN, SGN[:], W["sgu_norm"], W["sgu_norm"].h[l:l + 1, :].broadcast_to([128, 256]))

        WBR = WP.get()
        WBRv = WBR[0:64, :].rearrange("p (g d) -> p g d", g=4)
        wload(WBR, WBRv, W["w_branch_b"], W["w_branch_b"].h[l].rearrange("(g j) d -> j g d", j=64))
        for g in range(4):
            for c in range(2):
                tb = TB.get()
                tb2 = TB.get()
                for half, tt in enumerate((tb, tb2)):
                    ps = PF.get()
                    O("pe", "matmul", [ps], [CS64, WBR], ps[0:64, :], lhsT=CS64[:, c * 64:(c + 1) * 64],
                      rhs=WBRv[:, g, half * 512:(half + 1) * 512], start=True, stop=True)
                    evac(tt, tt[0:64, :], ps, ps[0:64, :])
                    cx.dma("sp", "st", wbp_d, wbp_d.h[:, g * 2 + c, half * 512:(half + 1) * 512], tt, tt[0:64, :])


    def setup_mod(l):
        for cb in range(12):
            wt = WP.get()
            wv = wt[:].rearrange("p (kc n) -> p kc n", kc=8)
            wload(wt, wv, W["w_mod"], W["w_mod"].h[l, :, cb * 512:(cb + 1) * 512].rearrange("(kc p) n -> p kc n", p=128))
            bch = TMPH.get()
            load(bch, bch[0:2, :], W["b_mod"], W["b_mod"].h[l:l + 1, cb * 512:(cb + 1) * 512].broadcast_to([2, 512]))
            ps = PF.get()
            for kc in range(8):
                O("pe", "matmul", [ps], [c2Tb, wt], ps[0:2, :], lhsT=c2Tb[:, kc, :], rhs=wv[:, kc, :],
                  start=(kc == 0), stop=(kc == 7))
            O("dve", "tensor_tensor", [bch], [ps, bch], out=bch[0:2, :], in0=ps[0:2, :], in1=bch[0:2, :], op=ALU.add)
            cx.dma("sp", "st", modraw_d, modraw_d.h[l][:, cb * 512:(cb + 1) * 512], bch, bch[0:2, :])

        setup_modv(l)

    def setup_modv(l):
        def bc(t, ap, src, sap):
            load(t, ap, src, sap.broadcast_to([128, D]))
        for row in range(2):
            for which in range(2):
                o = 3 * which
                gpre = W["g_pre_mix"] if which == 0 else W["g_pre_mlp"]
                gpost = W["g_post_mix"] if which == 0 else W["g_post_mlp"]
                ta = TMPF.get()
                t1 = TMPF.get()
                bc(ta, ta[:], modraw_d, modraw_d.h[l][row:row + 1, (o + 1) * D:(o + 2) * D])
                bc(t1, t1[:], gpre, gpre.h[l:l + 1, :])
                O("dve", "scalar_tensor_tensor", [ta], [ta, t1], out=ta[:], in0=ta[:], scalar=1.0, in1=t1[:], op0=ALU.add, op1=ALU.mult)
                cx.dma("sp", "st", modv_d, modv_d.h[l][row, which, 0:1, :], ta, ta[0:1, :])
                tg = TMPF.get()
                t2 = TMPF.get()
                bc(tg, tg[:], modraw_d, modraw_d.h[l][row:row + 1, (o + 2) * D:(o + 3) * D])
                bc(t2, t2[:], gpost, gpost.h[l:l + 1, :])
                O("dve", "tensor_tensor", [tg], [tg, t2], out=tg[:], in0=tg[:], in1=t2[:], op=ALU.mult)
                cx.dma("sp", "st", modv_d, modv_d.h[l][row, which, 1:2, :], tg, tg[0:1, :])

    def load_vec(t, row, which, kind, l):
        if kind == "B":
            src, sap = modraw_d, modraw_d.h[l][row:row + 1, (3 * which) * D:(3 * which + 1) * D]
        else:
            i = 0 if kind == "A" else 1
            src, sap = modv_d, modv_d.h[l][row, which, i:i + 1, :]
        load(t, t[:], src, sap.broadcast_to([128, D]))

    def g_rms_rstd(src_t, src_aps, n, st):
        srcs = src_t if isinstance(src_t, (list, tuple)) else [src_t] * len(src_aps)
        O("dve", "memset", [st], [], st[:], 0.0)
        yield
        single = len(src_aps) == 1
        for i, sap in enumerate(src_aps):
            junk = JUNK.get()
            jv = junk[:, 0:sap.shape[-1]]
            col = 0 if single else 4 + i
            O("act", "activation", [junk, st], [srcs[i], st], out=jv, in_=sap, func=AF.Square, accum_out=st[:, col:col + 1])
            yield
        if not single:
            O("dve", "tensor_tensor", [st], [st], out=st[:, 0:1], in0=st[:, 4:5], in1=st[:, 5:6], op=ALU.add)
            yield
        O("act", "activation", [st], [st, epsb], out=st[:, 1:2], in_=st[:, 0:1], func=AF.Sqrt, scale=1.0 / n, bias=epsb[:, 0:1])
        yield
        O("dve", "reciprocal", [st], [st], out=st[:, 2:3], in_=st[:, 1:2])
        yield

    def run(gen):
        for _ in gen:
            pass

    def interleave(gens):
        gens = list(gens)
        while gens:
            for g in list(gens):
                try:
                    next(g)
                except StopIteration:
                    gens.remove(g)

    def rms_rstd(src_t, src_aps, n, st):
        run(g_rms_rstd(src_t, src_aps, n, st))

    def g_norm_mod_T(xt, dstT, ci, out=None):
        st = STAT.get()
        yield from g_rms_rstd(xt, [xt[:]], D, st)
        tmp = TMPF.get()
        O("dve", "scalar_tensor_tensor", [tmp], [xt, st, MODA], out=tmp[:], in0=xt[:], scalar=st[:, 2:3], in1=MODA[:],
          op0=ALU.mult, op1=ALU.mult)
        yield
        hb = HB.get()
        O("dve", "tensor_tensor", [hb], [tmp, MODB], out=hb[:], in0=tmp[:], in1=MODB[:], op=ALU.add)
        if out is not None:
            out.append(hb)
        yield
        transpose_blocks(hb, [hb[:, kc * 128:(kc + 1) * 128] for kc in range(8)], dstT, dstT[:, :, ci * 128:(ci + 1) * 128])
        yield

    def norm_mod_T(xt, dstT, ci, add_eng="dve"):
        out = []
        run(g_norm_mod_T(xt, dstT, ci, out))
        return out[0]

    def zblock(hT, ci, wv, ncols, col0=0):
        ps = PF.get()
        for kc in range(8):
            O("pe", "matmul", [ps], [hT, wv.tile], ps[:, 0:ncols], lhsT=hT[:, kc, ci * 128:(ci + 1) * 128],
              rhs=wv.ap[:, kc, col0:col0 + ncols], start=(kc == 0), stop=(kc == 7))
        return ps

    class WV:
        def __init__(self, tile, ap):
            self.tile = tile
            self.ap = ap

    def load_win(l, c0, ncols):
        wt = WP.get()
        wv = wt[:, 0:8 * ncols].rearrange("p (kc n) -> p kc n", kc=8)
        wload(wt, wv, W["w_in"], W["w_in"].h[l, :, c0:c0 + ncols].rearrange("(kc p) n -> p kc n", p=128))
        return WV(wt, wv)

    def rope(ps, ropet, ci, dst_t, dst_ap, comb_eng="dve"):
        pv = ps[:].rearrange("p (h t i) -> p h t i", h=4, t=2)
        cosb = ropet[:, 0, ci, :].unsqueeze(1).broadcast_to([128, 4, 64])
        sinb = ropet[:, 1, ci, :].unsqueeze(1).broadcast_to([128, 4, 64])
        t1 = TMPH.get()
        t2 = TMPH.get()
        t1v = t1[:].rearrange("p (h t i) -> p h t i", h=4, t=2)
        t2v = t2[:].rearrange("p (h t i) -> p h t i", h=4, t=2)
        dv = dst_ap.rearrange("p (h t i) -> p h t i", h=4, t=2)
        O("dve", "tensor_tensor", [t1], [ps, ropet], out=t1v[:, :, 0, :], in0=pv[:, :, 0, :], in1=cosb, op=ALU.mult)
        O("dve", "tensor_tensor", [t1], [ps, ropet], out=t1v[:, :, 1, :], in0=pv[:, :, 1, :], in1=cosb, op=ALU.mult)
        O("dve", "tensor_tensor", [t2], [ps, ropet], out=t2v[:, :, 0, :], in0=pv[:, :, 1, :], in1=sinb, op=ALU.mult)
        O("dve", "tensor_tensor", [t2], [ps, ropet], out=t2v[:, :, 1, :], in0=pv[:, :, 0, :], in1=sinb, op=ALU.mult)
        O(comb_eng, "tensor_tensor", [dst_t], [t1, t2], out=dv[:, :, 0, :], in0=t1v[:, :, 0, :], in1=t2v[:, :, 0, :], op=ALU.subtract)
        O(comb_eng, "tensor_tensor", [dst_t], [t1, t2], out=dv[:, :, 1, :], in0=t1v[:, :, 1, :], in1=t2v[:, :, 1, :], op=ALU.add)

    def alloc_passA(es):
        return dict(hT=cx.sbuf("hT", [128, 8, G * 128], BF16, stack=es), k_tok=cx.sbuf("k_tok", [128, G, 512], BF16, stack=es),
                    vf=cx.sbuf("vf", [128, G, 512], BF16, stack=es), vb=cx.sbuf("vb", [128, G, 512], BF16, stack=es),
                    f_tok=cx.sbuf("f_tok", [128, G, 256], BF16, stack=es), v_pl=cx.sbuf("v_pl", [128, G, 512], BF16, stack=es))

    def pass_A(l, grp, bufs=None):
        nch, xsrc, x_ap, is_ctx = grp["nch"], grp["xsrc"], grp["x_ap"], grp["is_ctx"]
        ropet = ROPE1 if is_ctx else ROPE
        es = None
        if bufs is None:
            es = ExitStack()
            bufs = alloc_passA(es)
        hT, k_tok, vf, vb, f_tok, v_pl = bufs["hT"], bufs["k_tok"], bufs["vf"], bufs["vb"], bufs["f_tok"], bufs["v_pl"]
        kvd_ap = kvc_d.h if is_ctx else kv_d.h[grp["c0"] // G]
        kvd = kvc_d if is_ctx else kv_d
        k3, vf3, vb3, f3 = k_tok[:], vf[:], vb[:], f_tok[:]
        row = 1 if is_ctx else 0
        load_vec(MODA, row, 0, "A", l)
        load_vec(MODB, row, 0, "B", l)
        if not is_ctx:
            c0 = grp["c0"]
            for i, nm in enumerate(("rope_cos", "rope_sin")):
                load(ROPE, ROPE[:, i], K[nm], K[nm].h[c0 * 128:(c0 + nch) * 128, :].rearrange("(c p) i -> p c i", p=128))
        for c2 in range(0, nch, 2):
            gens = []
            for ci in range(c2, min(c2 + 2, nch)):
                xt = XIN.get()
                load(xt, xt[:], xsrc, x_ap(ci))
                gens.append(g_norm_mod_T(xt, hT, ci))
            interleave(gens)
        hTd = hTc_d if is_ctx else hT_d
        hTd_ap = hTc_d.h if is_ctx else hT_d.h[grp["c0"] // G]
        cx.dma("sp", "st", hTd, hTd_ap[:, :, 0:nch * 128], hT, hT[:, :, 0:nch * 128])
        wv = load_win(l, 512, 512)
        for ci in range(nch):
            ps = zblock(hT, ci, wv, 512)
            rope(ps, ropet, ci, k_tok, k3[:, ci, :])
        cx.dma("sp", "st", kvd, kvd_ap[0, :, 0:nch, :], k_tok, k3[:, 0:nch, :])
        wv = load_win(l, 1024, 512)
        for ci in range(nch):
            ps = zblock(hT, ci, wv, 512)
            pv = ps[:].rearrange("p (h e) -> p h e", h=4)
            O("dve", "tensor_tensor", [vf], [ps, WFB], out=vf3[:, ci, :].rearrange("p (h e) -> p h e", h=4), in0=pv,
              in1=WFB[:, 0, :].unsqueeze(2).broadcast_to([128, 4, 128]), op=ALU.mult)
            O("dve", "tensor_tensor", [vb], [ps, WFB], out=vb3[:, ci, :].rearrange("p (h e) -> p h e", h=4), in0=pv,
              in1=WFB[:, 1, :].unsqueeze(2).broadcast_to([128, 4, 128]), op=ALU.mult)
            O("dve", "tensor_copy", [v_pl], [ps], out=v_pl[:, ci, :], in_=ps[:])
        cx.dma("sp", "st", kvd, kvd_ap[1, :, 0:nch, :], v_pl, v_pl[:, 0:nch, :])
        wv = load_win(l, 2048, 256)
        for ci in range(nch):
            ps = zblock(hT, ci, wv, 256)
            evac(f_tok, f3[:, ci, :], ps, ps[:, 0:256])
            if not is_ctx:
                tok0 = (grp["c0"] + ci) * 128
                with nc.allow_non_contiguous_dma(reason="quarter-major f layout for the Fourier exchange"):
                    cx.dma("sp", "st", f_loc,
                           f_loc.h.rearrange("(q t) c -> t q c", q=4)[tok0:tok0 + 128, :, :], f_tok,
                           f3[:, ci, :].rearrange("p (q c) -> p q c", q=4))
        Ud = Uc_d if is_ctx else U_d
        for ci in range(nch):
            gci = ci if is_ctx else grp["c0"] + ci
            for d, vw in enumerate((vf3, vb3)):
                ps = PF.get()
                for h in range(4):
                    O("pe", "matmul", [ps], [k_tok, vf if d == 0 else vb], ps[:, h * 128:(h + 1) * 128],
                      lhsT=k3[:, ci, h * 128:(h + 1) * 128], rhs=vw[:, ci, h * 128:(h + 1) * 128], start=True, stop=True)
                tmp = TMPH.get()
                evac(tmp, tmp[:], ps, ps[:])
                cx.dma("sp", "st", Ud, Ud.h[d, gci], tmp, tmp[:])
                if not is_ctx:
                    pw = STAT.get()
                    O("act", "activation", [pw], [LG], out=pw[:, 0:4], in_=LG[:, 4 * d:4 * d + 4], func=AF.Exp,
                      scale=float(128 * ((NCH - 1 - gci) if d == 0 else gci)))
                    t2 = TMPH.get()
                    O("dve", "tensor_tensor", [t2], [tmp, pw], out=t2[:].rearrange("p (h e) -> p h e", h=4),
                      in0=tmp[:].rearrange("p (h e) -> p h e", h=4), in1=pw[:, 0:4].unsqueeze(2).broadcast_to([128, 4, 128]), op=ALU.mult)
                    if gci == 0:
                        O("dve", "tensor_copy", [SCUR[d]], [t2], out=SCUR[d][:], in_=t2[:])
                    else:
                        O("dve", "tensor_tensor", [SCUR[d]], [SCUR[d], t2], out=SCUR[d][:], in0=SCUR[d][:], in1=t2[:], op=ALU.add)
        if is_ctx:
            dbg("kctx_l%d" % l, k_tok, k3[:, 0, :], [128, 512], BF16)
            if not grp["last"]:
                fourier_ctx(f_tok)
        else:
            if grp["c0"] == 0:
                dbg("k0_l%d" % l, k_tok, k3[:, 0, :], [128, 512], BF16)
        if es is not None:
            cx.barrier()
            es.close()

    def decay_mul(S, d):
        sv = S[:].rearrange("p (h e) -> p h e", h=4)
        O("dve", "tensor_tensor", [S], [S, G128], out=sv, in0=sv,
          in1=G128[:, 4 * d:4 * d + 4].unsqueeze(2).broadcast_to([128, 4, 128]), op=ALU.mult)

    def recur_steps(Ud, Sd, nch, start, store):
        steps = []

        def init(d):
            S = SCUR[d]
            if start is None:
                O("dve", "memset", [S], [], S[:], 0.0)
            else:
                O("dve", "tensor_copy", [S], [start[d]], out=S[:], in_=start[d][:])

        def step(d, ci):
            S = SCUR[d]
            if store:
                sb = TB.get()
                O("dve", "tensor_copy", [sb], [S], out=sb[:], in_=S[:])
                cx.dma("sp", "st", Sd, Sd.h[d, ci], sb, sb[:])
            u = TMPH.get()
            load(u, u[:], Ud, Ud.h[d, ci])
            decay_mul(S, d)
            O("dve", "tensor_tensor", [S], [S, u], out=S[:], in0=S[:], in1=u[:], op=ALU.add)

        steps.append(lambda: (init(0), init(1)))
        for k in range(nch):
            steps.append(lambda k=k: step(0, k))
            steps.append(lambda k=k: step(1, nch - 1 - k))
        return steps

    def recur(Ud, Sd, nch, start, store):
        for f in recur_steps(Ud, Sd, nch, start, store):
            f()

    def exchange_states():
        for d in range(2):
            cx.dma("sp", "st", st_loc, st_loc.h[d * 128:(d + 1) * 128, :], SCUR[d], SCUR[d][:])
        cx.custom("pool", lambda e: e.collective_compute("AllGather", ALU.bypass, replica_groups=[[0, 1, 2, 3], [4, 5, 6, 7]],
                                                         ins=[st_loc.h.opt()], outs=[st_all.h.opt()]),
                  reads=[st_loc], writes=[st_all], st=cc, inc=1)
        for d in range(2):
            S = SSTART[d]
            sv = S[:].rearrange("p (h e) -> p h e", h=4)
            O("dve", "tensor_tensor", [S], [SCTX[d], COEF], out=sv, in0=SCTX[d][:].rearrange("p (h e) -> p h e", h=4),
              in1=COEF[:, d, 4, :].unsqueeze(2).broadcast_to([128, 4, 128]), op=ALU.mult)
            for r in range(4):
                u = TMPH.get()
                load(u, u[:], st_all, st_all.h[(r * 2 + d) * 128:(r * 2 + d + 1) * 128, :])
                O("dve", "tensor_tensor", [u], [u, COEF], out=u[:].rearrange("p (h e) -> p h e", h=4),
                  in0=u[:].rearrange("p (h e) -> p h e", h=4),
                  in1=COEF[:, d, r, :].unsqueeze(2).broadcast_to([128, 4, 128]), op=ALU.mult)
                O("dve", "tensor_tensor", [S], [S, u], out=S[:], in0=S[:], in1=u[:], op=ALU.add)

    def fourier_ctx(f_tok):
        f3 = f_tok[:]
        dt_ = WP.get()
        D256 = dt_[:, 0:1024].rearrange("p (n k) -> p n k", n=2)
        load(dt_, D256, K["D256"], K["D256"].h)
        for q in range(4):
            ps = PF.get()
            for nchk in range(2):
                O("pe", "matmul", [ps], [f_tok, dt_], ps[0:64, :], lhsT=f3[:, nchk, q * 64:(q + 1) * 64],
                  rhs=D256[:, nchk, :], start=(nchk == 0), stop=(nchk == 1))
            evac(ZTC, ZTC[:, 2 * q:2 * q + 2, :], ps, ps[0:64, :].rearrange("p (c k) -> p c k", c=2))

    def fourier_gather():
        cx.custom("pool", lambda e: e.collective_compute("AllGather", ALU.bypass, replica_groups=[[0, 1, 2, 3], [4, 5, 6, 7]],
                                                         ins=[f_loc.h.opt()], outs=[f_all.h.opt()]),
                  reads=[f_loc], writes=[f_all], st=cc, inc=1)

    def fourier_latent(side_steps=None):
        fav = f_all.h.rearrange("(r q a b) c -> r q a (b c)", r=4, q=4, a=16)
        cx.barrier(pool=True)
        es = ExitStack()
        X_t = cx.sbuf("fX", [64, 8192], BF16, stack=es)
        TT_t = cx.sbuf("fTT", [128, 8192], BF16, stack=es)
        E3 = cx.sbuf("E3", [128, 64, 96], BF16, stack=es)
        load(E3, E3[:], K["E3"], K["E3"].h, eng="pool", stream="fld")
        X, TT = X_t[:], TT_t[:]
        for q in range(4):
            for r in range(4):
                load(X_t, X[r * 16:(r + 1) * 16, 0:128 * 64], f_all, fav[r, q], eng="pool", stream="fld")
            Xv = X[0:64, :].rearrange("p (b c) -> p c b", c=64)
            TTv = TT[:, 0:64 * 128].rearrange("p (c k) -> p c k", c=64)
            for cg in range(16):
                if side_steps and cg % 2 == 0:
                    side_steps.pop(0)()
                ps = PF.get()
                for i in range(4):
                    O("pe", "matmul", [ps], [L1, X_t], ps[:, i * 128:(i + 1) * 128], lhsT=Xv[:, cg * 4 + i, :], rhs=L1[:], start=True, stop=True)
                evac(TT_t, TTv[:, cg * 4:(cg + 1) * 4, :], ps, ps[:].rearrange("p (c k) -> p c k", c=4), eng="act")
            zq = WP.get()
            zqv = zq[0:64, :].rearrange("p (c kh kl) -> p c kh kl", c=2, kl=64)
            for kg in range(8):
                ps = PF.get()
                for i in range(8):
                    klo = kg * 8 + i
                    O("pe", "matmul", [ps], [TT_t, E3], ps[0:64, i * 64:(i + 1) * 64], lhsT=TTv[:, :, klo], rhs=E3[:, klo, 32:96],
                      start=True, stop=False)
                    O("pe", "matmul", [ps], [TT_t, E3], ps[0:64, i * 64:(i + 1) * 64], lhsT=TTv[:, :, 64 + klo], rhs=E3[:, klo, 0:64],
                      start=False, stop=True)
                psv = ps[0:64, :].rearrange("p (kl c kh) -> p c kh kl", kl=8, c=2)
                for c in range(2):
                    evac(zq, zqv[:, c, :, kg * 8:(kg + 1) * 8], ps, psv[:, c, :, :], eng="act")
            cx.dma("pool", "fst", zt_d, zt_d.h[:, 2 * q:2 * q + 2, :], zq, zq[0:64, :].rearrange("p (c t) -> p c t", c=2))
        return es

    def g_post_norm_residual(xt, ysrc_t, y_aps, st, gt=None):
        srcs = ysrc_t if isinstance(ysrc_t, (list, tuple)) else [ysrc_t] * len(y_aps)
        yield from g_rms_rstd(srcs, y_aps, D, st)
        gtt = gt or MODG
        for i, yap in enumerate(y_aps):
            tmp = TMPH.get()
            O("dve", "scalar_tensor_tensor", [tmp], [srcs[i], st, gtt], out=tmp[:], in0=yap, scalar=st[:, 2:3],
              in1=gtt[:, i * 512:(i + 1) * 512], op0=ALU.mult, op1=ALU.mult)
            yield
            O("dve", "tensor_tensor", [xt], [tmp, xt], out=xt[:, i * 512:(i + 1) * 512], in0=tmp[:],
              in1=xt[:, i * 512:(i + 1) * 512], op=ALU.add)
            yield

    def post_norm_residual(xt, ysrc_t, y_aps, st, gt=None):
        run(g_post_norm_residual(xt, ysrc_t, y_aps, st, gt))

    def alloc_passB(es):
        return dict(mT=cx.sbuf("mT", [128, 8, G * 128], BF16, stack=es), hT=cx.sbuf("hT", [128, 8, G * 128], BF16, stack=es),
                    big16=cx.sbuf("big16", [128, G * D], F32, stack=es), r4=cx.sbuf("r4", [128, 4, G * 128], BF16, stack=es),
                    sguT=cx.sbuf("sguT", [128, 2, G * 128], BF16, stack=es), ZTL=cx.sbuf("ZTL", [64, 8, G * 128], BF16, stack=es))

    def pass_B(l, grp, last, bufs, prev_fin=None):
        nch, xsrc, x_ap, is_ctx = grp["nch"], grp["xsrc"], grp["x_ap"], grp["is_ctx"]
        xdst, xd_ap = grp["xdst"], grp["xd_ap"]
        xmid, xm_ap = grp["xmid"], grp["xm_ap"]
        T = nch * 128
        ropet = ROPE1 if is_ctx else ROPE
        Sd = Sc_d if is_ctx else S_d
        mT, hT, big16, r4, sguT, ZTL = bufs["mT"], bufs["hT"], bufs["big16"], bufs["r4"], bufs["sguT"], bufs["ZTL"]
        qT = kT = v_tok = gs = big16
        retT = aT = r4
        B16 = big16[:].bitcast(BF16)
        qT3 = B16[:, 0:2048].rearrange("p (h t) -> p h t", h=4)
        kT3 = B16[:, 2048:4096].rearrange("p (h t) -> p h t", h=4)
        v3 = B16[:, 4096:6144].rearrange("p (c n) -> p c n", c=G)
        gs3 = B16[:, 6144:8192].rearrange("p (c n) -> p c n", c=G)
        y2v = big16[:].rearrange("p (c d) -> p c d", c=G)
        retT3, sguT3 = r4[:], sguT[:]
        tag = "l%d_%s" % (l, "c" if is_ctx else "x%d" % grp["c0"])
        row = 1 if is_ctx else 0

        def prefetch_inputs(g):
            gctx = g["is_ctx"]
            Tg = g["nch"] * 128
            hTd = hTc_d if gctx else hT_d
            hTd_ap = hTc_d.h if gctx else hT_d.h[g["c0"] // G]
            load(hT, hT[:, :, 0:Tg], hTd, hTd_ap[:, :, 0:Tg])
            if not gctx:
                c0g = g["c0"]
                for i, nm in enumerate(("rope_cos", "rope_sin")):
                    load(ROPE, ROPE[:, i], K[nm], K[nm].h[c0g * 128:(c0g + g["nch"]) * 128, :].rearrange("(c p) i -> p c i", p=128))

        if not grp.get("prefetched"):
            prefetch_inputs(grp)
        if grp.get("load_mod", True):
            load_vec(MODG, row, 0, "G", l)
            load_vec(MODA, row, 1, "A", l)
            load_vec(MODB, row, 1, "B", l)
            load_vec(MODG2, row, 1, "G", l)
        wv = load_win(l, 2304, 512)

        def g_uvs_b(ci, ps):
            gu = TMPH.get()
            tt = TMPH.get()
            O("act", "activation", [gu], [ps], out=gu[:], in_=ps[:], func=AF.Gelu_apprx_tanh)
            yield
            st = STAT.get()
            O("act", "activation", [tt], [gu], out=tt[:, 0:256], in_=gu[:, 256:512], func=AF.Square)
            yield
            O("dve", "tensor_reduce", [st], [tt], out=st[:, 0:4], in_=tt[:, 0:256].rearrange("p (g c) -> p g c", g=4),
              axis=mybir.AxisListType.X, op=ALU.add)
            yield
            O("act", "activation", [st], [st, epsb], out=st[:, 0:4], in_=st[:, 0:4], func=AF.Sqrt, scale=1.0 / 64, bias=epsb[:, 0:1])
            yield
            O("dve", "reciprocal", [st], [st], out=st[:, 4:8], in_=st[:, 0:4])
            yield
            O("dve", "tensor_tensor", [tt], [gu, st], out=tt[:, 0:256].rearrange("p (g c) -> p g c", g=4),
              in0=gu[:, 256:512].rearrange("p (g c) -> p g c", g=4), in1=st[:, 4:8].unsqueeze(2).broadcast_to([128, 4, 64]), op=ALU.mult)
            yield
            vnb = TB.get()
            O("dve", "tensor_tensor", [vnb], [tt, SGN], out=vnb[:, 0:256], in0=tt[:, 0:256], in1=SGN[:], op=ALU.mult)
            yield
            ps2 = PF.get()
            for g in range(4):
                O("pe", "matmul", [ps2], [SGW, vnb], ps2[:, g * 64:(g + 1) * 64], lhsT=SGW[:, g, :], rhs=vnb[:, g * 64:(g + 1) * 64],
                  start=True, stop=True)
            O("dve", "tensor_tensor", [tt], [ps2, SGB], out=tt[:, 256:512].rearrange("p (g c) -> p g c", g=4),
              in0=ps2[:, 0:256].rearrange("p (g c) -> p g c", g=4), in1=SGB[:].unsqueeze(2).broadcast_to([128, 4, 64]), op=ALU.add)
            yield
            sgb = TB.get()
            O("dve", "tensor_tensor", [sgb], [tt, gu], out=sgb[:, 0:256], in0=tt[:, 256:512], in1=gu[:, 0:256], op=ALU.mult)
            if ci == 0:
                dbg("sgu_" + tag, sgb, sgb[:, 0:256], [128, 256], BF16)
            yield
            transpose_blocks(sgb, [sgb[:, h * 128:(h + 1) * 128] for h in range(2)], sguT, sguT3[:, :, ci * 128:(ci + 1) * 128])
            yield

        prev_fin = list(prev_fin or [])
        for c2 in range(0, nch, 2):
            cis = list(range(c2, min(c2 + 2, nch)))
            pss = [zblock(hT, ci, wv, 512) for ci in cis]
            interleave([g_uvs_b(ci, ps) for ci, ps in zip(cis, pss)])
            if prev_fin:
                prev_fin.pop(0)()
        while prev_fin:
            prev_fin.pop(0)()
        wv = load_win(l, 0, 512)

        def q_b(ci, ps):
            tb = TB.get()
            rope(ps, ropet, ci, tb, tb[:])
            transpose_blocks(tb, [tb[:, h * 128:(h + 1) * 128] for h in range(4)], qT, qT3[:, :, ci * 128:(ci + 1) * 128])
        pend = zblock(hT, 0, wv, 512)
        for ci in range(nch):
            nxt = zblock(hT, ci + 1, wv, 512) if ci + 1 < nch else None
            q_b(ci, pend)
            pend = nxt
        kvd_ap = kvc_d.h if is_ctx else kv_d.h[grp["c0"] // G]
        kvd = kvc_d if is_ctx else kv_d
        load(r4, r4[:].rearrange("p h t -> p (h t)").rearrange("p (c n) -> p c n", c=G)[:, 0:nch, :], kvd, kvd_ap[0, :, 0:nch, :])
        load(big16, v3[:, 0:nch, :], kvd, kvd_ap[1, :, 0:nch, :])
        kst = r4[:].rearrange("p h t -> p (h t)").rearrange("p (c n) -> p c n", c=G)
        for ci in range(nch):
            transpose_blocks(r4, [kst[:, ci, h * 128:(h + 1) * 128] for h in range(4)], kT, kT3[:, :, ci * 128:(ci + 1) * 128])
        wv = load_win(l, 1536, 512)
        for ci in range(nch):
            ps = zblock(hT, ci, wv, 512)
            O("act", "activation", [gs], [ps], out=gs3[:, ci, :], in_=ps[:], func=AF.Silu)
        def b3_s1(ci):
            gci = ci if is_ctx else grp["c0"] + ci
            sl = slice(ci * 128, (ci + 1) * 128)
            sfb = SFB.get()
            load(sfb, sfb[:], Sd, Sd.h[:, gci].rearrange("d p n -> p d n"))
            Sf = Sb = sfb
            ps = PF.get()
            for h in range(4):
                O("pe", "matmul", [ps], [kT, qT], ps[:, h * 128:(h + 1) * 128], lhsT=kT3[:, h, sl], rhs=qT3[:, h, sl], start=True, stop=True)
            scm = TB.get()
            O("dve", "tensor_tensor", [scm], [ps, DT], out=scm[:], in0=ps[:], in1=DT[:].rearrange("p h c -> p (h c)"), op=ALU.mult)
            qf = TB.get()
            qb = TB.get()
            O("dve", "tensor_tensor", [qf], [qT, QF], out=qf[:].rearrange("p (h c) -> p h c", h=4), in0=qT3[:, :, sl], in1=QF[:], op=ALU.mult)
            O("dve", "tensor_tensor", [qb], [qT, QB], out=qb[:].rearrange("p (h c) -> p h c", h=4), in0=qT3[:, :, sl], in1=QB[:], op=ALU.mult)
            return Sf, Sb, scm, qf, qb

        def g_b3_s2(ci, Sf, Sb, scm, qf, qb):
            sl = slice(ci * 128, (ci + 1) * 128)
            po = PF.get()
            for h in range(4):
                hs = slice(h * 128, (h + 1) * 128)
                O("pe", "matmul", [po], [scm, v_tok], po[:, hs], lhsT=scm[:, hs], rhs=v3[:, ci, hs], start=True, stop=False)
                O("pe", "matmul", [po], [qf, Sf], po[:, hs], lhsT=qf[:, hs], rhs=Sf[:, 0, hs], start=False, stop=False)
                O("pe", "matmul", [po], [qb, Sb], po[:, hs], lhsT=qb[:, hs], rhs=Sb[:, 1, hs], start=False, stop=True)
            st = STAT.get()
            O("dve", "memset", [st], [], st[:], 0.0)
            yield
            for h in range(4):
                junk = JUNK.get()
                O("act", "activation", [junk, st], [po, st], out=junk[:, 0:128], in_=po[:, h * 128:(h + 1) * 128], func=AF.Square,
                  accum_out=st[:, h:h + 1])
            yield
            O("act", "activation", [st], [st, epsb], out=st[:, 0:4], in_=st[:, 0:4], func=AF.Sqrt, scale=1.0 / 128, bias=epsb[:, 0:1])
            yield
            O("dve", "reciprocal", [st], [st], out=st[:, 4:8], in_=st[:, 0:4])
            yield
            rt = TB.get()
            for h in range(4):
                hs = slice(h * 128, (h + 1) * 128)
                O("dve", "scalar_tensor_tensor", [rt], [po, st, gs], out=rt[:, hs], in0=po[:, hs], scalar=st[:, 4 + h:5 + h],
                  in1=gs3[:, ci, hs], op0=ALU.mult, op1=ALU.mult)
            if ci == 0:
                dbg("ret_" + tag, rt, rt[:], [128, 512], BF16)
            yield
            transpose_blocks(rt, [rt[:, h * 128:(h + 1) * 128] for h in range(4)], retT, retT3[:, :, sl])
            yield

        s1 = [b3_s1(ci) for ci in range(nch)]
        for c2 in range(0, nch, 2):
            interleave([g_b3_s2(ci, *s1[ci]) for ci in range(c2, min(c2 + 2, nch))])
        if is_ctx:
            ZT_t, ZT3 = ZTC, ZTC[:]
        else:
            ZT_t = ZTL
            ZT3 = ZTL[:]
            load(ZTL, ZT3, zt_d, zt_d.h[:, :, grp["c0"] * 128:grp["c0"] * 128 + T])
        TBW = min(T, 512)
        ntb = T // TBW
        for db in range(8):
            j = db % 2
            if j == 0:
                dbp = db // 2
                wgA = WP.get()
                wgAv = wgA[:].rearrange("p (b kc n) -> p b kc n", b=2, kc=8)
                for br in range(2):
                    c0w = 2816 + br * 1024 + dbp * 256
                    wload(wgA, wgAv[:, br], W["w_in"], W["w_in"].h[l, :, c0w:c0w + 256].rearrange("(kc p) n -> p kc n", p=128))
                wgB = WP.get()
                wg2v = wgB[:, 0:2048].rearrange("p (kc n) -> p kc n", kc=8)
                wa2v = wgB[:, 2048:3072].rearrange("p (kc n) -> p kc n", kc=4)
                wc2v = wgB[:, 3072:3584].rearrange("p (kc n) -> p kc n", kc=2)
                c0w = 2816 + 2 * 1024 + dbp * 256
                wload(wgB, wg2v, W["w_in"], W["w_in"].h[l, :, c0w:c0w + 256].rearrange("(kc p) n -> p kc n", p=128))
                wload(wgB, wa2v, W["w_branch_a"], W["w_branch_a"].h[l, :, dbp * 256:(dbp + 1) * 256].rearrange("(kc p) n -> p kc n", p=128))
                wload(wgB, wc2v, W["w_branch_c"], W["w_branch_c"].h[l, :, dbp * 256:(dbp + 1) * 256].rearrange("(kc p) n -> p kc n", p=128))
                wgC = WP.get()
                wbp2v = wgC[0:64, 0:2048].rearrange("p (kc n) -> p kc n", kc=8)
                load(wgC, wbp2v, wbp_d, wbp_d.h[:, :, dbp * 256:(dbp + 1) * 256], eng="pool", stream="w")
            js = slice(j * 128, (j + 1) * 128)
            gate_t = [wgA, wgA, wgB]
            gate_v = [wgAv[:, 0, :, js], wgAv[:, 1, :, js], wg2v[:, :, js]]
            wa_v, wc_v, wbp_v = wa2v[:, :, js], wc2v[:, :, js], wbp2v[:, :, js]
            for tbi in range(ntb):
                ts = slice(tbi * TBW, (tbi + 1) * TBW)
                acc = TMPH.get()
                for br in range(3):
                    pg = PF.get()
                    for kc in range(8):
                        O("pe", "matmul", [pg], [gate_t[br], hT], pg[:, 0:TBW], lhsT=gate_v[br][:, kc, :], rhs=hT[:, kc, ts], start=(kc == 0), stop=(kc == 7))
                    pb = PF.get()
                    if br == 0:
                        for kc in range(4):
                            O("pe", "matmul", [pb], [wgB, retT], pb[:, 0:TBW], lhsT=wa_v[:, kc, :], rhs=retT3[:, kc, ts], start=(kc == 0), stop=(kc == 3))
                    elif br == 1:
                        for kc in range(8):
                            O("pe", "matmul", [pb], [wgC, ZT_t], pb[:, 0:TBW], lhsT=wbp_v[:, kc, :], rhs=ZT3[:, kc, ts], start=(kc == 0), stop=(kc == 7))
                    else:
                        for kc in range(2):
                            O("pe", "matmul", [pb], [wgC if False else wgB, sguT], pb[:, 0:TBW], lhsT=wc_v[:, kc, :], rhs=sguT3[:, kc, ts], start=(kc == 0), stop=(kc == 1))
                    sg = TMPH.get()
                    O("act", "activation", [sg], [pg], out=sg[:, 0:TBW], in_=pg[:, 0:TBW], func=AF.Sigmoid)
                    if br == 0:
                        O("dve", "tensor_tensor", [acc], [sg, pb], out=acc[:, 0:TBW], in0=sg[:, 0:TBW], in1=pb[:, 0:TBW], op=ALU.mult)
                    else:
                        O("dve", "tensor_tensor", [sg], [sg, pb], out=sg[:, 0:TBW], in0=sg[:, 0:TBW], in1=pb[:, 0:TBW], op=ALU.mult)
                        if br == 1:
                            O("dve", "tensor_tensor", [acc], [acc, sg], out=acc[:, 0:TBW], in0=acc[:, 0:TBW], in1=sg[:, 0:TBW], op=ALU.add)
                        else:
                            O("dve", "tensor_tensor", [mT], [acc, sg], out=mT[:, db, ts], in0=acc[:, 0:TBW], in1=sg[:, 0:TBW], op=ALU.add)
        dbg("mT_" + tag, mT, mT[:, :, 0:128], [128, 8, 128], BF16)
        wo = [WP.get(), WP.get()]
        wov = []
        for half in range(2):
            v = wo[half][:].rearrange("p (kc n) -> p kc n", kc=8)
            wload(wo[half], v, W["w_out"], W["w_out"].h[l, :, half * 512:(half + 1) * 512].rearrange("(kc p) n -> p kc n", p=128))
            wov.append(v)
        def b6_a(ci):
            xt = XIN.get()
            load(xt, xt[:], xsrc, x_ap(ci))
            pss = []
            for half in range(2):
                ps = PF.get()
                for kc in range(8):
                    O("pe", "matmul", [ps], [mT, wo[half]], ps[:], lhsT=mT[:, kc, ci * 128:(ci + 1) * 128], rhs=wov[half][:, kc, :],
                      start=(kc == 0), stop=(kc == 7))
                pss.append(ps)
            return xt, pss

        def g_b6_b(ci, xt, pss):
            st = STAT.get()
            yield from g_post_norm_residual(xt, pss, [pss[0][:], pss[1][:]], st)
            if ci == 0:
                dbg("xmid_" + tag, xt, xt[:], [128, D])
            cx.dma("sp", "st", xmid, xm_ap(ci), xt, xt[:])
            yield
            yield from g_norm_mod_T(xt, hT, ci)

        for c2 in range(0, nch, 2):
            cis = list(range(c2, min(c2 + 2, nch)))
            As = [b6_a(ci) for ci in cis]
            interleave([g_b6_b(ci, *a) for ci, a in zip(cis, As)])
        aT3 = r4[:]
        for fb in range(8):
            wu = WP.get()
            wuv = wu[:].rearrange("p (kc n) -> p kc n", kc=8)
            wload(wu, wuv, W["w_up"], W["w_up"].h[l, :, fb * 512:(fb + 1) * 512].rearrange("(kc p) n -> p kc n", p=128))
            wd = WP.get()
            wdv = wd[:].rearrange("p (f n) -> p f n", f=4)
            wload(wd, wdv, W["w_down"], W["w_down"].h[l, fb * 512:(fb + 1) * 512, :].rearrange("(f p) n -> p f n", p=128))
            for tbi in range(ntb):
                ts = slice(tbi * TBW, (tbi + 1) * TBW)
                for f in range(4):
                    ps = PF.get()
                    for kc in range(8):
                        O("pe", "matmul", [ps], [wu, hT], ps[:, 0:TBW], lhsT=wuv[:, kc, f * 128:(f + 1) * 128], rhs=hT[:, kc, ts],
                          start=(kc == 0), stop=(kc == 7))
                    r = TMPH.get()
                    O("act", "activation", [r], [ps], out=r[:, 0:TBW], in_=ps[:, 0:TBW], func=AF.Relu)
                    O("act", "activation", [aT], [r], out=aT3[:, f, ts], in_=r[:, 0:TBW], func=AF.Square)
            if fb == 7 and grp.get("next") is not None:
                prefetch_inputs(grp["next"])
                grp["next"]["prefetched"] = True
            for ci in range(nch):
                for half in range(2):
                    ps = PF.get()
                    for f in range(4):
                        O("pe", "matmul", [ps], [aT, wd], ps[:], lhsT=aT3[:, f, ci * 128:(ci + 1) * 128], rhs=wdv[:, f, half * 512:(half + 1) * 512],
                          start=(f == 0), stop=(f == 3))
                    ya = y2v[:, ci, half * 512:(half + 1) * 512]
                    if fb == 0:
                        evac(big16, ya, ps, ps[:])
                    else:
                        O("dve", "tensor_tensor", [big16], [big16, ps], out=ya, in0=ya, in1=ps[:], op=ALU.add)
        def g_fin(ci):
            xt = XIN.get()
            load(xt, xt[:], xmid, xm_ap(ci))
            st = STAT.get()
            yield from g_post_norm_residual(xt, big16, [y2v[:, ci, 0:512], y2v[:, ci, 512:1024]], st, gt=MODG2)
            if ci == 0:
                dbg("xout_" + tag, xt, xt[:], [128, D])
            cx.dma("sp", "st", xdst, xd_ap(ci), xt, xt[:])
            yield
        return [lambda c2=c2: interleave([g_fin(ci) for ci in range(c2, min(c2 + 2, nch))]) for c2 in range(0, nch, 2)]

    def row_ap(t, c0):
        return lambda ci: t.h[(c0 + ci) * 128:(c0 + ci + 1) * 128, :]

    for l in range(nlayers):
        last = (l == DEPTH - 1)
        if l == 0:
            setup_mod(0)
        layer_setup(l)
        csrc = ctx_in if l == 0 else cs_d
        cgrp = dict(nch=CTXCH, xsrc=csrc, x_ap=row_ap(csrc, 0), is_ctx=True, c0=0, last=last,
                    xmid=cs_d, xm_ap=row_ap(cs_d, 0), xdst=cs_d, xd_ap=row_ap(cs_d, 0))
        pass_A(l, cgrp)
        recur(Uc_d, Sc_d, CTXCH, None, store=not last)
        for d in range(2):
            O("dve", "tensor_copy", [SCTX[d]], [SCUR[d]], out=SCTX[d][:], in_=SCUR[d][:])
        dbg("sctx_l%d" % l, SCTX[0], SCTX[0][:], [128, 512])
        if not last:
            esB = ExitStack()
            for f in pass_B(l, cgrp, last, alloc_passB(esB)):
                f()
            cx.barrier()
            esB.close()
        xsrc = x_in if l == 0 else xs_d
        xdst = out_d if last else xs_d
        groups = []
        for g in range(NCH // G):
            c0 = g * G
            groups.append(dict(nch=G, xsrc=xsrc, x_ap=row_ap(xsrc, c0), is_ctx=False, c0=c0,
                               xmid=xs_d if not last else xs_d, xm_ap=row_ap(xs_d, c0), xdst=xdst, xd_ap=row_ap(xdst, c0)))
        esA = ExitStack()
        bufsA = alloc_passA(esA)
        for gi, grp in enumerate(groups):
            pass_A(l, grp, bufsA)
            if gi == 1 and l + 1 < nlayers:
                setup_mod(l + 1)
        cx.barrier()
        esA.close()
        if stop_after == "A":
            break
        fourier_gather()
        exchange_states()
        dbg("sstart_l%d" % l, SSTART[0], SSTART[0][:], [128, 512])
        es_f = fourier_latent()
        recur(U_d, S_d, NCH, SSTART, store=True)
        cx.barrier()
        es_f.close()
        if stop_after == "F":
            break
        esB = ExitStack()
        bufsB = alloc_passB(esB)
        for gi, grp in enumerate(groups):
            grp["next"] = groups[gi + 1] if gi + 1 < len(groups) else None
            grp["load_mod"] = (gi == 0)
            grp["prefetched"] = False
        fin = None
        for grp in groups:
            fin = pass_B(l, grp, last, bufsB, prev_fin=fin)
        for f in fin:
            f()
        cx.barrier()
        esB.close()

    cx.barrier(full=True)
    cx.emit()
    DEBUG['min_free'] = cx.min_free
    DEBUG['ops'] = {k: len(v.ops) for k, v in cx.engs.items()}
    cx.close()
    return nc, dbg_out


_CACHE = {}


def make_in_maps(inputs):
    in_maps = []
    f32 = np.float32
    shared = {}
    for k in WEIGHT_SPECS:
        a = np.ascontiguousarray(np.asarray(inputs[k], dtype=f32))
        shared[k] = a.reshape(WEIGHT_SPECS[k])
    x = np.asarray(inputs["x"], dtype=f32)
    ctx = np.asarray(inputs["ctx"], dtype=f32)
    c = np.asarray(inputs["c"], dtype=f32)
    c_ctx = np.asarray(inputs["c_ctx"], dtype=f32)
    for core in range(8):
        b, j = core // 4, core % 4
        m = dict(shared)
        m["x"] = np.ascontiguousarray(x[b, 2048 * j:2048 * (j + 1), :])
        m["ctx"] = np.ascontiguousarray(ctx[b])
        m["c2"] = np.ascontiguousarray(np.stack([c[b], c_ctx], 0))
        m.update(host_consts(core))
        in_maps.append(m)
    return in_maps


def kernel(**inputs):
    if "nc" not in _CACHE:
        _CACHE["nc"] = build()[0]
    nc = _CACHE["nc"]
    in_maps = make_in_maps(inputs)
    res = run_bass_kernel_spmd(nc, in_maps, core_ids=list(range(8)))
    out = np.zeros((2, 8192, D), np.float32)
    for core in range(8):
        b, j = core // 4, core % 4
        out[b, 2048 * j:2048 * (j + 1), :] = res.results[core]["out"]
    return out
```
